# Optimizing a Trainium2 kernel written in Bass

```python
import math
import jax, jax.numpy as jnp
from jax import lax
import numpy as np

D_MODEL = 1024
BATCH = 2
SEQ = 8192
DEPTH = 4
DEC_BATCH = 128
DEC_SEQ = 4
PAST_LEN = 8192
PAGE_SIZE = 128

N_A = DEPTH // 2
N_B = DEPTH - N_A
POOL_WINDOWS = (2, 4, 8, 16)
N_POOL_GROUPS = len(POOL_WINDOWS)
POOL_GROUP = D_MODEL // N_POOL_GROUPS
POOL_BUF = max(POOL_WINDOWS) - 1
HEAD_DIM = 64
N_HEADS = D_MODEL // HEAD_DIM
N_KV = 4
GROUP = N_HEADS // N_KV
WINDOW = 128
BLOCK = WINDOW
N_BUCKETS = 32
MAX_DISTANCE = 128
D_FF = 2816
CONV_W = 3
CONV_BUF = CONV_W - 1
PLE_DIM = 256
EPS = 1e-6

kernel_name = 'yoco_pool_swa_sink_convffn_step'


def rmsnorm(x, g):
    xf = x.astype(jnp.float32)
    y = xf * lax.rsqrt(jnp.mean(xf * xf, axis=-1, keepdims=True) + EPS)
    return (y * g.astype(jnp.float32)).astype(x.dtype)


def pool_mixer(xn, prefix, start, w_grp, scale):
    N, L, _ = xn.shape
    ext = jnp.concatenate([prefix.astype(xn.dtype), xn], axis=1).astype(jnp.float32)
    c = jnp.pad(jnp.cumsum(ext, axis=1), ((0, 0), (1, 0), (0, 0)))
    pos = start + jnp.arange(L)
    outs = []
    for g, w in enumerate(POOL_WINDOWS):
        sl = slice(g * POOL_GROUP, (g + 1) * POOL_GROUP)
        cg = c[:, :, sl]
        s = cg[:, POOL_BUF + 1:] - cg[:, POOL_BUF + 1 - w:POOL_BUF + 1 - w + L]
        cnt = jnp.minimum(pos + 1, w).astype(jnp.float32)
        outs.append(s / cnt[None, :, None] - ext[:, POOL_BUF:, sl])
    d = jnp.stack(outs, axis=2).astype(xn.dtype)
    y = jnp.einsum('nlgc,gcd->nlgd', d, w_grp).reshape(N, L, D_MODEL)
    return y * scale


def conv_ffn(xn, prefix, w_up, conv_w, conv_b, w_down):
    L = xn.shape[1]
    u = xn @ w_up
    ext = jnp.concatenate([prefix.astype(u.dtype), u], axis=1)
    c = conv_b + ext[:, 0:L] * conv_w[0]
    for k in range(1, CONV_W):
        c = c + ext[:, k:k + L] * conv_w[k]
    gate, val = jnp.split(c, 2, axis=-1)
    return (jax.nn.gelu(gate) * val) @ w_down, ext[:, -CONV_BUF:]


def t5_bucket(d):
    n = jnp.maximum(d, 0)
    max_exact = N_BUCKETS // 2
    nf = jnp.maximum(n, 1).astype(jnp.float32)
    large = max_exact + (jnp.log(nf / max_exact) / math.log(MAX_DISTANCE / max_exact)
                         * (N_BUCKETS - max_exact)).astype(jnp.int32)
    large = jnp.minimum(large, N_BUCKETS - 1)
    return jnp.where(n < max_exact, n, large)


def window_attention(q, k, v, bias, valid, sink):
    s = jnp.einsum('nbqhgd,nbjhd->nbhgqj', q, k).astype(jnp.float32) * (HEAD_DIM ** -0.5)
    s = jnp.where(valid[None, :, None, None], s + bias, -jnp.inf)
    sk = sink.astype(jnp.float32)[None, None, :, :, None, None]
    m = jnp.maximum(jnp.max(s, axis=-1, keepdims=True), sk)
    e = jnp.exp(s - m)
    p = e / (jnp.sum(e, axis=-1, keepdims=True) + jnp.exp(sk - m))
    return jnp.einsum('nbhgqj,nbjhd->nbqhgd', p.astype(v.dtype), v)


def shared_kv(h, kv_norm, w_k, w_v, k_norm):
    N, L, _ = h.shape
    xn = rmsnorm(h, kv_norm)
    k = rmsnorm((xn @ w_k).reshape(N, L, N_KV, HEAD_DIM), k_norm)
    v = (xn @ w_v).reshape(N, L, N_KV, HEAD_DIM)
    return k, v


def trunk(x, p, pool_prev, conv_prev, k_prev, v_prev, start, prompt, W):
    N, L, _ = x.shape
    h = x
    pool_new, conv_new = [], []
    nblk, qlen = (L // BLOCK, BLOCK) if prompt else (1, L)
    for i in range(DEPTH):
        if i == N_A:
            k_new, v_new = shared_kv(h, W['kv_norm'], W['w_k'], W['w_v'], W['k_norm'])
            if prompt:
                kb = k_new.reshape(N, nblk, BLOCK, N_KV, HEAD_DIM)
                vb = v_new.reshape(N, nblk, BLOCK, N_KV, HEAD_DIM)
                padb = ((0, 0), (1, 0), (0, 0), (0, 0), (0, 0))
                k_blk = jnp.concatenate([jnp.pad(kb[:, :-1], padb), kb], axis=2)
                v_blk = jnp.concatenate([jnp.pad(vb[:, :-1], padb), vb], axis=2)
                k_win, v_win = k_new[:, -WINDOW:], v_new[:, -WINDOW:]
                kj = jnp.arange(2 * BLOCK)[None, :]
                d = jnp.arange(BLOCK)[:, None] + BLOCK - kj
                valid = ((d >= 0) & (d < WINDOW))[None] & (
                    (jnp.arange(nblk)[:, None, None] > 0) | (kj >= BLOCK)[None])
            else:
                k_ext = jnp.concatenate([k_prev.astype(k_new.dtype), k_new], axis=1)
                v_ext = jnp.concatenate([v_prev.astype(v_new.dtype), v_new], axis=1)
                k_win, v_win = k_ext[:, -WINDOW:], v_ext[:, -WINDOW:]
                k_blk, v_blk = k_ext[:, None], v_ext[:, None]
                d = jnp.arange(L)[:, None] + WINDOW - jnp.arange(WINDOW + L)[None, :]
                valid = ((d >= 0) & (d < WINDOW))[None]
            bias = jnp.transpose(W['rel_bias'][t5_bucket(d)], (2, 0, 1))
            bias = bias.reshape(N_KV, GROUP, qlen, -1).astype(jnp.float32)
        xn = rmsnorm(h, W['norm_mix'][i])
        if i < N_A:
            pre = jnp.zeros((N, POOL_BUF, D_MODEL), xn.dtype) if prompt else pool_prev[i].astype(xn.dtype)
            h = h + pool_mixer(xn, pre, start, W['w_pool'][i], W['pool_scale'][i])
            pool_new.append(jnp.concatenate([pre, xn], axis=1)[:, -POOL_BUF:])
        else:
            j = i - N_A
            q = (xn @ W['w_q'][j]).reshape(N, nblk, qlen, N_KV, GROUP, HEAD_DIM)
            q = rmsnorm(q, W['q_norm'][j])
            o = window_attention(q, k_blk, v_blk, bias, valid, W['sinks'][j].reshape(N_KV, GROUP))
            h = h + o.reshape(N, L, D_MODEL) @ W['w_o'][j]
        xn = rmsnorm(h, W['norm_ffn'][i])
        cpre = jnp.zeros((N, CONV_BUF, 2 * D_FF), xn.dtype) if prompt else conv_prev[i]
        f, cstate = conv_ffn(xn, cpre, W['w_up'][i], W['conv_w'][i], W['conv_b'][i], W['w_down'][i])
        h = h + f
        conv_new.append(cstate)
        gate = jax.nn.sigmoid(rmsnorm(h, W['norm_ple'][i]) @ W['w_ple_gate'][i])
        h = h + gate * (p[i] @ W['w_ple_proj'][i])
    return h, jnp.stack(pool_new), jnp.stack(conv_new), k_win, v_win


def setup_inputs(seed: int = 0) -> dict:
    key = jax.random.key(seed)
    ks = iter(jax.random.split(key, 40))

    def nrm(shape, scale):
        return jax.random.normal(next(ks), shape, jnp.float32) * scale

    F2 = 2 * D_FF
    return {
        'x_prompt': nrm((BATCH, SEQ, D_MODEL), 1.0),
        'x_sample': nrm((DEC_BATCH, DEC_SEQ, D_MODEL), 1.0),
        'p_prompt': nrm((DEPTH, BATCH, SEQ, PLE_DIM), 1.0),
        'p_sample': nrm((DEPTH, DEC_BATCH, DEC_SEQ, PLE_DIM), 1.0),
        'state_pool': nrm((N_A, DEC_BATCH, POOL_BUF, D_MODEL), 1.0),
        'state_conv': nrm((DEPTH, DEC_BATCH, CONV_BUF, F2), 1.0),
        'cache_k': nrm((DEC_BATCH, WINDOW, N_KV, HEAD_DIM), 1.0),
        'cache_v': nrm((DEC_BATCH, WINDOW, N_KV, HEAD_DIM), 1.0),
        'norm_mix': 1.0 + nrm((DEPTH, D_MODEL), 0.05),
        'norm_ffn': 1.0 + nrm((DEPTH, D_MODEL), 0.05),
        'norm_ple': 1.0 + nrm((DEPTH, D_MODEL), 0.05),
        'w_pool': nrm((N_A, N_POOL_GROUPS, POOL_GROUP, POOL_GROUP), POOL_GROUP ** -0.5),
        'pool_scale': 1.0 + nrm((N_A, D_MODEL), 0.1),
        'kv_norm': 1.0 + nrm((D_MODEL,), 0.05),
        'w_k': nrm((D_MODEL, N_KV * HEAD_DIM), D_MODEL ** -0.5),
        'w_v': nrm((D_MODEL, N_KV * HEAD_DIM), D_MODEL ** -0.5),
        'k_norm': 1.0 + nrm((HEAD_DIM,), 0.05),
        'w_q': nrm((N_B, D_MODEL, D_MODEL), D_MODEL ** -0.5),
        'q_norm': 1.0 + nrm((N_B, HEAD_DIM), 0.05),
        'sinks': nrm((N_B, N_HEADS), 0.5),
        'w_o': nrm((N_B, D_MODEL, D_MODEL), D_MODEL ** -0.5),
        'rel_bias': nrm((N_BUCKETS, N_HEADS), 0.5),
        'w_up': nrm((DEPTH, D_MODEL, F2), D_MODEL ** -0.5),
        'conv_w': nrm((DEPTH, CONV_W, F2), 0.5),
        'conv_b': nrm((DEPTH, F2), 0.02),
        'w_down': nrm((DEPTH, D_FF, D_MODEL), D_FF ** -0.5),
        'w_ple_gate': nrm((DEPTH, D_MODEL, D_MODEL), D_MODEL ** -0.5),
        'w_ple_proj': nrm((DEPTH, PLE_DIM, D_MODEL), PLE_DIM ** -0.5),
    }


def reference(x_prompt, x_sample, p_prompt, p_sample, state_pool, state_conv, cache_k, cache_v,
              norm_mix, norm_ffn, norm_ple, w_pool, pool_scale, kv_norm, w_k, w_v, k_norm,
              w_q, q_norm, sinks, w_o, rel_bias, w_up, conv_w, conv_b, w_down, w_ple_gate, w_ple_proj):
    W = dict(norm_mix=norm_mix, norm_ffn=norm_ffn, norm_ple=norm_ple, w_pool=w_pool,
             pool_scale=pool_scale, kv_norm=kv_norm, w_k=w_k, w_v=w_v, k_norm=k_norm,
             w_q=w_q, q_norm=q_norm, sinks=sinks, w_o=w_o, rel_bias=rel_bias, w_up=w_up,
             conv_w=conv_w, conv_b=conv_b, w_down=w_down, w_ple_gate=w_ple_gate,
             w_ple_proj=w_ple_proj)
    y_prompt, pool_p, conv_p, k_p, v_p = trunk(x_prompt, p_prompt, None, None, None, None,
                                               0, True, W)
    y_sample, pool_s, conv_s, k_s, v_s = trunk(x_sample, p_sample, state_pool, state_conv,
                                               cache_k, cache_v, PAST_LEN, False, W)
    return (y_prompt, y_sample, pool_p, pool_s, conv_p, conv_s, k_p, k_s, v_p, v_s)
```

```python
import math
import types
import numpy as np
import concourse.bass as bass
import concourse.mybir as mybir
from concourse.bass_utils import run_bass_kernel_spmd
from contextlib import ExitStack

F32 = mybir.dt.float32
BF16 = mybir.dt.bfloat16
AF = mybir.ActivationFunctionType
ALU = mybir.AluOpType
AX = mybir.AxisListType

NCORES = 8
D = 1024
NCH = 8
FF = 2816
NJ = 22
NEG = -30000.0
EPS = 1e-6
HALO = 256
CHUNK = 2048
NPROMPT = HALO + CHUNK
NSAMP = 64
NROWS = NPROMPT + NSAMP
TA = 1280
POOL_W = (2, 4, 8, 16)

C_NM, C_NF, C_NP, C_KVN, C_PSC, C_CW, C_CB, C_QN, C_KN, C_EPS = 0, 32, 64, 96, 104, 120, 648, 824, 826, 827
NCOLS = 828
CC_M, CC_INV, CC_FM, NCC = 0, 1, 61, 189

PARTS = [(0, 4), (4, 8), (8, 12), (12, 16), (16, 19), (19, 22)]


def _freeze(fn):
    if fn.__closure__ is None:
        return fn
    cells = []
    for c in fn.__closure__:
        try:
            cells.append(types.CellType(c.cell_contents))
        except ValueError:
            cells.append(c)
    return types.FunctionType(fn.__code__, fn.__globals__, fn.__name__, fn.__defaults__, tuple(cells))


class Buf:
    __slots__ = ("name", "w", "r", "alias", "excl")

    def __init__(self, name, excl=False):
        self.name = name
        self.w = None
        self.r = {}
        self.alias = []
        self.excl = excl


class Sched:
    ENG = ("pe", "act", "dve", "pool", "sp")
    SEM_LIMIT = 3500

    def __init__(self, nc, es):
        self.nc = nc
        self.es = es
        self.rec = {e: [] for e in self.ENG}
        self.sem = {e: es.enter_context(nc.semaphore("s_" + e)) for e in ("pe", "act", "dve", "pool")}
        self.cnt = {e: 0 for e in self.sem}
        self.epoch = {e: 0 for e in self.sem}
        self.key = {e: e + "#0" for e in self.sem}
        self.pending = {e: False for e in self.sem}
        self.waited = {e: {} for e in self.ENG}
        self.dsem = {}
        self.final = {}

    def _wait(self, eng, ev):
        if ev is None:
            return
        sem, val, key = ev
        if key == self.key.get(eng) and val > self.cnt[eng]:
            return
        w = self.waited[eng]
        if w.get(key, 0) >= val:
            return
        w[key] = val
        self.rec[eng].append(lambda e, sem=sem, val=val: e.wait_ge(sem, val))

    def _deps(self, eng, reads, writes):
        for b in reads:
            self._wait(eng, b.w)
            if b.excl:
                mine = self.key.get(eng, eng).split("#")[0]
                for k, ev in list(b.r.items()):
                    if k.split("#")[0] != mine:
                        self._wait(eng, ev)
        for b in writes:
            self._wait(eng, b.w)
            for ev in list(b.r.values()):
                self._wait(eng, ev)
            for a in b.alias:
                self._wait(eng, a.w)
                for ev in list(a.r.values()):
                    self._wait(eng, ev)

    def _mark(self, ev, reads, writes):
        for b in reads:
            old = b.r.get(ev[2])
            if old is None or old[1] < ev[1]:
                b.r[ev[2]] = ev
        for b in writes:
            b.w = ev
            b.r = {}
            for a in b.alias:
                a.w = None
                a.r = {}

    def op(self, eng, fn, reads=(), writes=(), signal=True):
        fn = _freeze(fn)
        if self.cnt[eng] >= self.SEM_LIMIT and not self.pending[eng]:
            self.epoch[eng] += 1
            self.sem[eng] = self.es.enter_context(self.nc.semaphore(f"s_{eng}_{self.epoch[eng]}"))
            self.cnt[eng] = 0
            self.key[eng] = f"{eng}#{self.epoch[eng]}"
        self._deps(eng, reads, writes)
        s = self.sem[eng]
        if signal:
            self.cnt[eng] += 1
            ev = (s, self.cnt[eng], self.key[eng])
            self.rec[eng].append(lambda e, fn=fn, s=s: fn(e).then_inc(s, 1))
            self.pending[eng] = False
        else:
            ev = (s, self.cnt[eng] + 1, self.key[eng])
            self.rec[eng].append(lambda e, fn=fn: fn(e))
            self.pending[eng] = True
        self._mark(ev, reads, writes)
        return ev

    def dma(self, q, pairs, reads=(), writes=(), sem=None, final=False, **kw):
        self._deps(q, reads, writes)
        if sem not in self.dsem:
            self.dsem[sem] = [self.es.enter_context(self.nc.semaphore("d_" + sem)), 0]
        ent = self.dsem[sem]
        for (o, i) in pairs:
            ent[1] += 16
            s = ent[0]
            self.rec[q].append(lambda e, o=o, i=i, s=s, kw=kw: e.dma_start(out=o, in_=i, **kw).then_inc(s, 16))
        ev = (ent[0], ent[1], "dma_" + sem)
        self._mark(ev, reads, writes)
        if final:
            self.final[sem] = ev
        return ev

    def finish(self):
        for name, ev in self.final.items():
            self._wait("sp", ev)
        for e in ("pe", "act", "dve"):
            if self.cnt[e]:
                self._wait("sp", (self.sem[e], self.cnt[e], self.key[e]))

    def run(self):
        nc = self.nc
        engs = {"pe": "tensor", "act": "scalar", "dve": "vector", "pool": "gpsimd", "sp": "sync"}
        with nc.Block() as block:
            for k, attr in engs.items():
                lst = self.rec[k]
                if not lst:
                    continue

                def body(e, lst=lst):
                    for f in lst:
                        f(e)
                getattr(block, attr)(body)


class Arena:
    def __init__(self, nc, es, name, nbytes):
        self.name = name
        self.nbytes = nbytes
        self.t = es.enter_context(nc.sbuf_tensor(name, [128, nbytes // 4], F32))
        self.regs = []

    def ap(self, off, shape, dtype):
        isz = 4 if dtype == F32 else 2
        n = 1
        for s in shape:
            n *= s
        nb = n * isz
        assert off % 4 == 0 and nb % 4 == 0 and off + nb <= self.nbytes, (self.name, off, nb)
        a = self.t[:, off // 4:(off + nb) // 4]
        if dtype != F32:
            a = a.bitcast(dtype)
        if len(shape) == 2:
            a = a.rearrange("p (a b) -> p a b", a=shape[0])
        elif len(shape) == 3:
            a = a.rearrange("p (a b c) -> p a b c", a=shape[0], b=shape[1])
        return a

    def bufs(self, name, off, nbytes, n=1):
        assert off + nbytes <= self.nbytes, (self.name, name, off, nbytes)
        new = [Buf(f"{name}{k}") for k in range(n)]
        for (lo, hi, bs) in self.regs:
            if lo < off + nbytes and off < hi:
                for b in new:
                    for o in bs:
                        b.alias.append(o)
                        o.alias.append(b)
        self.regs.append((off, off + nbytes, new))
        return new


def t5_bucket_np(d):
    n = np.maximum(d, 0)
    nf = np.maximum(n, 1).astype(np.float32)
    large = 16 + (np.log(nf / np.float32(16)) / np.float32(math.log(128 / 16)) * np.float32(16)).astype(np.int32)
    large = np.minimum(large, 31)
    return np.where(n < 16, n, large)


def build_program():
    nc = bass.Bass("TRN2", target_bir_lowering=False)

    def din(name, shape):
        return nc.dram_tensor(name, shape, F32, kind="ExternalInput").ap()

    def dout(name, shape):
        return nc.dram_tensor(name, shape, F32, kind="ExternalOutput").ap()

    xin = din("xin", [NROWS, D])
    pin = din("pin", [4, NROWS, 256])
    spool = din("spool", [2, 16, 15, D])
    sconv = din("sconv", [4, 16, 2, 2 * FF])
    ckin = din("ck", [16, 128, 256])
    cvin = din("cv", [16, 128, 256])
    colsd = din("cols", [128, NCOLS])
    ccd = din("cc", [128, NCC])
    relbd = din("relb", [32, 16])
    ohd = din("oh", [33, 384])
    sinksd = din("sinks", [2, 16])
    wupd = din("wup", [4, NJ, 128, 2048])
    wdnd = din("wdn", [4, NJ, 128, 1024])
    wqd = din("wq", [2, 8, 128, 1024])
    wod = din("wo", [2, 8, 128, 1024])
    wgd = din("wg", [4, 8, 128, 1024])
    wprd = din("wpr", [4, 2, 128, 1024])
    wpld = din("wpl", [2, 4, 2, 128, 256])
    wkd = din("wk", [8, 128, 256])
    wvd = din("wv", [8, 128, 256])

    yout = dout("y", [NROWS, D])
    poolp_o = dout("poolp", [2, 15, D])
    pools_o = dout("pools", [2, 16, 15, D])
    convp_o = dout("convp", [4, 2, 2 * FF])
    convs_o = dout("convs", [4, 16, 2, 2 * FF])
    ckp_o = dout("ckp", [128, 256])
    cvp_o = dout("cvp", [128, 256])
    cks_o = dout("cks", [16, 128, 256])
    cvs_o = dout("cvs", [16, 128, 256])
    gdt = nc.dram_tensor("gd", [16, 384], F32, kind="Internal")
    gd = gdt.ap()

    with ExitStack() as es:
        S = Sched(nc, es)

        def sb(name, shape, dt):
            return es.enter_context(nc.sbuf_tensor(name, shape, dt))

        hT = sb("hT", [128, NCH, TA], F32)
        KT = sb("KT", [128, 2, 128 + TA], BF16)
        Vt = sb("Vt", [128, 11, 256], BF16)
        Tb = sb("Tb", [128, 16, 256], F32)
        Ts = sb("Ts", [128, 4, 256], F32)
        skb = sb("skb", [128, 32], F32)
        sks = sb("sks", [128, 2, 4], F32)
        cols = sb("colst", [128, NCOLS], F32)
        cc = sb("cct", [128, NCC], F32)
        ident_f = sb("ident_f", [128, 128], F32)
        ident_b = sb("ident_b", [128, 128], BF16)
        Jm = sb("Jm", [128, 128], F32)
        ones_b = sb("ones_b", [128, 128], BF16)
        blk_b = sb("blk_b", [128, 128], BF16)
        pcarry = sb("pcarry", [128, 2, NCH, 15], F32)
        ccarry = sb("ccarry", [128, 2, 4, 2, 44], F32)
        wup = [sb(f"wup{k}", [128, 2, 8, 128], BF16) for k in range(3)]
        wdn = [sb(f"wdn{k}", [128, 4, 1024], BF16) for k in range(2)]
        xm = sb("xm", [128, NCH, 512], BF16)
        pT = sb("pT", [128, 2, 512], BF16)
        pst = [sb(f"pst{k}", [128, 256], F32) for k in range(2)]
        sqb = [sb(f"sqb{k}", [128, 512], BF16) for k in range(2)]
        cprev = sb("cprev", [128, 44, 32], F32)
        KTs = sb("KTs", [128, 2, 64], BF16)
        Vsm = sb("Vsm", [64, 256], BF16)
        Ks = [sb(f"Ks{k}", [128, 2, 256], BF16) for k in range(2)]
        Vs = [sb(f"Vs{k}", [128, 256], BF16) for k in range(2)]
        Vp = [sb(f"Vp{k}", [128, 256], BF16) for k in range(2)]
        small = sb("small", [128, 64], F32)
        relb_aug = sb("relb_aug", [33, 16], F32)
        xns = sb("xns", [128, 64], F32)
        xnsB = Buf("xns")
        ugs = [sb(f"ugs{hf}", [128, 96], F32) for hf in range(2)]
        ugsB = [Buf(f"ugs{hf}") for hf in range(2)]
        qs2 = sb("qs2", [128, 2, 16, 16], BF16)
        qs2B = Buf("qs2")

        hB = {(c, b): Buf(f"h{c}_{b}") for c in range(NCH) for b in range(3)}
        KTB = [Buf(f"KT{b}") for b in range(3)]
        KTcB = Buf("KTc")
        VtB = [Buf(f"Vt{b}") for b in range(3)]
        VtcB = Buf("Vtc")
        TbB, TsB, skbB, sksB, colsB, ccB = Buf("Tb"), Buf("Ts"), Buf("skb"), Buf("sks"), Buf("cols"), Buf("cc")
        idfB, idbB, JB, onesB, blkB = Buf("idf"), Buf("idb"), Buf("J"), Buf("ones"), Buf("blk")
        pcB = {(i, c): Buf(f"pc{i}_{c}") for i in range(2) for c in range(NCH)}
        ccB2 = {(p_, i, ch): Buf(f"ccar{p_}_{i}_{ch}") for p_ in range(2) for i in range(4) for ch in range(44)}
        wupB = [Buf(f"wup{k}") for k in range(3)]
        wdnB = [Buf(f"wdn{k}") for k in range(2)]
        xmB = Buf("xm")
        pTB = Buf("pT")
        pstB = [Buf(f"pst{k}") for k in range(2)]
        sqB = [Buf(f"sq{k}") for k in range(2)]
        cprevB = [Buf(f"cprev{ch}") for ch in range(44)]
        KTsB, VsmB = Buf("KTs"), Buf("Vsm")
        KsB = [Buf(f"Ks{k}") for k in range(2)]
        VsB = [Buf(f"Vs{k}") for k in range(2)]
        VpB = [Buf(f"Vp{k}") for k in range(2)]
        smallB = [Buf(f"small{k}") for k in range(8)]
        relbB = Buf("relb")
        gdB = Buf("gd")

        AB = Arena(nc, es, "arenaB", 32768)
        AC = Arena(nc, es, "arenaC", 20480)
        AF_ = Arena(nc, es, "arenaF", 18432)

        xnf = AB.ap(0, (NCH, TA), BF16)
        xnfB = AB.bufs("xnf", 0, NCH * TA * 2, 3)
        actT = AB.ap(20480, (4, TA), BF16)
        _actl = AB.bufs("act", 20480, 4 * TA * 2, 12)
        actB = {(jj, b): _actl[jj * 3 + b] for jj in range(4) for b in range(3)}
        wq = AB.ap(0, (8, 1024), BF16)
        wqB = AB.bufs("wq", 0, 16384)[0]
        wo = AB.ap(16384, (8, 1024), BF16)
        woB = AB.bufs("wo", 16384, 16384)[0]
        wpl = AB.ap(0, (4, 2, 256), BF16)
        wplB = AB.bufs("wpl", 0, 4096)[0]
        Hk = AB.ap(0, (16, 256), F32)
        HkB = AB.bufs("Hk", 0, 16384)[0]

        qT = AC.ap(0, (NCH, 512), BF16)
        qTB = AC.bufs("qT", 0, 8192)[0]
        OT = AC.ap(8192, (NCH, 512), BF16)
        OTB = AC.bufs("OT", 8192, 8192)[0]
        ckst = [AC.ap(16384 + 1024 * k, (256,), F32) for k in range(2)]
        ckstB = [AC.bufs(f"ckst{k}", 16384 + 1024 * k, 1024)[0] for k in range(2)]
        cvst = [AC.ap(18432 + 1024 * k, (256,), F32) for k in range(2)]
        cvstB = [AC.bufs(f"cvst{k}", 18432 + 1024 * k, 1024)[0] for k in range(2)]
        wg = AC.ap(0, (8, 1024), BF16)
        wgB = AC.bufs("wg", 0, 16384)[0]
        wpr = AC.ap(16384, (2, 1024), BF16)
        wprB = AC.bufs("wpr", 16384, 4096)[0]
        wk = AC.ap(0, (8, 256), BF16)
        wkB = AC.bufs("wk", 0, 4096)[0]
        wv = AC.ap(4096, (8, 256), BF16)
        wvB = AC.bufs("wv", 4096, 4096)[0]
        spst = [AC.ap(4096 * k, (1024,), F32) for k in range(2)]
        spstB = [AC.bufs(f"spst{k}", 4096 * k, 4096)[0] for k in range(2)]

        cbuf = [[AF_.ap((s * 2 + hf) * 2056, (514,), F32) for hf in range(2)] for s in range(3)]
        cbB = [[AF_.bufs(f"cb{s}{hf}", (s * 2 + hf) * 2056, 2056)[0] for hf in range(2)] for s in range(3)]
        xe = [AF_.ap(2108 * k, (527,), F32) for k in range(2)]
        xeB = [AF_.bufs(f"xe{k}", 2108 * k, 2108)[0] for k in range(2)]
        tab = [AF_.ap(2108 * (2 + k), (527,), F32) for k in range(2)]
        tabB = [AF_.bufs(f"tab{k}", 2108 * (2 + k), 2108)[0] for k in range(2)]
        nr = [AF_.ap(8432 + 2048 * k, (512,), F32) for k in range(2)]
        nrB = [AF_.bufs(f"nr{k}", 8432 + 2048 * k, 2048)[0] for k in range(2)]
        scb = [AF_.ap(1024 * k, (256,), F32) for k in range(8)]
        scB = [AF_.bufs(f"sc{k}", 1024 * k, 1024)[0] for k in range(8)]
        eb = [AF_.ap(8192 + 512 * k, (256,), BF16) for k in range(8)]
        ebB = [AF_.bufs(f"e{k}", 8192 + 512 * k, 512)[0] for k in range(8)]
        pTs = [AF_.ap(12528 + 512 * k, (256,), BF16) for k in range(4)]
        pTsB = [AF_.bufs(f"pTs{k}", 12528 + 512 * k, 512)[0] for k in range(4)]
        pb = [AF_.ap(14576 + 512 * k, (256,), BF16) for k in range(4)]
        pbB = [AF_.bufs(f"p{k}", 14576 + 512 * k, 512)[0] for k in range(4)]
        knf = AF_.ap(5120, (512,), F32)
        knfB = AF_.bufs("knf", 5120, 2048)[0]
        sg = [AF_.ap(2048 * k, (512,), F32) for k in range(2)]
        sgB = [AF_.bufs(f"sg{k}", 2048 * k, 2048)[0] for k in range(2)]
        tmpb = [AF_.ap(4096 + 2048 * k, (512,), F32) for k in range(2)]
        tmpB = [AF_.bufs(f"tmp{k}", 4096 + 2048 * k, 2048)[0] for k in range(2)]
        xst = [AF_.ap(4096 * k, (1024,), F32) for k in range(2)]
        xstB = [AF_.bufs(f"xst{k}", 4096 * k, 4096)[0] for k in range(2)]
        scst = AF_.ap(12528, (1408,), F32)
        scstB = AF_.bufs("scst", 12528, 5632)[0]
        ost = AF_.ap(12528, (1024,), F32)
        ostB = AF_.bufs("ost", 12528, 4096)[0]
        gsb = AF_.ap(0, (384,), F32)
        gsbB = AF_.bufs("gsb", 0, 1536)[0]
        oht = AF_.ap(2048, (384,), F32)
        ohB = AF_.bufs("oht", 2048, 1536)[0]

        pst_t = [es.enter_context(nc.psum_tensor(f"ps{k}", [128, 512], F32)) for k in range(8)]
        psB = [Buf(f"ps{k}", excl=True) for k in range(8)]
        free = list(range(8))

        def bank(hold=False):
            k = free.pop(0)
            if not hold:
                free.append(k)
            return pst_t[k], psB[k], k

        def release(k):
            free.append(k)

        flip = [0]

        def evac_eng():
            flip[0] ^= 1
            return "act" if flip[0] else "dve"

        def copy_op(eng, out, in_, reads, writes):
            if eng == "act":
                S.op("act", lambda e: e.activation(out=out, in_=in_, func=AF.Copy), reads=reads, writes=writes)
            else:
                S.op("dve", lambda e: e.tensor_copy(out=out, in_=in_), reads=reads, writes=writes)

        def col(k):
            return cols[:, k:k + 1]

        S.dma("sp", [(cols[:], colsd)], writes=[colsB], sem="cols")
        S.dma("sp", [(cc[:], ccd)], writes=[ccB], sem="cc")
        S.op("pool", lambda e: e.memset(ident_f[:], 0.0), writes=[idfB])
        S.op("pool", lambda e: e.affine_select(out=ident_f[:], in_=ident_f[:], compare_op=ALU.not_equal, fill=1.0,
                                                base=0, pattern=[[-1, 128]], channel_multiplier=1),
             reads=[idfB], writes=[idfB])
        S.op("pool", lambda e: e.memset(Jm[:], 0.0), writes=[JB])
        S.op("pool", lambda e: e.affine_select(out=Jm[:], in_=Jm[:], compare_op=ALU.not_equal, fill=1.0,
                                                base=-127, pattern=[[1, 128]], channel_multiplier=1),
             reads=[JB], writes=[JB])
        S.op("dve", lambda e: e.tensor_copy(out=ident_b[:], in_=ident_f[:]), reads=[idfB], writes=[idbB])
        S.op("dve", lambda e: e.memset(ones_b[:], 1.0), writes=[onesB])
        S.op("dve", lambda e: e.memset(blk_b[:], 0.0), writes=[blkB])
        S.op("dve", lambda e: e.memset(blk_b[0:64, 0:64], 1.0), writes=[blkB])
        S.op("dve", lambda e: e.memset(blk_b[64:128, 64:128], 1.0), writes=[blkB])
        S.op("dve", lambda e: e.memset(KT[:, :, 0:128], 0.0), writes=[KTcB])
        S.op("dve", lambda e: e.memset(Vt[:, 0, :], 0.0), writes=[VtcB])
        S.op("dve", lambda e: e.memset(pcarry[:], 0.0), writes=list(pcB.values()))
        S.op("dve", lambda e: e.memset(ccarry[:], 0.0), writes=list(ccB2.values()))
        for k in range(2):
            S.op("dve", lambda e, k=k: e.memset(Ks[k][:], 0.0), writes=[KsB[k]])
            S.op("dve", lambda e, k=k: e.memset(Vp[k][:], 0.0), writes=[VpB[k]])
        S.op("dve", lambda e: e.memset(Ts[:], 0.0), writes=[TsB])
        S.op("dve", lambda e: e.memset(sks[:], 0.0), writes=[sksB])
        S.op("dve", lambda e: e.memset(small[:], 0.0), writes=smallB)

        S.op("dve", lambda e: e.memset(relb_aug[:], 1.0), writes=[relbB])
        S.dma("sp", [(relb_aug[0:32, :], relbd)], writes=[relbB], sem="relb")
        S.dma("sp", [(oht[0:33, :], ohd)], writes=[ohB], sem="oh")
        pt, pbk, _ = bank()
        S.op("pe", lambda e: e.matmul(pt[0:16, 0:384], lhsT=relb_aug[0:33, 0:16], rhs=oht[0:33, 0:384], start=True, stop=True),
             reads=[relbB, ohB], writes=[pbk])
        S.op("act", lambda e: e.activation(out=gsb[0:16, :], in_=pt[0:16, 0:384], func=AF.Copy), reads=[pbk], writes=[gsbB])
        S.dma("sp", [(gd, gsb[0:16, :])], reads=[gsbB], writes=[gdB], sem="gd")
        S.dma("sp", [(Hk, bass.AP(gdt, 0, [[1, 128], [384, 16], [1, 256]]))], reads=[gdB], writes=[HkB], sem="hk")
        for h in range(16):
            pt, pbk, _ = bank()
            S.op("pe", lambda e, pt=pt, h=h: e.matmul(pt[:, 128:256], lhsT=Hk[:, h, 0:128], rhs=Jm[:], start=True, stop=True),
                 reads=[HkB, JB], writes=[pbk], signal=False)
            S.op("pe", lambda e, pt=pt, h=h: e.matmul(pt[:, 0:128], lhsT=Hk[:, h, 128:256], rhs=Jm[:], start=True, stop=True),
                 reads=[HkB, JB], writes=[pbk])
            copy_op(evac_eng(), Tb[:, h, :], pt[:, 0:256], [pbk], [TbB])
        for kv in range(4):
            for g in range(4):
                S.dma("sp", [(Ts[4 * g:4 * g + 4, kv, :], Tb[0:4, 4 * kv + g, :])], reads=[TbB], writes=[TsB], sem="ts")
        S.dma("sp", [(skb[:], sinksd.rearrange("a b -> (a b)").partition_broadcast(128))], writes=[skbB], sem="skb")
        for j in range(2):
            for kv in range(4):
                for g in range(4):
                    S.dma("sp", [(sks[4 * g:4 * g + 4, j, kv:kv + 1],
                                  sinksd[j, 4 * kv + g:4 * kv + g + 1].partition_broadcast(4))],
                          writes=[sksB], sem="sks")

        def load_w(dst, src, buf, sem):
            S.dma("pool", [(dst, src)], writes=[buf], sem=sem)

        def load_wup(i, j):
            k = j % 3
            load_w(wup[k][:].rearrange("p h k c -> p (h k c)"), wupd[i, j], wupB[k], f"wup{k}")

        def load_wdn(i, pi):
            j0, j1 = PARTS[pi]
            k = pi % 2
            load_w(wdn[k][:, 0:j1 - j0, :], wdnd[i, j0:j1].rearrange("j p n -> p j n"), wdnB[k], f"wdn{k}")

        def load_mixer(i):
            if i < 2:
                load_w(wpl, wpld[i].rearrange("g k p n -> p g k n"), wplB, "wpl")
            else:
                load_w(wq, wqd[i - 2].rearrange("k p n -> p k n"), wqB, "wq")
                load_w(wo, wod[i - 2].rearrange("k p n -> p k n"), woB, "wo")

        def load_ple(i):
            load_w(wg, wgd[i].rearrange("k p n -> p k n"), wgB, "wg")
            load_w(wpr, wprd[i].rearrange("k p n -> p k n"), wprB, "wpr")

        import os
        SKIP = os.environ.get("KSKIP", "").split(",")

        def hreads(b):
            return [hB[(c, b)] for c in range(NCH)]

        def rstd_of(c0, N, b, scale, lhs, lhsB, src_fn, src_reads, nk, k):
            pt, pbk, _ = bank()
            for c in range(nk):
                q = c % 2
                S.op("act", lambda e, c=c, q=q: e.activation(out=sqb[q][:, 0:N], in_=src_fn(c), func=AF.Square),
                     reads=src_reads(c), writes=[sqB[q]])
                S.op("pe", lambda e, c=c, q=q, pt=pt: e.matmul(pt[:, 0:N], lhsT=lhs, rhs=sqb[q][:, 0:N], start=(c == 0), stop=(c == nk - 1)),
                     reads=[sqB[q], lhsB], writes=[pbk], signal=True)
            S.op("act", lambda e, pt=pt: e.activation(out=nr[k][:, 0:N], in_=pt[:, 0:N], func=AF.Ln, bias=col(C_EPS), scale=scale),
                 reads=[pbk, colsB], writes=[nrB[k]])
            S.op("act", lambda e: e.activation(out=nr[k][:, 0:N], in_=nr[k][:, 0:N], func=AF.Exp, scale=-0.5), reads=[nrB[k]], writes=[nrB[k]])

        def rms_block(blk, b, gbase, out, outB, k=0):
            c0, N, kind = blk
            rstd_of(c0, N, b, 1.0 / D, ones_b[:], onesB, lambda c: hT[:, c, c0:c0 + N], lambda c: [hB[(c, b)]], NCH, k)
            for c in range(NCH):
                S.op("dve", lambda e, c=c: e.scalar_tensor_tensor(out=out(c), in0=hT[:, c, c0:c0 + N], scalar=col(gbase + c),
                                                                  in1=nr[k][:, 0:N], op0=ALU.mult, op1=ALU.mult),
                     reads=[hB[(c, b)], nrB[k], colsB], writes=[outB])

        def load_x(sbi, blocks):
            row0 = 0 if sbi == 0 else TA
            tiles = []
            for (c0, N, kind) in blocks:
                if kind == "p":
                    for t in range(N // 128):
                        tiles.append((row0 + c0 + 128 * t, c0 + 128 * t, 128))
                else:
                    tiles.append((NPROMPT, c0, 64))
            for ti, (r0, cc0, rows) in enumerate(tiles):
                q = ti % 2
                b = cc0 // 512
                S.dma("sp", [(xst[q][0:rows, :], xin[r0:r0 + rows, :])], writes=[xstB[q]], sem=f"xst{q}")
                for hb in range(2):
                    pt, pbk, _ = bank()
                    for cq in range(4):
                        c = hb * 4 + cq
                        S.op("pe", lambda e, pt=pt, c=c, cq=cq, q=q, rows=rows: e.transpose(
                            out=pt[:, cq * 128:cq * 128 + rows], in_=xst[q][0:rows, c * 128:(c + 1) * 128], identity=ident_f[0:rows, 0:rows]),
                            reads=[xstB[q], idfB], writes=[pbk], signal=(cq == 3))
                    src = pt[:, :].rearrange("p (a b) -> p a b", a=4)[:, :, 0:rows]
                    copy_op(evac_eng(), hT[:, hb * 4:hb * 4 + 4, cc0:cc0 + rows], src, [pbk], [hB[(c, b)] for c in range(hb * 4, hb * 4 + 4)])

        def store_y(sbi, blocks):
            row0 = 0 if sbi == 0 else TA
            tiles = []
            for (c0, N, kind) in blocks:
                if kind == "p":
                    for t in range(N // 128):
                        tiles.append((row0 + c0 + 128 * t, c0 + 128 * t, 128))
                else:
                    tiles.append((NPROMPT, c0, 64))
            for ti, (r0, cc0, rows) in enumerate(tiles):
                q = ti % 2
                b = cc0 // 512
                for hb in range(2):
                    pt, pbk, _ = bank()
                    for cq in range(4):
                        c = hb * 4 + cq
                        S.op("pe", lambda e, pt=pt, c=c, cq=cq, rows=rows, cc0=cc0: e.transpose(
                            out=pt[0:rows, cq * 128:(cq + 1) * 128], in_=hT[:, c, cc0:cc0 + rows], identity=ident_f[:]),
                            reads=[hB[(c, b)], idfB], writes=[pbk], signal=(cq == 3))
                    copy_op(evac_eng(), xst[q][0:rows, hb * 512:(hb + 1) * 512], pt[0:rows, :], [pbk], [xstB[q]])
                S.dma("sp", [(yout[r0:r0 + rows, :], xst[q][0:rows, :])], reads=[xstB[q]], sem=f"xst{q}", final=True)

        def pool_block(i, sbi, blk, b):
            c0, N, kind = blk
            rstd_of(c0, N, b, 1.0 / D, ones_b[:], onesB, lambda c: hT[:, c, c0:c0 + N], lambda c: [hB[(c, b)]], NCH, 0)
            first = (sbi == 0 and b == 0)
            if kind == "s":
                for hf in range(2):
                    S.dma("sp", [(spst[hf][0:120, :], spool[i, 8 * hf:8 * hf + 8].rearrange("s r d -> (s r) d"))],
                          writes=[spstB[hf]], sem=f"spst{hf}")
                for hf in range(2):
                    S.dma("sp", [(pools_o[i, 8 * hf + s_, 0:11, :], spst[hf][15 * s_ + 4:15 * s_ + 15, :]) for s_ in range(8)],
                          reads=[spstB[hf]], sem=f"spo{hf}", final=True)
                hold = [bank(hold=True), bank(hold=True)]
            for c in range(NCH):
                g = c // 2
                w = POOL_W[g]
                q = c % 2
                X = xe[q]
                if kind == "p":
                    L = 15 + N
                    S.op("act", lambda e, X=X, c=c: e.activation(out=X[:, 0:15], in_=pcarry[:, i, c, :], func=AF.Copy),
                         reads=[pcB[(i, c)]], writes=[xeB[q]])
                    S.op("dve", lambda e, X=X, c=c: e.scalar_tensor_tensor(out=X[:, 15:15 + N], in0=hT[:, c, c0:c0 + N], scalar=col(C_NM + i * 8 + c),
                                                                           in1=nr[0][:, 0:N], op0=ALU.mult, op1=ALU.mult),
                         reads=[hB[(c, b)], nrB[0], colsB], writes=[xeB[q]])
                    if first:
                        S.op("dve", lambda e, X=X: e.tensor_scalar(out=X[:, 15 + 241:15 + 256], in0=X[:, 15 + 241:15 + 256],
                                                                   scalar1=cc[:, CC_M:CC_M + 1], scalar2=None, op0=ALU.mult),
                             reads=[xeB[q], ccB], writes=[xeB[q]])
                    S.op("act", lambda e, X=X, c=c: e.activation(out=pcarry[:, i, c, :], in_=X[:, N:N + 15], func=AF.Copy),
                         reads=[xeB[q]], writes=[pcB[(i, c)]])
                    xnew = X[:, 15:15 + N]
                    dout_ap = xm[:, c, 0:N]
                else:
                    L = 304
                    X3 = X[:, 0:304].rearrange("p (s k) -> p s k", k=19)
                    pt, pbk, _ = bank()
                    for hf in range(2):
                        S.op("pe", lambda e, pt=pt, hf=hf, c=c: e.transpose(out=pt[:, hf * 120:hf * 120 + 120], in_=spst[hf][0:120, c * 128:(c + 1) * 128],
                                                                            identity=ident_f[0:120, 0:120]),
                             reads=[spstB[hf], idfB], writes=[pbk], signal=(hf == 1))
                    copy_op("act", X3[:, :, 0:15], pt[:, 0:240].rearrange("p (s k) -> p s k", k=15), [pbk], [xeB[q]])
                    hv = hT[:, c, c0:c0 + 64].rearrange("p (t s) -> p s t", s=16)
                    nv = nr[0][:, 0:64].rearrange("p (t s) -> p s t", s=16)
                    S.op("dve", lambda e, c=c: e.scalar_tensor_tensor(out=xns[:, :], in0=hT[:, c, c0:c0 + 64], scalar=col(C_NM + i * 8 + c),
                                                                      in1=nr[0][:, 0:64], op0=ALU.mult, op1=ALU.mult),
                         reads=[hB[(c, b)], nrB[0], colsB], writes=[xnsB])
                    S.op("act", lambda e, X3=X3: e.activation(out=X3[:, :, 15:19], in_=xns[:, :].rearrange("p (t s) -> p s t", s=16), func=AF.Copy),
                         reads=[xnsB], writes=[xeB[q]])
                    hp, hpb, _ = hold[c // 4]
                    S.op("pe", lambda e, hp=hp, c=c: e.transpose(out=hp[0:64, (c % 4) * 128:(c % 4 + 1) * 128], in_=xns[:, :], identity=ident_f[:]),
                         reads=[xnsB, idfB], writes=[hpb])
                    xnew = X3[:, :, 15:19]
                    dout_ap = xm[:, c, 0:64].rearrange("p (t s) -> p s t", s=16)
                a, aB = X, xeB[q]
                lo = 0
                for si, s in enumerate((1, 2, 4, 8)[:g + 1]):
                    lo2 = lo + s
                    tb_, tbB_ = tab[si % 2], tabB[si % 2]
                    S.op("dve", lambda e, a=a, tb_=tb_, lo2=lo2, s=s, L=L: e.tensor_tensor(out=tb_[:, lo2:L], in0=a[:, lo2:L], in1=a[:, lo2 - s:L - s], op=ALU.add),
                         reads=[aB], writes=[tbB_])
                    a, aB, lo = tb_, tbB_, lo2
                if kind == "p":
                    asum = a[:, 15:15 + N]
                else:
                    asum = a[:, 0:304].rearrange("p (s k) -> p s k", k=19)[:, :, 15:19]
                S.op("dve", lambda e, asum=asum, xnew=xnew, dout_ap=dout_ap, w=w: e.scalar_tensor_tensor(out=dout_ap, in0=asum, scalar=1.0 / w, in1=xnew,
                                                                                                      op0=ALU.mult, op1=ALU.subtract),
                     reads=[aB, xeB[q]], writes=[xmB])
                if first:
                    t15 = tab[(g + 1) % 2][:, 0:15]
                    S.op("dve", lambda e, a=a, t15=t15, g=g: e.tensor_tensor(out=t15, in0=a[:, 15 + 256:15 + 271], in1=cc[:, CC_INV + 15 * g:CC_INV + 15 * g + 15], op=ALU.mult),
                         reads=[aB, ccB], writes=[tabB[(g + 1) % 2]])
                    S.op("dve", lambda e, X=X, t15=t15, c=c: e.tensor_tensor(out=xm[:, c, 256:271], in0=t15, in1=X[:, 15 + 256:15 + 271], op=ALU.subtract),
                         reads=[tabB[(g + 1) % 2], xeB[q]], writes=[xmB])
            if kind == "s":
                for hb in range(2):
                    hp, hpb, hk = hold[hb]
                    copy_op(evac_eng(), ost[0:64, hb * 512:(hb + 1) * 512], hp[0:64, :], [hpb], [ostB])
                    release(hk)
                S.dma("sp", [(pools_o[i, :, 11 + t, :], ost[16 * t:16 * t + 16, :]) for t in range(4)], reads=[ostB], sem="ost", final=True)
            M = N
            for g in range(4):
                for mo in range(2):
                    pt, pbk, _ = bank()
                    for kc in range(2):
                        S.op("pe", lambda e, pt=pt, g=g, mo=mo, kc=kc: e.matmul(pt[:, 0:M], lhsT=wpl[:, g, kc, mo * 128:(mo + 1) * 128], rhs=xm[:, 2 * g + kc, 0:M],
                                                                               start=(kc == 0), stop=(kc == 1)),
                             reads=[wplB, xmB], writes=[pbk], signal=(kc == 1))
                    m = 2 * g + mo
                    S.op("dve", lambda e, pt=pt, m=m: e.scalar_tensor_tensor(out=hT[:, m, c0:c0 + M], in0=pt[:, 0:M], scalar=col(C_PSC + i * 8 + m),
                                                                            in1=hT[:, m, c0:c0 + M], op0=ALU.mult, op1=ALU.add),
                         reads=[pbk, hB[(m, b)], colsB], writes=[hB[(m, b)]])

        def qk_norm(pt, pbk, N, gcol, out_bf, out_bf_B, out_f32=None, out_f32_B=None, k=1):
            rstd_of(0, N, 0, 1.0 / 64, blk_b[:], blkB, lambda c: pt[:, 0:N], lambda c: [pbk], 1, k)
            S.op("dve", lambda e: e.scalar_tensor_tensor(out=out_bf, in0=pt[:, 0:N], scalar=col(gcol), in1=nr[k][:, 0:N], op0=ALU.mult, op1=ALU.mult),
                 reads=[pbk, nrB[k], colsB], writes=[out_bf_B])
            if out_f32 is not None:
                S.op("dve", lambda e: e.scalar_tensor_tensor(out=out_f32, in0=pt[:, 0:N], scalar=col(gcol), in1=nr[k][:, 0:N], op0=ALU.mult, op1=ALU.mult),
                     reads=[pbk, nrB[k], colsB], writes=[out_f32_B])

        def kv_block(sbi, blk, b, last_prompt):
            c0, N, kind = blk
            rms_block(blk, b, C_KVN, lambda c: xm[:, c, 0:N], xmB, k=0)
            want_cache = (sbi == 1) and (kind == "s" or last_prompt) and ("kvout" not in SKIP)
            for kp in range(2):
                if kind == "s" and "skv_k" in SKIP:
                    continue
                pt, pbk, _ = bank()
                for k in range(NCH):
                    S.op("pe", lambda e, pt=pt, k=k, kp=kp: e.matmul(pt[:, 0:N], lhsT=wk[:, k, kp * 128:(kp + 1) * 128], rhs=xm[:, k, 0:N], start=(k == 0), stop=(k == 7)),
                         reads=[wkB, xmB], writes=[pbk], signal=(k == 7))
                if kind == "p":
                    qk_norm(pt, pbk, N, C_KN, KT[:, kp, 128 + c0:128 + c0 + N], KTB[b],
                            knf[:, 0:N] if want_cache else None, knfB)
                else:
                    qk_norm(pt, pbk, N, C_KN, KTs[:, kp, 0:64], KTsB, knf[:, 0:64], knfB)
                if want_cache:
                    rows = 128 if kind == "p" else 64
                    p2, p2b, _ = bank()
                    src = knf[:, N - 128:N] if kind == "p" else knf[:, 0:64]
                    S.op("pe", lambda e, p2=p2, src=src, rows=rows: e.transpose(out=p2[0:rows, 0:128], in_=src, identity=ident_f[:]),
                         reads=[knfB, idfB], writes=[p2b])
                    copy_op(evac_eng(), ost[0:rows, kp * 128:(kp + 1) * 128], p2[0:rows, 0:128], [p2b], [ostB])
            if want_cache:
                if kind == "p":
                    S.dma("sp", [(ckp_o, ost[:, 0:256])], reads=[ostB], sem="ost", final=True)
                else:
                    S.dma("sp", [(cks_o[:, 124 + t, :], ost[16 * t:16 * t + 16, 0:256]) for t in range(4)], reads=[ostB], sem="ost", final=True)
            if kind == "p":
                for t in range(N // 128):
                    pt, pbk, _ = bank()
                    for k in range(NCH):
                        S.op("pe", lambda e, pt=pt, k=k, t=t: e.matmul(pt[:, 0:256], lhsT=xm[:, k, 128 * t:128 * t + 128], rhs=wv[:, k, :], start=(k == 0), stop=(k == 7)),
                             reads=[wvB, xmB], writes=[pbk], signal=(k == 7))
                    gt = c0 // 128 + t
                    copy_op(evac_eng(), Vt[:, 1 + gt, :], pt[:, 0:256], [pbk], [VtB[b]])
                    if want_cache and t == N // 128 - 1:
                        copy_op("act", ost[:, 256:512], pt[:, 0:256], [pbk], [ostB])
                        S.dma("sp", [(cvp_o, ost[:, 256:512])], reads=[ostB], sem="ost", final=True)
            elif "skv_v" not in SKIP:
                pt, pbk, _ = bank()
                for k in range(NCH):
                    S.op("pe", lambda e, pt=pt, k=k: e.matmul(pt[0:64, 0:256], lhsT=xm[:, k, 0:64], rhs=wv[:, k, :], start=(k == 0), stop=(k == 7)),
                         reads=[wvB, xmB], writes=[pbk], signal=(k == 7))
                copy_op("dve", Vsm[:, :], pt[0:64, 0:256], [pbk], [VsmB])
                if "skv_o" not in SKIP:
                    copy_op("act", ost[0:64, 256:512], pt[0:64, 0:256], [pbk], [ostB])
                    S.dma("sp", [(cvs_o[:, 124 + t, :], ost[16 * t:16 * t + 16, 256:512]) for t in range(4)], reads=[ostB], sem="ost", final=True)

        sm_ctr = [0]

        def attn_stage_a(items):
            gsl = (sm_ctr[0] // 4) % 2
            for idx, it in enumerate(items):
                it["k"] = sm_ctr[0] % 8
                it["k4"] = sm_ctr[0] % 4
                sm_ctr[0] += 1
                it["sm"] = small[0:it["P"], 32 * gsl + 8 * idx:32 * gsl + 8 * idx + 8]
                it["smB"] = smallB[4 * gsl + idx]
                it["gsm"] = small[0:it["P"], 32 * gsl:32 * gsl + 32].rearrange("p (i c) -> p i c", c=8)
            for it in items:
                k, P, sm, smB = it["k"], it["P"], it["sm"], it["smB"]
                if not it["mask"]:
                    S.op("dve", lambda e, it=it, k=k, P=P, sm=sm: e.tensor_scalar(out=scb[k][0:P, :], in0=it["pss"], scalar1=0.125, scalar2=it["sink"],
                                                                                   op0=ALU.mult, op1=ALU.max, accum_out=sm[:, 0:1]),
                         reads=[it["pssB"], it["sinkB"]], writes=[scB[k], smB])
                else:
                    S.op("dve", lambda e, it=it, k=k, P=P: e.tensor_scalar(out=scb[k][0:P, :], in0=it["pss"], scalar1=0.125, scalar2=None, op0=ALU.mult),
                         reads=[it["pssB"]], writes=[scB[k]])
                    S.op("dve", lambda e, k=k: e.tensor_tensor(out=scb[k][:, 0:128], in0=scb[k][:, 0:128], in1=cc[:, CC_FM:CC_FM + 128], op=ALU.add),
                         reads=[scB[k], ccB], writes=[scB[k]])
                    S.op("dve", lambda e, k=k, P=P, sm=sm: e.reduce_max(out=sm[:, 6:7], in_=scb[k][0:P, :], axis=AX.X), reads=[scB[k]], writes=[smB])
                    S.op("dve", lambda e, it=it, sm=sm: e.tensor_scalar(out=sm[:, 0:1], in0=sm[:, 6:7], scalar1=it["sink"], scalar2=None, op0=ALU.max),
                         reads=[smB, it["sinkB"]], writes=[smB])
            for it in items:
                sm, smB = it["sm"], it["smB"]
                S.op("dve", lambda e, sm=sm: e.tensor_scalar(out=sm[:, 1:2], in0=sm[:, 0:1], scalar1=-1.0, scalar2=None, op0=ALU.mult), reads=[smB], writes=[smB])
            for it in items:
                k, P, sm, smB = it["k"], it["P"], it["sm"], it["smB"]
                S.op("act", lambda e, k=k, P=P, sm=sm: e.activation(out=eb[k][0:P, :], in_=scb[k][0:P, :], func=AF.Exp, bias=sm[:, 1:2], accum_out=sm[:, 2:3]),
                     reads=[scB[k], smB], writes=[ebB[k], smB])
                S.op("act", lambda e, it=it, sm=sm: e.activation(out=sm[:, 3:4], in_=it["sink"], func=AF.Exp, bias=sm[:, 1:2]),
                     reads=[smB, it["sinkB"]], writes=[smB])

        def attn_stage_b(items):
            ptt, pttB, _ = bank()
            ptb = ptt[:, :].bitcast(BF16)
            g0 = items[0]
            gsm, gB = g0["gsm"], [it["smB"] for it in items]
            S.op("dve", lambda e, gsm=gsm: e.tensor_tensor(out=gsm[:, :, 4], in0=gsm[:, :, 2], in1=gsm[:, :, 3], op=ALU.add), reads=gB, writes=gB)
            S.op("dve", lambda e, gsm=gsm: e.reciprocal(out=gsm[:, :, 5], in_=gsm[:, :, 4]), reads=gB, writes=gB)
            for it in items:
                k, k4, P, sm, smB = it["k"], it["k4"], it["P"], it["sm"], it["smB"]
                S.op("dve", lambda e, k=k, k4=k4, P=P, sm=sm: e.tensor_scalar(out=pb[k4][0:P, :], in0=eb[k][0:P, :], scalar1=sm[:, 5:6], scalar2=None, op0=ALU.mult),
                     reads=[ebB[k], smB], writes=[pbB[k4]])
            for idx, it in enumerate(items):
                k4, P = it["k4"], it["P"]
                base = idx * 2 * P
                for hh in range(2):
                    S.op("pe", lambda e, k4=k4, P=P, hh=hh, base=base: e.transpose(out=ptb[:, base + hh * P:base + (hh + 1) * P], in_=pb[k4][0:P, hh * 128:(hh + 1) * 128], identity=ident_b[0:P, 0:P]),
                         reads=[pbB[k4], idbB], writes=[pttB], signal=(hh == 1))
            for idx, it in enumerate(items):
                k4, P = it["k4"], it["P"]
                base = idx * 2 * P
                copy_op("act", pTs[k4][:, 0:2 * P], ptb[:, base:base + 2 * P], [pttB], [pTsB[k4]])
            for it in items:
                k4, P = it["k4"], it["P"]
                for hh in range(2):
                    vap, vB = it["v"][hh]
                    S.op("pe", lambda e, it=it, k4=k4, P=P, hh=hh, vap=vap: e.matmul(it["po"], lhsT=vap, rhs=pTs[k4][:, hh * P:(hh + 1) * P], start=(hh == 0), stop=(hh == 1)),
                         reads=[pTsB[k4], vB], writes=[it["poB"]], signal=(hh == 1))

        pendg = [None]

        def attn_block(i, sbi, blk, b):
            j = i - 2
            c0, N, kind = blk
            rms_block(blk, b, C_NM + i * 8, lambda c: xm[:, c, 0:N], xmB, k=0)
            for cp in range(NCH):
                pt, pbk, _ = bank()
                for k in range(NCH):
                    S.op("pe", lambda e, pt=pt, k=k, cp=cp: e.matmul(pt[:, 0:N], lhsT=wq[:, k, cp * 128:(cp + 1) * 128], rhs=xm[:, k, 0:N], start=(k == 0), stop=(k == 7)),
                         reads=[wqB, xmB], writes=[pbk], signal=(k == 7))
                qk_norm(pt, pbk, N, C_QN + j, qT[:, cp, 0:N], qTB, k=1)
            if kind == "p":
                groups = [(t, cpp) for t in range(N // 128) for cpp in range(0, NCH, 2)]

                def emit_scores(t, cpp):
                    tcol = c0 + 128 * t
                    gt = tcol // 128
                    kb_prev = (KTcB, VtcB) if gt == 0 else (KTB[(tcol - 128) // 512], VtB[(tcol - 128) // 512])
                    items = []
                    for cp in (cpp, cpp + 1):
                        pss, pssB, _ = bank()
                        kp = cp // 4
                        for hf in range(2):
                            kv = 2 * kp + hf
                            head = 4 * kv + cp % 4
                            r0, r1 = 64 * hf, 64 * hf + 64
                            S.op("pe", lambda e, pss=pss, hf=hf, r0=r0, r1=r1, cp=cp, t=t, kp=kp, tcol=tcol: e.matmul(
                                pss[:, 256 * hf:256 * hf + 256], lhsT=qT[r0:r1, cp, 128 * t:128 * t + 128], rhs=KT[r0:r1, kp, tcol:tcol + 256], start=True, stop=False),
                                reads=[qTB, KTB[b], kb_prev[0]], writes=[pssB], signal=False)
                            S.op("pe", lambda e, pss=pss, hf=hf, head=head: e.matmul(
                                pss[:, 256 * hf:256 * hf + 256], lhsT=ident_f[:, :], rhs=Tb[:, head, :], start=False, stop=True),
                                reads=[idfB, TbB], writes=[pssB])
                            items.append(dict(pss=pss[:, 256 * hf:256 * hf + 256], pssB=pssB, bias=Tb[:, head, :], biasB=TbB,
                                              sink=skb[:, 16 * j + head:16 * j + head + 1], sinkB=skbB, mask=(sbi == 0 and gt == 2),
                                              v=[(Vt[:, gt, kv * 64:(kv + 1) * 64], kb_prev[1]), (Vt[:, gt + 1, kv * 64:(kv + 1) * 64], VtB[b])],
                                              P=128, cp=cp, rows=(r0, r1)))
                    return items

                def fin_p(items, t):
                    outs = {}
                    for it in items:
                        if it["cp"] not in outs:
                            outs[it["cp"]] = bank()
                        psO, psOB, _ = outs[it["cp"]]
                        it["po"] = psO[it["rows"][0]:it["rows"][1], 0:128]
                        it["poB"] = psOB
                    attn_stage_b(items)

                    def cp_out(outs=outs, t=t):
                        for cp, (psO, psOB, _) in outs.items():
                            copy_op("act", OT[:, cp, 128 * t:128 * t + 128], psO[:, 0:128], [psOB], [OTB])
                    return cp_out

                sc_items = [None] * len(groups)
                sc_items[0] = emit_scores(*groups[0])
                prev = None
                late = None
                for gi, (t, cpp) in enumerate(groups):
                    if gi + 1 < len(groups):
                        sc_items[gi + 1] = emit_scores(*groups[gi + 1])
                    if late is not None:
                        late()
                        late = None
                    attn_stage_a(sc_items[gi])
                    if prev is not None:
                        late = fin_p(*prev)
                    prev = (sc_items[gi], t)
                if late is not None:
                    late()
                if prev is not None:
                    fin_p(*prev)()
            elif "sattn" in SKIP:
                S.op("dve", lambda e: e.memset(OT[:, :, 0:64], 0.0), writes=[OTB])
            else:
                for cp in range(NCH):
                    S.op("dve", lambda e, cp=cp: e.tensor_copy(out=qs2[:, cp // 4, :, 4 * (cp % 4):4 * (cp % 4) + 4],
                                                             in_=qT[:, cp, 0:64].rearrange("p (t s) -> p s t", s=16)),
                         reads=[qTB], writes=[qs2B])
                for s in range(16):
                    q = s % 2
                    S.dma("sp", [(ckst[q][:, :], ckin[s])], writes=[ckstB[q]], sem=f"ckst{q}")
                    S.dma("sp", [(cvst[q][:, :], cvin[s])], writes=[cvstB[q]], sem=f"cvst{q}")
                    if j == 0:
                        S.dma("sp", [(cks_o[s, 0:124, :], ckst[q][4:128, :]), (cvs_o[s, 0:124, :], cvst[q][4:128, :])],
                              reads=[ckstB[q], cvstB[q]], sem=f"cko{q}", final=True)
                    pt, pbk, _ = bank()
                    for kp in range(2):
                        S.op("pe", lambda e, pt=pt, kp=kp, q=q: e.transpose(out=pt[:, kp * 128:(kp + 1) * 128], in_=ckst[q][:, kp * 128:(kp + 1) * 128], identity=ident_f[:]),
                             reads=[ckstB[q], idfB], writes=[pbk], signal=(kp == 1))
                    copy_op("act", Ks[q][:, :, 0:128], pt[:, 0:256].rearrange("p (a b) -> p a b", a=2), [pbk], [KsB[q]])
                    S.op("dve", lambda e, q=q, s=s: e.tensor_copy(out=Ks[q][:, :, 128:132], in_=KTs[:, :, s:64:16]), reads=[KTsB], writes=[KsB[q]])
                    copy_op("dve", Vs[q][:, :], cvst[q][:, :], [cvstB[q]], [VsB[q]])
                    S.dma("sp", [(Vp[q][t:t + 1, :], Vsm[16 * t + s:16 * t + s + 1, :]) for t in range(4)], reads=[VsmB], writes=[VpB[q]], sem=f"vp{q}")
                    items = []
                    for kp in range(2):
                        pss, pssB, _ = bank()
                        for hf in range(2):
                            kv = 2 * kp + hf
                            r0, r1 = 64 * hf, 64 * hf + 64
                            S.op("pe", lambda e, pss=pss, hf=hf, r0=r0, r1=r1, kp=kp, q=q, s=s: e.matmul(
                                pss[0:16, 256 * hf:256 * hf + 256], lhsT=qs2[r0:r1, kp, s, :], rhs=Ks[q][r0:r1, kp, :], start=True, stop=False),
                                reads=[qs2B, KsB[q]], writes=[pssB], signal=False)
                            S.op("pe", lambda e, pss=pss, hf=hf, kv=kv: e.matmul(
                                pss[0:16, 256 * hf:256 * hf + 256], lhsT=ident_f[0:16, 0:16], rhs=Ts[0:16, kv, :], start=False, stop=True),
                                reads=[idfB, TsB], writes=[pssB])
                            items.append(dict(pss=pss[0:16, 256 * hf:256 * hf + 256], pssB=pssB, bias=Ts[0:16, kv, :], biasB=TsB,
                                              sink=sks[0:16, j, kv:kv + 1], sinkB=sksB, mask=False,
                                              v=[(Vs[q][:, kv * 64:(kv + 1) * 64], VsB[q]), (Vp[q][:, kv * 64:(kv + 1) * 64], VpB[q])],
                                              P=16, cp=kp, rows=(r0, r1)))
                    attn_stage_a(items)
                    if pendg[0] is not None:
                        pendg[0]()

                    def fin_s(items=items, s=s):
                        outs = {}
                        for it in items:
                            if it["cp"] not in outs:
                                outs[it["cp"]] = bank()
                            psO, psOB, _ = outs[it["cp"]]
                            it["po"] = psO[it["rows"][0]:it["rows"][1], 0:16]
                            it["poB"] = psOB
                        attn_stage_b(items)
                        for kp, (psO, psOB, _) in outs.items():
                            for hf in range(2):
                                r0, r1 = 64 * hf, 64 * hf + 64
                                src = psO[r0:r1, 0:16].rearrange("p (g t) -> p g t", t=4)
                                copy_op(evac_eng(), OT[r0:r1, 4 * kp:4 * kp + 4, s:64:16], src, [psOB], [OTB])
                    pendg[0] = fin_s
                if pendg[0] is not None:
                    pendg[0]()
                    pendg[0] = None
            for m in range(NCH):
                pt, pbk, _ = bank()
                for cp in range(NCH):
                    S.op("pe", lambda e, pt=pt, cp=cp, m=m: e.matmul(pt[:, 0:N], lhsT=wo[:, cp, m * 128:(m + 1) * 128], rhs=OT[:, cp, 0:N], start=(cp == 0), stop=(cp == 7)),
                         reads=[woB, OTB], writes=[pbk], signal=(cp == 7))
                S.op("dve", lambda e, pt=pt, m=m: e.tensor_tensor(out=hT[:, m, c0:c0 + N], in0=hT[:, m, c0:c0 + N], in1=pt[:, 0:N], op=ALU.add),
                     reads=[pbk, hB[(m, b)]], writes=[hB[(m, b)]])

        def load_cprev(i):
            for pc in range(4):
                S.dma("sp", [(scst[0:32, :], sconv[i, :, :, pc * 1408:(pc + 1) * 1408].rearrange("s r n -> (s r) n"))], writes=[scstB], sem="scst")
                pt, pbk, _ = bank()
                for x in range(11):
                    S.op("pe", lambda e, pt=pt, x=x: e.transpose(out=pt[:, x * 32:(x + 1) * 32], in_=scst[0:32, x * 128:(x + 1) * 128], identity=ident_f[0:32, 0:32]),
                         reads=[scstB, idfB], writes=[pbk], signal=(x == 10))
                copy_op(evac_eng(), cprev[:, pc * 11:(pc + 1) * 11, :], pt[:, 0:352].rearrange("p (a b) -> p a b", a=11), [pbk],
                        [cprevB[ch] for ch in range(pc * 11, (pc + 1) * 11)])

        ws_ctr = [0]
        PROD_ENG = os.environ.get("KPROD", "pool")

        def ffn(i, sbi, blocks):
            for b, blk in enumerate(blocks):
                c0, N, kind = blk
                rms_block(blk, b, C_NF + i * 8, lambda c, c0=c0, N=N: xnf[:, c, c0:c0 + N], xnfB[b], k=b % 2)
            pend = [None]
            for pi, (j0, j1) in enumerate(PARTS):
                for j in range(j0, j1):
                    if j + 2 < NJ:
                        load_wup(i, j + 2)
                    wu, wuB = wup[j % 3], wupB[j % 3]
                    jj = j - j0
                    for b, (c0, N, kind) in enumerate(blocks):
                        st = ws_ctr[0] % 3
                        ws_ctr[0] += 1
                        pss = [bank(), bank()]
                        for hf in range(2):
                            pt, pbk, _ = pss[hf]
                            for k in range(NCH):
                                S.op("pe", lambda e, pt=pt, hf=hf, k=k, wu=wu, c0=c0, N=N: e.matmul(pt[:, 0:N], lhsT=wu[:, hf, k, :], rhs=xnf[:, k, c0:c0 + N], start=(k == 0), stop=(k == 7)),
                                     reads=[wuB, xnfB[b]], writes=[pbk], signal=(k == 7))
                        W = []
                        for hf in range(2):
                            ch = hf * NJ + j
                            W.append((col(C_CW + (i * 3 + 0) * 44 + ch), col(C_CW + (i * 3 + 1) * 44 + ch), col(C_CW + (i * 3 + 2) * 44 + ch), col(C_CB + i * 44 + ch), ch))
                        if kind == "p":
                            gb = sbi * 3 + b
                            rp, wp = gb % 2, (gb + 1) % 2
                            if sbi == 0 and b == 0:
                                for hf in range(2):
                                    pt, pbk, _ = pss[hf]
                                    S.op("dve", lambda e, pt=pt: e.tensor_scalar(out=pt[:, 254:256], in0=pt[:, 254:256], scalar1=cc[:, CC_M:CC_M + 1], scalar2=None, op0=ALU.mult),
                                         reads=[pbk, ccB], writes=[pbk])
                            for hf in range(2):
                                pt, pbk, _ = pss[hf]
                                w0, w1, w2, bcol, ch = W[hf]
                                Cb, CbB = cbuf[st][hf], cbB[st][hf]
                                S.op("act", lambda e, Cb=Cb, pt=pt, w2=w2, bcol=bcol, N=N: e.activation(out=Cb[:, 0:N], in_=pt[:, 0:N], func=AF.Identity, bias=bcol, scale=w2),
                                     reads=[pbk, colsB], writes=[CbB])
                                S.op("act", lambda e, pt=pt, ch=ch, N=N, wp=wp: e.activation(out=ccarry[:, wp, i, :, ch], in_=pt[:, N - 2:N], func=AF.Copy),
                                     reads=[pbk], writes=[ccB2[(wp, i, ch)]])
                            for hf in range(2):
                                pt, pbk, _ = pss[hf]
                                w0, w1, w2, bcol, ch = W[hf]
                                Cb, CbB = cbuf[st][hf], cbB[st][hf]
                                S.op("dve", lambda e, Cb=Cb, pt=pt, w1=w1, N=N: e.scalar_tensor_tensor(out=Cb[:, 1:N], in0=pt[:, 0:N - 1], scalar=w1, in1=Cb[:, 1:N], op0=ALU.mult, op1=ALU.add),
                                     reads=[pbk, CbB, colsB], writes=[CbB])
                            for hf in range(2):
                                pt, pbk, _ = pss[hf]
                                w0, w1, w2, bcol, ch = W[hf]
                                Cb, CbB = cbuf[st][hf], cbB[st][hf]
                                S.op("dve", lambda e, Cb=Cb, pt=pt, w0=w0, N=N: e.scalar_tensor_tensor(out=Cb[:, 2:N], in0=pt[:, 0:N - 2], scalar=w0, in1=Cb[:, 2:N], op0=ALU.mult, op1=ALU.add),
                                     reads=[pbk, CbB, colsB], writes=[CbB])
                            for hf in range(2):
                                pt, pbk, _ = pss[hf]
                                w0, w1, w2, bcol, ch = W[hf]
                                Cb, CbB = cbuf[st][hf], cbB[st][hf]
                                S.op("dve", lambda e, Cb=Cb, w1=w1, ch=ch, rp=rp: e.scalar_tensor_tensor(out=Cb[:, 0:1], in0=ccarry[:, rp, i, 1:2, ch], scalar=w1, in1=Cb[:, 0:1], op0=ALU.mult, op1=ALU.add),
                                     reads=[ccB2[(rp, i, ch)], CbB, colsB], writes=[CbB])
                                S.op("dve", lambda e, Cb=Cb, w0=w0, ch=ch, rp=rp: e.scalar_tensor_tensor(out=Cb[:, 0:2], in0=ccarry[:, rp, i, :, ch], scalar=w0, in1=Cb[:, 0:2], op0=ALU.mult, op1=ALU.add),
                                     reads=[ccB2[(rp, i, ch)], CbB, colsB], writes=[CbB])
                        else:
                            for hf in range(2):
                                pt, pbk, _ = pss[hf]
                                w0, w1, w2, bcol, ch = W[hf]
                                Cb, CbB = cbuf[st][hf], cbB[st][hf]
                                U3 = ugs[hf][:, 0:96].rearrange("p (s k) -> p s k", k=6)
                                UB = ugsB[hf]
                                S.op("act", lambda e, U3=U3, pt=pt: e.activation(out=U3[:, :, 2:6], in_=pt[:, 0:64].rearrange("p (t s) -> p s t", s=16), func=AF.Copy),
                                     reads=[pbk], writes=[UB])
                                S.op("dve", lambda e, U3=U3, ch=ch: e.tensor_copy(out=U3[:, :, 0:2], in_=cprev[:, ch, :].rearrange("p (s r) -> p s r", r=2)),
                                     reads=[cprevB[ch]], writes=[UB])
                                S.op("dve", lambda e, U3=U3, ch=ch: e.tensor_copy(out=cprev[:, ch, :].rearrange("p (s r) -> p s r", r=2), in_=U3[:, :, 4:6]),
                                     reads=[UB], writes=[cprevB[ch]])
                                t0, t1, t2 = U3[:, :, 0:4], U3[:, :, 1:5], U3[:, :, 2:6]
                                cv_ = Cb[:, 0:64].rearrange("p (t s) -> p s t", s=16)
                                S.op("act", lambda e, cv_=cv_, t0=t0, w0=w0, bcol=bcol: e.activation(out=cv_, in_=t0, func=AF.Identity, bias=bcol, scale=w0),
                                     reads=[UB, colsB], writes=[CbB])
                                S.op("dve", lambda e, cv_=cv_, t1=t1, w1=w1: e.scalar_tensor_tensor(out=cv_, in0=t1, scalar=w1, in1=cv_, op0=ALU.mult, op1=ALU.add),
                                     reads=[UB, CbB, colsB], writes=[CbB])
                                S.op("dve", lambda e, cv_=cv_, t2=t2, w2=w2: e.scalar_tensor_tensor(out=cv_, in0=t2, scalar=w2, in1=cv_, op0=ALU.mult, op1=ALU.add),
                                     reads=[UB, CbB, colsB], writes=[CbB])
                        if pend[0] is not None:
                            pend[0]()
                        Cg, Cv = cbuf[st][0], cbuf[st][1]

                        def fin(Cg=Cg, Cv=Cv, st=st, jj=jj, b=b, c0=c0, N=N):
                            S.op("act", lambda e: e.activation(out=Cg[:, 0:N], in_=Cg[:, 0:N], func=AF.Gelu_apprx_tanh), reads=[cbB[st][0]], writes=[cbB[st][0]])
                            S.op(PROD_ENG, lambda e: e.tensor_tensor(out=actT[:, jj, c0:c0 + N], in0=Cg[:, 0:N], in1=Cv[:, 0:N], op=ALU.mult),
                                 reads=[cbB[st][0], cbB[st][1]], writes=[actB[(jj, b)]])
                        pend[0] = fin
                if pend[0] is not None:
                    pend[0]()
                    pend[0] = None
                if pi + 1 < len(PARTS):
                    load_wdn(i, pi + 1)
                wd, wdB = wdn[pi % 2], wdnB[pi % 2]
                nj = j1 - j0
                for b, (c0, N, kind) in enumerate(blocks):
                    for m in range(NCH):
                        pt, pbk, _ = bank()
                        for jj in range(nj):
                            S.op("pe", lambda e, pt=pt, jj=jj, m=m, wd=wd, c0=c0, N=N: e.matmul(pt[:, 0:N], lhsT=wd[:, jj, m * 128:(m + 1) * 128], rhs=actT[:, jj, c0:c0 + N], start=(jj == 0), stop=(jj == nj - 1)),
                                 reads=[wdB, actB[(jj, b)]], writes=[pbk], signal=(jj == nj - 1))
                        S.op("dve", lambda e, pt=pt, m=m, c0=c0, N=N: e.tensor_tensor(out=hT[:, m, c0:c0 + N], in0=hT[:, m, c0:c0 + N], in1=pt[:, 0:N], op=ALU.add),
                             reads=[pbk, hB[(m, b)]], writes=[hB[(m, b)]])

        def conv_state_out(i, sbi):
            if sbi != 1:
                return
            pt, pbk, _ = bank()
            S.op("pe", lambda e, pt=pt: e.transpose(out=pt[0:88, 0:128], in_=ccarry[:, 1, i, :, :], identity=ident_f[:]),
                 reads=[ccB2[(1, i, ch)] for ch in range(44)] + [idfB], writes=[pbk])
            copy_op(evac_eng(), ost[0:88, 0:128], pt[0:88, 0:128], [pbk], [ostB])
            S.dma("sp", [(convp_o[i, r, :].rearrange("(j p) -> j p", p=128), ost[44 * r:44 * r + 44, 0:128]) for r in range(2)], reads=[ostB], sem="ost", final=True)
            for x in range(11):
                pt, pbk, _ = bank()
                S.op("pe", lambda e, pt=pt, x=x: e.transpose(out=pt[:, 0:128], in_=cprev[:, 4 * x:4 * x + 4, :], identity=ident_f[:]),
                     reads=[cprevB[ch] for ch in range(4 * x, 4 * x + 4)] + [idfB], writes=[pbk])
                copy_op(evac_eng(), ost[:, 0:128], pt[:, 0:128], [pbk], [ostB])
                S.dma("sp", [(convs_o[i, :, :, (4 * x + j4) * 128:(4 * x + j4 + 1) * 128].rearrange("s r p -> (s r) p"), ost[32 * j4:32 * j4 + 32, 0:128]) for j4 in range(4)],
                      reads=[ostB], sem="ost", final=True)

        def ple_block(i, sbi, blk, b):
            c0, N, kind = blk
            row0 = 0 if sbi == 0 else TA
            rms_block(blk, b, C_NP + i * 8, lambda c: xm[:, c, 0:N], xmB, k=0)
            if kind == "p":
                tiles = [(row0 + c0 + 128 * t, 128 * t, 128) for t in range(N // 128)]
            else:
                tiles = [(NPROMPT, 0, 64)]
            for ti, (r0, lc, rows) in enumerate(tiles):
                q = ti % 2
                S.dma("sp", [(pst[q][0:rows, :], pin[i, r0:r0 + rows, :])], writes=[pstB[q]], sem=f"pst{q}")
                pt, pbk, _ = bank()
                for kc in range(2):
                    S.op("pe", lambda e, pt=pt, kc=kc, q=q, rows=rows: e.transpose(out=pt[:, kc * 128:kc * 128 + rows], in_=pst[q][0:rows, kc * 128:(kc + 1) * 128], identity=ident_f[0:rows, 0:rows]),
                         reads=[pstB[q], idfB], writes=[pbk], signal=(kc == 1))
                copy_op(evac_eng(), pT[:, :, lc:lc + rows], pt[:, 0:256].rearrange("p (a b) -> p a b", a=2)[:, :, 0:rows], [pbk], [pTB])
            for m in range(NCH):
                q = m % 2
                pg, pgB, _ = bank()
                for k in range(NCH):
                    S.op("pe", lambda e, pg=pg, k=k, m=m: e.matmul(pg[:, 0:N], lhsT=wg[:, k, m * 128:(m + 1) * 128], rhs=xm[:, k, 0:N], start=(k == 0), stop=(k == 7)),
                         reads=[wgB, xmB], writes=[pgB], signal=(k == 7))
                pp, ppB, _ = bank()
                for k in range(2):
                    S.op("pe", lambda e, pp=pp, k=k, m=m: e.matmul(pp[:, 0:N], lhsT=wpr[:, k, m * 128:(m + 1) * 128], rhs=pT[:, k, 0:N], start=(k == 0), stop=(k == 1)),
                         reads=[wprB, pTB], writes=[ppB], signal=(k == 1))
                S.op("act", lambda e, pg=pg, q=q: e.activation(out=sg[q][:, 0:N], in_=pg[:, 0:N], func=AF.Sigmoid), reads=[pgB], writes=[sgB[q]])
                S.op("dve", lambda e, pp=pp, q=q: e.tensor_tensor(out=tmpb[q][:, 0:N], in0=sg[q][:, 0:N], in1=pp[:, 0:N], op=ALU.mult),
                     reads=[sgB[q], ppB], writes=[tmpB[q]])
                S.op("dve", lambda e, q=q, m=m: e.tensor_tensor(out=hT[:, m, c0:c0 + N], in0=hT[:, m, c0:c0 + N], in1=tmpb[q][:, 0:N], op=ALU.add),
                     reads=[tmpB[q], hB[(m, b)]], writes=[hB[(m, b)]])

        SBS = [
            [(0, 512, "p"), (512, 512, "p"), (1024, 256, "p")],
            [(0, 512, "p"), (512, 512, "p"), (1024, 64, "s")],
        ]
        import os
        STAGE = int(os.environ.get("KSTAGE", "9"))
        if STAGE >= 3:
            load_mixer(0)
        for sbi in range(2 if STAGE >= 5 else (1 if STAGE >= 2 else 0)):
            blocks = SBS[sbi]
            load_x(sbi, blocks)
            KLB = int(os.environ.get("KLB", "4"))
            KREP = int(os.environ.get("KREP", "0"))
            for i in (list(range(4 if STAGE >= 4 else (1 if STAGE >= 3 else 0))) if sbi == 0 else list(range(KLB)) + [1] * KREP):
                load_wup(i, 0)
                load_wup(i, 1)
                load_wdn(i, 0)
                if i == 2:
                    load_w(wk, wkd.rearrange("k p n -> p k n"), wkB, "wk")
                    load_w(wv, wvd.rearrange("k p n -> p k n"), wvB, "wv")
                    for b, blk in enumerate(blocks):
                        last_prompt = (b == 1)
                        if blk[2] == "s" and "skv" in SKIP:
                            continue
                        kv_block(sbi, blk, b, last_prompt)
                if sbi == 1:
                    load_cprev(i)
                for b, blk in enumerate(blocks):
                    if i < 2:
                        pool_block(i, sbi, blk, b)
                    else:
                        attn_block(i, sbi, blk, b)
                load_ple(i)
                ffn(i, sbi, blocks)
                conv_state_out(i, sbi)
                nxt = (sbi, i + 1) if i < 3 else ((sbi + 1, 0) if sbi == 0 else None)
                if STAGE == 3 or (STAGE == 4 and i == 3):
                    nxt = None
                if sbi == 0 and i == 3 and KLB == 0:
                    nxt = None
                if sbi == 1 and i == KLB - 1:
                    nxt = None
                if nxt is not None:
                    load_mixer(nxt[1])
                for b, blk in enumerate(blocks):
                    ple_block(i, sbi, blk, b)
            store_y(sbi, blocks)
            if sbi == 0:
                S.op("act", lambda e: e.activation(out=KT[:, :, 0:128], in_=KT[:, :, TA:TA + 128], func=AF.Copy), reads=[KTB[2]], writes=[KTcB])
                S.op("act", lambda e: e.activation(out=Vt[:, 0, :], in_=Vt[:, 10, :], func=AF.Copy), reads=[VtB[2]], writes=[VtcB])
            elif KLB >= 2:
                for i in range(2):
                    for hb in range(2):
                        pt, pbk, _ = bank()
                        for cq in range(4):
                            c = hb * 4 + cq
                            S.op("pe", lambda e, pt=pt, c=c, cq=cq, i=i: e.transpose(out=pt[0:15, cq * 128:(cq + 1) * 128], in_=pcarry[:, i, c, :], identity=ident_f[:]),
                                 reads=[pcB[(i, c)], idfB], writes=[pbk], signal=(cq == 3))
                        copy_op(evac_eng(), ost[0:15, hb * 512:(hb + 1) * 512], pt[0:15, :], [pbk], [ostB])
                    S.dma("sp", [(poolp_o[i], ost[0:15, :])], reads=[ostB], sem="ost", final=True)
        S.finish()
        S.run()
    return nc


_PROG = {}


def _host_inputs(inp):
    f32 = np.float32
    x_prompt, x_sample = inp["x_prompt"], inp["x_sample"]
    p_prompt, p_sample = inp["p_prompt"], inp["p_sample"]
    perm = np.zeros(1024, np.int64)
    for cp in range(8):
        for hf in range(2):
            head = 4 * (2 * (cp // 4) + hf) + cp % 4
            perm[cp * 128 + hf * 64: cp * 128 + hf * 64 + 64] = head * 64 + np.arange(64)
    wq = np.ascontiguousarray(inp["w_q"][:, :, perm]).reshape(2, 8, 128, 1024)
    wo = np.ascontiguousarray(inp["w_o"][:, perm, :]).reshape(2, 8, 128, 1024)
    wup = np.ascontiguousarray(inp["w_up"].reshape(4, 8, 128, 2, NJ, 128).transpose(0, 4, 2, 3, 1, 5)).reshape(4, NJ, 128, 2048)
    wdn = np.ascontiguousarray(inp["w_down"]).reshape(4, NJ, 128, 1024)
    wg = np.ascontiguousarray(inp["w_ple_gate"]).reshape(4, 8, 128, 1024)
    wpr = np.ascontiguousarray(inp["w_ple_proj"]).reshape(4, 2, 128, 1024)
    wpl = np.ascontiguousarray(inp["w_pool"]).reshape(2, 4, 2, 128, 256)
    wk = np.ascontiguousarray(inp["w_k"]).reshape(8, 128, 256)
    wv = np.ascontiguousarray(inp["w_v"]).reshape(8, 128, 256)

    cols = np.zeros((128, NCOLS), f32)

    def colmajor(v):
        return np.ascontiguousarray(v.reshape(-1, 128).T)
    for i in range(4):
        cols[:, C_NM + 8 * i:C_NM + 8 * i + 8] = colmajor(inp["norm_mix"][i])
        cols[:, C_NF + 8 * i:C_NF + 8 * i + 8] = colmajor(inp["norm_ffn"][i])
        cols[:, C_NP + 8 * i:C_NP + 8 * i + 8] = colmajor(inp["norm_ple"][i])
        for tap in range(3):
            cols[:, C_CW + (i * 3 + tap) * 44:C_CW + (i * 3 + tap + 1) * 44] = colmajor(inp["conv_w"][i, tap])
        cols[:, C_CB + i * 44:C_CB + (i + 1) * 44] = colmajor(inp["conv_b"][i])
    cols[:, C_KVN:C_KVN + 8] = colmajor(inp["kv_norm"])
    for i in range(2):
        cols[:, C_PSC + 8 * i:C_PSC + 8 * i + 8] = colmajor(inp["pool_scale"][i])
        cols[:, C_QN + i] = np.tile(inp["q_norm"][i], 2)
    cols[:, C_KN] = np.tile(inp["k_norm"], 2)
    cols[:, C_EPS] = EPS

    oh = np.zeros((33, 384), f32)
    ii = np.arange(384)
    dist = ii - 127
    valid = (dist >= 0) & (dist < 128)
    bk = t5_bucket_np(np.clip(dist, 0, 127))
    oh[bk[valid], ii[valid]] = 8.0
    oh[32, ~valid] = NEG

    shared = dict(cols=cols, relb=np.ascontiguousarray(inp["rel_bias"], f32), oh=oh, sinks=np.ascontiguousarray(inp["sinks"], f32),
                  wup=wup, wdn=wdn, wq=wq, wo=wo, wg=wg, wpr=wpr, wpl=wpl, wk=wk, wv=wv)
    maps = []
    for core in range(NCORES):
        bi, ch = core // 4, core % 4
        s = ch * CHUNK
        xin = np.zeros((NROWS, D), f32)
        pin = np.zeros((4, NROWS, 256), f32)
        lo = s - HALO
        if lo >= 0:
            xin[0:NPROMPT] = x_prompt[bi, lo:s + CHUNK]
            pin[:, 0:NPROMPT] = p_prompt[:, bi, lo:s + CHUNK]
        else:
            xin[HALO:NPROMPT] = x_prompt[bi, 0:CHUNK]
            pin[:, HALO:NPROMPT] = p_prompt[:, bi, 0:CHUNK]
        sq = slice(16 * core, 16 * core + 16)
        xin[NPROMPT:] = x_sample[sq].transpose(1, 0, 2).reshape(64, D)
        pin[:, NPROMPT:] = p_sample[:, sq].transpose(0, 2, 1, 3).reshape(4, 64, 256)
        ccv = np.zeros((128, NCC), f32)
        first = (ch == 0)
        ccv[:, CC_M] = 0.0 if first else 1.0
        for g, w in enumerate(POOL_W):
            pos = np.arange(15)
            cnt = np.minimum(pos + 1, w) if first else np.full(15, w)
            ccv[:, CC_INV + 15 * g:CC_INV + 15 * g + 15] = (1.0 / cnt.astype(np.float64)).astype(f32)[None, :]
        ccv[:, CC_FM:CC_FM + 128] = NEG if first else 0.0
        m = dict(shared)
        m.update(xin=xin, pin=pin, cc=ccv,
                 spool=np.ascontiguousarray(inp["state_pool"][:, sq]),
                 sconv=np.ascontiguousarray(inp["state_conv"][:, sq]),
                 ck=np.ascontiguousarray(inp["cache_k"][sq]).reshape(16, 128, 256),
                 cv=np.ascontiguousarray(inp["cache_v"][sq]).reshape(16, 128, 256))
        maps.append(m)
    return maps


def kernel(**inputs):
    inp = {k: np.asarray(v) for k, v in inputs.items()}
    if "nc" not in _PROG:
        _PROG["nc"] = build_program()
    nc = _PROG["nc"]
    maps = _host_inputs(inp)
    res = run_bass_kernel_spmd(nc, maps, core_ids=list(range(NCORES)))
    R = res.results
    f32 = np.float32
    y_prompt = np.zeros((2, 8192, D), f32)
    y_sample = np.zeros((128, 4, D), f32)
    pool_p = np.zeros((2, 2, 15, D), f32)
    pool_s = np.zeros((2, 128, 15, D), f32)
    conv_p = np.zeros((4, 2, 2, 2 * FF), f32)
    conv_s = np.zeros((4, 128, 2, 2 * FF), f32)
    k_p = np.zeros((2, 128, 4, 64), f32)
    v_p = np.zeros((2, 128, 4, 64), f32)
    k_s = np.zeros((128, 128, 4, 64), f32)
    v_s = np.zeros((128, 128, 4, 64), f32)
    for core in range(NCORES):
        bi, ch = core // 4, core % 4
        r = R[core]
        y = np.asarray(r["y"])
        y_prompt[bi, ch * CHUNK:(ch + 1) * CHUNK] = y[HALO:NPROMPT]
        sq = slice(16 * core, 16 * core + 16)
        y_sample[sq] = y[NPROMPT:].reshape(4, 16, D).transpose(1, 0, 2)
        pool_s[:, sq] = np.asarray(r["pools"])
        conv_s[:, sq] = np.asarray(r["convs"])
        k_s[sq] = np.asarray(r["cks"]).reshape(16, 128, 4, 64)
        v_s[sq] = np.asarray(r["cvs"]).reshape(16, 128, 4, 64)
        if ch == 3:
            pool_p[:, bi] = np.asarray(r["poolp"])
            conv_p[:, bi] = np.asarray(r["convp"])
            k_p[bi] = np.asarray(r["ckp"]).reshape(128, 4, 64)
            v_p[bi] = np.asarray(r["cvp"]).reshape(128, 4, 64)
    return (y_prompt, y_sample, pool_p, pool_s, conv_p, conv_s, k_p, k_s, v_p, v_s)
```

```python
import math
import types
import numpy as np
import concourse.bass as bass
import concourse.mybir as mybir
from concourse.bass_utils import run_bass_kernel_spmd
from contextlib import ExitStack

F32 = mybir.dt.float32
BF16 = mybir.dt.bfloat16
AF = mybir.ActivationFunctionType
ALU = mybir.AluOpType
AX = mybir.AxisListType

NCORES = 8
D = 1024
NCH = 8
FF = 2816
NJ = 22
NEG = -30000.0
EPS = 1e-6
HALO = 256
CHUNK = 2048
NPROMPT = HALO + CHUNK
NSAMP = 64
NROWS = NPROMPT + NSAMP
TA = 1280
POOL_W = (2, 4, 8, 16)

C_NM, C_NF, C_NP, C_KVN, C_PSC, C_CW, C_CB, C_QN, C_KN, C_EPS = 0, 32, 64, 96, 104, 120, 648, 824, 826, 827
NCOLS = 828
CC_M, CC_INV, CC_FM, NCC = 0, 1, 61, 189

PARTS = [(0, 4), (4, 8), (8, 12), (12, 16), (16, 19), (19, 22)]


def _freeze(fn):
    if fn.__closure__ is None:
        return fn
    cells = []
    for c in fn.__closure__:
        try:
            cells.append(types.CellType(c.cell_contents))
        except ValueError:
            cells.append(c)
    return types.FunctionType(fn.__code__, fn.__globals__, fn.__name__, fn.__defaults__, tuple(cells))


class Buf:
    __slots__ = ("name", "w", "r", "alias", "excl")

    def __init__(self, name, excl=False):
        self.name = name
        self.w = None
        self.r = {}
        self.alias = []
        self.excl = excl


class Sched:
    ENG = ("pe", "act", "dve", "pool", "sp")
    SEM_LIMIT = 3500

    def __init__(self, nc, es):
        self.nc = nc
        self.es = es
        self.rec = {e: [] for e in self.ENG}
        self.sem = {e: es.enter_context(nc.semaphore("s_" + e)) for e in ("pe", "act", "dve", "pool")}
        self.cnt = {e: 0 for e in self.sem}
        self.epoch = {e: 0 for e in self.sem}
        self.key = {e: e + "#0" for e in self.sem}
        self.pending = {e: False for e in self.sem}
        self.waited = {e: {} for e in self.ENG}
        self.dsem = {}
        self.final = {}

    def _wait(self, eng, ev):
        if ev is None:
            return
        sem, val, key = ev
        if key == self.key.get(eng) and val > self.cnt[eng]:
            return
        w = self.waited[eng]
        if w.get(key, 0) >= val:
            return
        w[key] = val
        self.rec[eng].append(lambda e, sem=sem, val=val: e.wait_ge(sem, val))

    def _deps(self, eng, reads, writes):
        for b in reads:
            self._wait(eng, b.w)
            if b.excl:
                mine = self.key.get(eng, eng).split("#")[0]
                for k, ev in list(b.r.items()):
                    if k.split("#")[0] != mine:
                        self._wait(eng, ev)
        for b in writes:
            self._wait(eng, b.w)
            for ev in list(b.r.values()):
                self._wait(eng, ev)
            for a in b.alias:
                self._wait(eng, a.w)
                for ev in list(a.r.values()):
                    self._wait(eng, ev)

    def _mark(self, ev, reads, writes):
        for b in reads:
            old = b.r.get(ev[2])
            if old is None or old[1] < ev[1]:
                b.r[ev[2]] = ev
        for b in writes:
            b.w = ev
            b.r = {}
            for a in b.alias:
                a.w = None
                a.r = {}

    def op(self, eng, fn, reads=(), writes=(), signal=True):
        fn = _freeze(fn)
        if self.cnt[eng] >= self.SEM_LIMIT and not self.pending[eng]:
            self.epoch[eng] += 1
            self.sem[eng] = self.es.enter_context(self.nc.semaphore(f"s_{eng}_{self.epoch[eng]}"))
            self.cnt[eng] = 0
            self.key[eng] = f"{eng}#{self.epoch[eng]}"
        self._deps(eng, reads, writes)
        s = self.sem[eng]
        if signal:
            self.cnt[eng] += 1
            ev = (s, self.cnt[eng], self.key[eng])
            self.rec[eng].append(lambda e, fn=fn, s=s: fn(e).then_inc(s, 1))
            self.pending[eng] = False
        else:
            ev = (s, self.cnt[eng] + 1, self.key[eng])
            self.rec[eng].append(lambda e, fn=fn: fn(e))
            self.pending[eng] = True
        self._mark(ev, reads, writes)
        return ev

    def dma(self, q, pairs, reads=(), writes=(), sem=None, final=False, **kw):
        self._deps(q, reads, writes)
        if sem not in self.dsem:
            self.dsem[sem] = [self.es.enter_context(self.nc.semaphore("d_" + sem)), 0]
        ent = self.dsem[sem]
        for (o, i) in pairs:
            ent[1] += 16
            s = ent[0]
            self.rec[q].append(lambda e, o=o, i=i, s=s, kw=kw: e.dma_start(out=o, in_=i, **kw).then_inc(s, 16))
        ev = (ent[0], ent[1], "dma_" + sem)
        self._mark(ev, reads, writes)
        if final:
            self.final[sem] = ev
        return ev

    def finish(self):
        for name, ev in self.final.items():
            self._wait("sp", ev)
        for e in ("pe", "act", "dve"):
            if self.cnt[e]:
                self._wait("sp", (self.sem[e], self.cnt[e], self.key[e]))

    def run(self):
        nc = self.nc
        engs = {"pe": "tensor", "act": "scalar", "dve": "vector", "pool": "gpsimd", "sp": "sync"}
        with nc.Block() as block:
            for k, attr in engs.items():
                lst = self.rec[k]
                if not lst:
                    continue

                def body(e, lst=lst):
                    for f in lst:
                        f(e)
                getattr(block, attr)(body)


class Arena:
    def __init__(self, nc, es, name, nbytes):
        self.name = name
        self.nbytes = nbytes
        self.t = es.enter_context(nc.sbuf_tensor(name, [128, nbytes // 4], F32))
        self.regs = []

    def ap(self, off, shape, dtype):
        isz = 4 if dtype == F32 else 2
        n = 1
        for s in shape:
            n *= s
        nb = n * isz
        assert off % 4 == 0 and nb % 4 == 0 and off + nb <= self.nbytes, (self.name, off, nb)
        a = self.t[:, off // 4:(off + nb) // 4]
        if dtype != F32:
            a = a.bitcast(dtype)
        if len(shape) == 2:
            a = a.rearrange("p (a b) -> p a b", a=shape[0])
        elif len(shape) == 3:
            a = a.rearrange("p (a b c) -> p a b c", a=shape[0], b=shape[1])
        return a

    def bufs(self, name, off, nbytes, n=1):
        assert off + nbytes <= self.nbytes, (self.name, name, off, nbytes)
        new = [Buf(f"{name}{k}") for k in range(n)]
        for (lo, hi, bs) in self.regs:
            if lo < off + nbytes and off < hi:
                for b in new:
                    for o in bs:
                        b.alias.append(o)
                        o.alias.append(b)
        self.regs.append((off, off + nbytes, new))
        return new


def t5_bucket_np(d):
    n = np.maximum(d, 0)
    nf = np.maximum(n, 1).astype(np.float32)
    large = 16 + (np.log(nf / np.float32(16)) / np.float32(math.log(128 / 16)) * np.float32(16)).astype(np.int32)
    large = np.minimum(large, 31)
    return np.where(n < 16, n, large)


def build_program():
    nc = bass.Bass("TRN2", target_bir_lowering=False)

    def din(name, shape):
        return nc.dram_tensor(name, shape, F32, kind="ExternalInput").ap()

    def dout(name, shape):
        return nc.dram_tensor(name, shape, F32, kind="ExternalOutput").ap()

    xin = din("xin", [NROWS, D])
    pin = din("pin", [4, NROWS, 256])
    spool = din("spool", [2, 16, 15, D])
    sconv = din("sconv", [4, 16, 2, 2 * FF])
    ckin = din("ck", [16, 128, 256])
    cvin = din("cv", [16, 128, 256])
    colsd = din("cols", [128, NCOLS])
    ccd = din("cc", [128, NCC])
    relbd = din("relb", [32, 16])
    ohd = din("oh", [33, 384])
    sinksd = din("sinks", [2, 16])
    wupd = din("wup", [4, NJ, 128, 2048])
    wdnd = din("wdn", [4, NJ, 128, 1024])
    wqd = din("wq", [2, 8, 128, 1024])
    wod = din("wo", [2, 8, 128, 1024])
    wgd = din("wg", [4, 8, 128, 1024])
    wprd = din("wpr", [4, 2, 128, 1024])
    wpld = din("wpl", [2, 4, 2, 128, 256])
    wkd = din("wk", [8, 128, 256])
    wvd = din("wv", [8, 128, 256])

    yout = dout("y", [NROWS, D])
    poolp_o = dout("poolp", [2, 15, D])
    pools_o = dout("pools", [2, 16, 15, D])
    convp_o = dout("convp", [4, 2, 2 * FF])
    convs_o = dout("convs", [4, 16, 2, 2 * FF])
    ckp_o = dout("ckp", [128, 256])
    cvp_o = dout("cvp", [128, 256])
    cks_o = dout("cks", [16, 128, 256])
    cvs_o = dout("cvs", [16, 128, 256])
    gdt = nc.dram_tensor("gd", [16, 384], F32, kind="Internal")
    gd = gdt.ap()

    with ExitStack() as es:
        S = Sched(nc, es)

        def sb(name, shape, dt):
            return es.enter_context(nc.sbuf_tensor(name, shape, dt))

        hT = sb("hT", [128, NCH, TA], F32)
        KT = sb("KT", [128, 2, 128 + TA], BF16)
        Vt = sb("Vt", [128, 11, 256], BF16)
        Tb = sb("Tb", [128, 16, 256], F32)
        Ts = sb("Ts", [128, 4, 256], F32)
        skb = sb("skb", [128, 32], F32)
        sks = sb("sks", [128, 2, 4], F32)
        cols = sb("colst", [128, NCOLS], F32)
        cc = sb("cct", [128, NCC], F32)
        ident_f = sb("ident_f", [128, 128], F32)
        ident_b = sb("ident_b", [128, 128], BF16)
        Jm = sb("Jm", [128, 128], F32)
        ones_b = sb("ones_b", [128, 128], BF16)
        blk_b = sb("blk_b", [128, 128], BF16)
        pcarry = sb("pcarry", [128, 2, NCH, 15], F32)
        ccarry = sb("ccarry", [128, 2, 4, 2, 44], F32)
        wup = [sb(f"wup{k}", [128, 2, 8, 128], BF16) for k in range(3)]
        wdn = [sb(f"wdn{k}", [128, 4, 1024], BF16) for k in range(2)]
        xm = sb("xm", [128, NCH, 512], BF16)
        pT = sb("pT", [128, 2, 512], BF16)
        pst = [sb(f"pst{k}", [128, 256], F32) for k in range(2)]
        sqb = [sb(f"sqb{k}", [128, 512], BF16) for k in range(2)]
        cprev = sb("cprev", [128, 44, 32], F32)
        KTs = sb("KTs", [128, 2, 64], BF16)
        Vsm = sb("Vsm", [64, 256], BF16)
        Ks = [sb(f"Ks{k}", [128, 2, 256], BF16) for k in range(2)]
        Vs = [sb(f"Vs{k}", [128, 256], BF16) for k in range(2)]
        Vp = [sb(f"Vp{k}", [128, 256], BF16) for k in range(2)]
        small = sb("small", [128, 64], F32)
        relb_aug = sb("relb_aug", [33, 16], F32)
        xns = sb("xns", [128, 64], F32)
        xnsB = Buf("xns")
        ugs = [sb(f"ugs{hf}", [128, 96], F32) for hf in range(2)]
        ugsB = [Buf(f"ugs{hf}") for hf in range(2)]
        qs2 = sb("qs2", [128, 2, 16, 16], BF16)
        qs2B = Buf("qs2")

        hB = {(c, b): Buf(f"h{c}_{b}") for c in range(NCH) for b in range(3)}
        KTB = [Buf(f"KT{b}") for b in range(3)]
        KTcB = Buf("KTc")
        VtB = [Buf(f"Vt{b}") for b in range(3)]
        VtcB = Buf("Vtc")
        TbB, TsB, skbB, sksB, colsB, ccB = Buf("Tb"), Buf("Ts"), Buf("skb"), Buf("sks"), Buf("cols"), Buf("cc")
        idfB, idbB, JB, onesB, blkB = Buf("idf"), Buf("idb"), Buf("J"), Buf("ones"), Buf("blk")
        pcB = {(i, c): Buf(f"pc{i}_{c}") for i in range(2) for c in range(NCH)}
        ccB2 = {(p_, i, ch): Buf(f"ccar{p_}_{i}_{ch}") for p_ in range(2) for i in range(4) for ch in range(44)}
        wupB = [Buf(f"wup{k}") for k in range(3)]
        wdnB = [Buf(f"wdn{k}") for k in range(2)]
        xmB = Buf("xm")
        pTB = Buf("pT")
        pstB = [Buf(f"pst{k}") for k in range(2)]
        sqB = [Buf(f"sq{k}") for k in range(2)]
        cprevB = [Buf(f"cprev{ch}") for ch in range(44)]
        KTsB, VsmB = Buf("KTs"), Buf("Vsm")
        KsB = [Buf(f"Ks{k}") for k in range(2)]
        VsB = [Buf(f"Vs{k}") for k in range(2)]
        VpB = [Buf(f"Vp{k}") for k in range(2)]
        smallB = [Buf(f"small{k}") for k in range(8)]
        relbB = Buf("relb")
        gdB = Buf("gd")

        AB = Arena(nc, es, "arenaB", 32768)
        AC = Arena(nc, es, "arenaC", 20480)
        AF_ = Arena(nc, es, "arenaF", 18432)

        xnf = AB.ap(0, (NCH, TA), BF16)
        xnfB = AB.bufs("xnf", 0, NCH * TA * 2, 3)
        actT = AB.ap(20480, (4, TA), BF16)
        _actl = AB.bufs("act", 20480, 4 * TA * 2, 12)
        actB = {(jj, b): _actl[jj * 3 + b] for jj in range(4) for b in range(3)}
        wq = AB.ap(0, (8, 1024), BF16)
        wqB = AB.bufs("wq", 0, 16384)[0]
        wo = AB.ap(16384, (8, 1024), BF16)
        woB = AB.bufs("wo", 16384, 16384)[0]
        wpl = AB.ap(0, (4, 2, 256), BF16)
        wplB = AB.bufs("wpl", 0, 4096)[0]
        Hk = AB.ap(0, (16, 256), F32)
        HkB = AB.bufs("Hk", 0, 16384)[0]

        qT = AC.ap(0, (NCH, 512), BF16)
        qTB = AC.bufs("qT", 0, 8192)[0]
        OT = AC.ap(8192, (NCH, 512), BF16)
        OTB = AC.bufs("OT", 8192, 8192)[0]
        ckst = [AC.ap(16384 + 1024 * k, (256,), F32) for k in range(2)]
        ckstB = [AC.bufs(f"ckst{k}", 16384 + 1024 * k, 1024)[0] for k in range(2)]
        cvst = [AC.ap(18432 + 1024 * k, (256,), F32) for k in range(2)]
        cvstB = [AC.bufs(f"cvst{k}", 18432 + 1024 * k, 1024)[0] for k in range(2)]
        wg = AC.ap(0, (8, 1024), BF16)
        wgB = AC.bufs("wg", 0, 16384)[0]
        wpr = AC.ap(16384, (2, 1024), BF16)
        wprB = AC.bufs("wpr", 16384, 4096)[0]
        wk = AC.ap(0, (8, 256), BF16)
        wkB = AC.bufs("wk", 0, 4096)[0]
        wv = AC.ap(4096, (8, 256), BF16)
        wvB = AC.bufs("wv", 4096, 4096)[0]
        spst = [AC.ap(4096 * k, (1024,), F32) for k in range(2)]
        spstB = [AC.bufs(f"spst{k}", 4096 * k, 4096)[0] for k in range(2)]

        cbuf = [[AF_.ap((s * 2 + hf) * 2056, (514,), F32) for hf in range(2)] for s in range(3)]
        cbB = [[AF_.bufs(f"cb{s}{hf}", (s * 2 + hf) * 2056, 2056)[0] for hf in range(2)] for s in range(3)]
        xe = [AF_.ap(2108 * k, (527,), F32) for k in range(2)]
        xeB = [AF_.bufs(f"xe{k}", 2108 * k, 2108)[0] for k in range(2)]
        tab = [AF_.ap(2108 * (2 + k), (527,), F32) for k in range(2)]
        tabB = [AF_.bufs(f"tab{k}", 2108 * (2 + k), 2108)[0] for k in range(2)]
        nr = [AF_.ap(8432 + 2048 * k, (512,), F32) for k in range(2)]
        nrB = [AF_.bufs(f"nr{k}", 8432 + 2048 * k, 2048)[0] for k in range(2)]
        scb = [AF_.ap(1024 * k, (256,), F32) for k in range(8)]
        scB = [AF_.bufs(f"sc{k}", 1024 * k, 1024)[0] for k in range(8)]
        eb = [AF_.ap(8192 + 512 * k, (256,), BF16) for k in range(8)]
        ebB = [AF_.bufs(f"e{k}", 8192 + 512 * k, 512)[0] for k in range(8)]
        pTs = [AF_.ap(12528 + 512 * k, (256,), BF16) for k in range(4)]
        pTsB = [AF_.bufs(f"pTs{k}", 12528 + 512 * k, 512)[0] for k in range(4)]
        pb = [AF_.ap(14576 + 512 * k, (256,), BF16) for k in range(4)]
        pbB = [AF_.bufs(f"p{k}", 14576 + 512 * k, 512)[0] for k in range(4)]
        knf = AF_.ap(5120, (512,), F32)
        knfB = AF_.bufs("knf", 5120, 2048)[0]
        sg = [AF_.ap(2048 * k, (512,), F32) for k in range(2)]
        sgB = [AF_.bufs(f"sg{k}", 2048 * k, 2048)[0] for k in range(2)]
        tmpb = [AF_.ap(4096 + 2048 * k, (512,), F32) for k in range(2)]
        tmpB = [AF_.bufs(f"tmp{k}", 4096 + 2048 * k, 2048)[0] for k in range(2)]
        xst = [AF_.ap(4096 * k, (1024,), F32) for k in range(2)]
        xstB = [AF_.bufs(f"xst{k}", 4096 * k, 4096)[0] for k in range(2)]
        scst = AF_.ap(12528, (1408,), F32)
        scstB = AF_.bufs("scst", 12528, 5632)[0]
        ost = AF_.ap(12528, (1024,), F32)
        ostB = AF_.bufs("ost", 12528, 4096)[0]
        gsb = AF_.ap(0, (384,), F32)
        gsbB = AF_.bufs("gsb", 0, 1536)[0]
        oht = AF_.ap(2048, (384,), F32)
        ohB = AF_.bufs("oht", 2048, 1536)[0]

        pst_t = [es.enter_context(nc.psum_tensor(f"ps{k}", [128, 512], F32)) for k in range(8)]
        psB = [Buf(f"ps{k}", excl=True) for k in range(8)]
        free = list(range(8))

        def bank(hold=False):
            k = free.pop(0)
            if not hold:
                free.append(k)
            return pst_t[k], psB[k], k

        def release(k):
            free.append(k)

        flip = [0]

        def evac_eng():
            return "act"

        def copy_op(eng, out, in_, reads, writes):
            if eng == "act":
                S.op("act", lambda e: e.activation(out=out, in_=in_, func=AF.Copy), reads=reads, writes=writes)
            else:
                S.op("dve", lambda e: e.tensor_copy(out=out, in_=in_), reads=reads, writes=writes)

        def col(k):
            return cols[:, k:k + 1]

        S.dma("sp", [(cols[:], colsd)], writes=[colsB], sem="cols")
        S.dma("sp", [(cc[:], ccd)], writes=[ccB], sem="cc")
        S.op("pool", lambda e: e.memset(ident_f[:], 0.0), writes=[idfB])
        S.op("pool", lambda e: e.affine_select(out=ident_f[:], in_=ident_f[:], compare_op=ALU.not_equal, fill=1.0,
                                                base=0, pattern=[[-1, 128]], channel_multiplier=1),
             reads=[idfB], writes=[idfB])
        S.op("pool", lambda e: e.memset(Jm[:], 0.0), writes=[JB])
        S.op("pool", lambda e: e.affine_select(out=Jm[:], in_=Jm[:], compare_op=ALU.not_equal, fill=1.0,
                                                base=-127, pattern=[[1, 128]], channel_multiplier=1),
             reads=[JB], writes=[JB])
        S.op("dve", lambda e: e.tensor_copy(out=ident_b[:], in_=ident_f[:]), reads=[idfB], writes=[idbB])
        S.op("dve", lambda e: e.memset(ones_b[:], 1.0), writes=[onesB])
        S.op("dve", lambda e: e.memset(blk_b[:], 0.0), writes=[blkB])
        S.op("dve", lambda e: e.memset(blk_b[0:64, 0:64], 1.0), writes=[blkB])
        S.op("dve", lambda e: e.memset(blk_b[64:128, 64:128], 1.0), writes=[blkB])
        S.op("dve", lambda e: e.memset(KT[:, :, 0:128], 0.0), writes=[KTcB])
        S.op("dve", lambda e: e.memset(Vt[:, 0, :], 0.0), writes=[VtcB])
        S.op("dve", lambda e: e.memset(pcarry[:], 0.0), writes=list(pcB.values()))
        S.op("dve", lambda e: e.memset(ccarry[:], 0.0), writes=list(ccB2.values()))
        for k in range(2):
            S.op("dve", lambda e, k=k: e.memset(Ks[k][:], 0.0), writes=[KsB[k]])
            S.op("dve", lambda e, k=k: e.memset(Vp[k][:], 0.0), writes=[VpB[k]])
        S.op("dve", lambda e: e.memset(Ts[:], 0.0), writes=[TsB])
        S.op("dve", lambda e: e.memset(sks[:], 0.0), writes=[sksB])
        S.op("dve", lambda e: e.memset(small[:], 0.0), writes=smallB)

        S.op("dve", lambda e: e.memset(relb_aug[:], 1.0), writes=[relbB])
        S.dma("sp", [(relb_aug[0:32, :], relbd)], writes=[relbB], sem="relb")
        S.dma("sp", [(oht[0:33, :], ohd)], writes=[ohB], sem="oh")
        pt, pbk, _ = bank()
        S.op("pe", lambda e: e.matmul(pt[0:16, 0:384], lhsT=relb_aug[0:33, 0:16], rhs=oht[0:33, 0:384], start=True, stop=True),
             reads=[relbB, ohB], writes=[pbk])
        S.op("act", lambda e: e.activation(out=gsb[0:16, :], in_=pt[0:16, 0:384], func=AF.Copy), reads=[pbk], writes=[gsbB])
        S.dma("sp", [(gd, gsb[0:16, :])], reads=[gsbB], writes=[gdB], sem="gd")
        S.dma("sp", [(Hk, bass.AP(gdt, 0, [[1, 128], [384, 16], [1, 256]]))], reads=[gdB], writes=[HkB], sem="hk")
        for h in range(16):
            pt, pbk, _ = bank()
            S.op("pe", lambda e, pt=pt, h=h: e.matmul(pt[:, 128:256], lhsT=Hk[:, h, 0:128], rhs=Jm[:], start=True, stop=True),
                 reads=[HkB, JB], writes=[pbk], signal=False)
            S.op("pe", lambda e, pt=pt, h=h: e.matmul(pt[:, 0:128], lhsT=Hk[:, h, 128:256], rhs=Jm[:], start=True, stop=True),
                 reads=[HkB, JB], writes=[pbk])
            copy_op(evac_eng(), Tb[:, h, :], pt[:, 0:256], [pbk], [TbB])
        for kv in range(4):
            for g in range(4):
                S.dma("sp", [(Ts[4 * g:4 * g + 4, kv, :], Tb[0:4, 4 * kv + g, :])], reads=[TbB], writes=[TsB], sem="ts")
        S.dma("sp", [(skb[:], sinksd.rearrange("a b -> (a b)").partition_broadcast(128))], writes=[skbB], sem="skb")
        for j in range(2):
            for kv in range(4):
                for g in range(4):
                    S.dma("sp", [(sks[4 * g:4 * g + 4, j, kv:kv + 1],
                                  sinksd[j, 4 * kv + g:4 * kv + g + 1].partition_broadcast(4))],
                          writes=[sksB], sem="sks")

        def load_w(dst, src, buf, sem):
            S.dma("pool", [(dst, src)], writes=[buf], sem=sem)

        def load_wup(i, j):
            k = j % 3
            load_w(wup[k][:].rearrange("p h k c -> p (h k c)"), wupd[i, j], wupB[k], f"wup{k}")

        def load_wdn(i, pi):
            j0, j1 = PARTS[pi]
            k = pi % 2
            load_w(wdn[k][:, 0:j1 - j0, :], wdnd[i, j0:j1].rearrange("j p n -> p j n"), wdnB[k], f"wdn{k}")

        def load_mixer(i):
            if i < 2:
                load_w(wpl, wpld[i].rearrange("g k p n -> p g k n"), wplB, "wpl")
            else:
                load_w(wq, wqd[i - 2].rearrange("k p n -> p k n"), wqB, "wq")
                load_w(wo, wod[i - 2].rearrange("k p n -> p k n"), woB, "wo")

        def load_ple(i):
            load_w(wg, wgd[i].rearrange("k p n -> p k n"), wgB, "wg")
            load_w(wpr, wprd[i].rearrange("k p n -> p k n"), wprB, "wpr")

        import os
        SKIP = os.environ.get("KSKIP", "").split(",")

        def hreads(b):
            return [hB[(c, b)] for c in range(NCH)]

        def rstd_of(c0, N, b, scale, lhs, lhsB, src_fn, src_reads, nk, k):
            pt, pbk, _ = bank()
            for c in range(nk):
                q = c % 2
                S.op("act", lambda e, c=c, q=q: e.activation(out=sqb[q][:, 0:N], in_=src_fn(c), func=AF.Square),
                     reads=src_reads(c), writes=[sqB[q]])
                S.op("pe", lambda e, c=c, q=q, pt=pt: e.matmul(pt[:, 0:N], lhsT=lhs, rhs=sqb[q][:, 0:N], start=(c == 0), stop=(c == nk - 1)),
                     reads=[sqB[q], lhsB], writes=[pbk], signal=True)
            S.op("act", lambda e, pt=pt: e.activation(out=nr[k][:, 0:N], in_=pt[:, 0:N], func=AF.Ln, bias=col(C_EPS), scale=scale),
                 reads=[pbk, colsB], writes=[nrB[k]])
            S.op("act", lambda e: e.activation(out=nr[k][:, 0:N], in_=nr[k][:, 0:N], func=AF.Exp, scale=-0.5), reads=[nrB[k]], writes=[nrB[k]])

        def rms_block(blk, b, gbase, out, outB, k=0):
            c0, N, kind = blk
            rstd_of(c0, N, b, 1.0 / D, ones_b[:], onesB, lambda c: hT[:, c, c0:c0 + N], lambda c: [hB[(c, b)]], NCH, k)
            for c in range(NCH):
                S.op("dve", lambda e, c=c: e.scalar_tensor_tensor(out=out(c), in0=hT[:, c, c0:c0 + N], scalar=col(gbase + c),
                                                                  in1=nr[k][:, 0:N], op0=ALU.mult, op1=ALU.mult),
                     reads=[hB[(c, b)], nrB[k], colsB], writes=[outB])

        def load_x(sbi, blocks):
            row0 = 0 if sbi == 0 else TA
            tiles = []
            for (c0, N, kind) in blocks:
                if kind == "p":
                    for t in range(N // 128):
                        tiles.append((row0 + c0 + 128 * t, c0 + 128 * t, 128))
                else:
                    tiles.append((NPROMPT, c0, 64))
            for ti, (r0, cc0, rows) in enumerate(tiles):
                q = ti % 2
                b = cc0 // 512
                S.dma("sp", [(xst[q][0:rows, :], xin[r0:r0 + rows, :])], writes=[xstB[q]], sem=f"xst{q}")
                for hb in range(2):
                    pt, pbk, _ = bank()
                    for cq in range(4):
                        c = hb * 4 + cq
                        S.op("pe", lambda e, pt=pt, c=c, cq=cq, q=q, rows=rows: e.transpose(
                            out=pt[:, cq * 128:cq * 128 + rows], in_=xst[q][0:rows, c * 128:(c + 1) * 128], identity=ident_f[0:rows, 0:rows]),
                            reads=[xstB[q], idfB], writes=[pbk], signal=(cq == 3))
                    src = pt[:, :].rearrange("p (a b) -> p a b", a=4)[:, :, 0:rows]
                    copy_op(evac_eng(), hT[:, hb * 4:hb * 4 + 4, cc0:cc0 + rows], src, [pbk], [hB[(c, b)] for c in range(hb * 4, hb * 4 + 4)])

        def store_y(sbi, blocks):
            row0 = 0 if sbi == 0 else TA
            tiles = []
            for (c0, N, kind) in blocks:
                if kind == "p":
                    for t in range(N // 128):
                        tiles.append((row0 + c0 + 128 * t, c0 + 128 * t, 128))
                else:
                    tiles.append((NPROMPT, c0, 64))
            for ti, (r0, cc0, rows) in enumerate(tiles):
                q = ti % 2
                b = cc0 // 512
                for hb in range(2):
                    pt, pbk, _ = bank()
                    for cq in range(4):
                        c = hb * 4 + cq
                        S.op("pe", lambda e, pt=pt, c=c, cq=cq, rows=rows, cc0=cc0: e.transpose(
                            out=pt[0:rows, cq * 128:(cq + 1) * 128], in_=hT[:, c, cc0:cc0 + rows], identity=ident_f[:]),
                            reads=[hB[(c, b)], idfB], writes=[pbk], signal=(cq == 3))
                    copy_op(evac_eng(), xst[q][0:rows, hb * 512:(hb + 1) * 512], pt[0:rows, :], [pbk], [xstB[q]])
                S.dma("sp", [(yout[r0:r0 + rows, :], xst[q][0:rows, :])], reads=[xstB[q]], sem=f"xst{q}", final=True)

        def pool_block(i, sbi, blk, b):
            c0, N, kind = blk
            rstd_of(c0, N, b, 1.0 / D, ones_b[:], onesB, lambda c: hT[:, c, c0:c0 + N], lambda c: [hB[(c, b)]], NCH, 0)
            first = (sbi == 0 and b == 0)
            if kind == "s":
                for hf in range(2):
                    S.dma("sp", [(spst[hf][0:120, :], spool[i, 8 * hf:8 * hf + 8].rearrange("s r d -> (s r) d"))],
                          writes=[spstB[hf]], sem=f"spst{hf}")
                for hf in range(2):
                    S.dma("sp", [(pools_o[i, 8 * hf + s_, 0:11, :], spst[hf][15 * s_ + 4:15 * s_ + 15, :]) for s_ in range(8)],
                          reads=[spstB[hf]], sem=f"spo{hf}", final=True)
                hold = [bank(hold=True), bank(hold=True)]
            for c in range(NCH):
                g = c // 2
                w = POOL_W[g]
                q = c % 2
                X = xe[q]
                if kind == "p":
                    L = 15 + N
                    S.op("act", lambda e, X=X, c=c: e.activation(out=X[:, 0:15], in_=pcarry[:, i, c, :], func=AF.Copy),
                         reads=[pcB[(i, c)]], writes=[xeB[q]])
                    S.op("dve", lambda e, X=X, c=c: e.scalar_tensor_tensor(out=X[:, 15:15 + N], in0=hT[:, c, c0:c0 + N], scalar=col(C_NM + i * 8 + c),
                                                                           in1=nr[0][:, 0:N], op0=ALU.mult, op1=ALU.mult),
                         reads=[hB[(c, b)], nrB[0], colsB], writes=[xeB[q]])
                    if first:
                        S.op("dve", lambda e, X=X: e.tensor_scalar(out=X[:, 15 + 241:15 + 256], in0=X[:, 15 + 241:15 + 256],
                                                                   scalar1=cc[:, CC_M:CC_M + 1], scalar2=None, op0=ALU.mult),
                             reads=[xeB[q], ccB], writes=[xeB[q]])
                    S.op("act", lambda e, X=X, c=c: e.activation(out=pcarry[:, i, c, :], in_=X[:, N:N + 15], func=AF.Copy),
                         reads=[xeB[q]], writes=[pcB[(i, c)]])
                    xnew = X[:, 15:15 + N]
                    dout_ap = xm[:, c, 0:N]
                else:
                    L = 304
                    X3 = X[:, 0:304].rearrange("p (s k) -> p s k", k=19)
                    pt, pbk, _ = bank()
                    for hf in range(2):
                        S.op("pe", lambda e, pt=pt, hf=hf, c=c: e.transpose(out=pt[:, hf * 120:hf * 120 + 120], in_=spst[hf][0:120, c * 128:(c + 1) * 128],
                                                                            identity=ident_f[0:120, 0:120]),
                             reads=[spstB[hf], idfB], writes=[pbk], signal=(hf == 1))
                    copy_op("act", X3[:, :, 0:15], pt[:, 0:240].rearrange("p (s k) -> p s k", k=15), [pbk], [xeB[q]])
                    hv = hT[:, c, c0:c0 + 64].rearrange("p (t s) -> p s t", s=16)
                    nv = nr[0][:, 0:64].rearrange("p (t s) -> p s t", s=16)
                    S.op("dve", lambda e, c=c: e.scalar_tensor_tensor(out=xns[:, :], in0=hT[:, c, c0:c0 + 64], scalar=col(C_NM + i * 8 + c),
                                                                      in1=nr[0][:, 0:64], op0=ALU.mult, op1=ALU.mult),
                         reads=[hB[(c, b)], nrB[0], colsB], writes=[xnsB])
                    S.op("act", lambda e, X3=X3: e.activation(out=X3[:, :, 15:19], in_=xns[:, :].rearrange("p (t s) -> p s t", s=16), func=AF.Copy),
                         reads=[xnsB], writes=[xeB[q]])
                    hp, hpb, _ = hold[c // 4]
                    S.op("pe", lambda e, hp=hp, c=c: e.transpose(out=hp[0:64, (c % 4) * 128:(c % 4 + 1) * 128], in_=xns[:, :], identity=ident_f[:]),
                         reads=[xnsB, idfB], writes=[hpb])
                    xnew = X3[:, :, 15:19]
                    dout_ap = xm[:, c, 0:64].rearrange("p (t s) -> p s t", s=16)
                a, aB = X, xeB[q]
                lo = 0
                for si, s in enumerate((1, 2, 4, 8)[:g + 1]):
                    lo2 = lo + s
                    tb_, tbB_ = tab[si % 2], tabB[si % 2]
                    S.op("dve", lambda e, a=a, tb_=tb_, lo2=lo2, s=s, L=L: e.tensor_tensor(out=tb_[:, lo2:L], in0=a[:, lo2:L], in1=a[:, lo2 - s:L - s], op=ALU.add),
                         reads=[aB], writes=[tbB_])
                    a, aB, lo = tb_, tbB_, lo2
                if kind == "p":
                    asum = a[:, 15:15 + N]
                else:
                    asum = a[:, 0:304].rearrange("p (s k) -> p s k", k=19)[:, :, 15:19]
                S.op("dve", lambda e, asum=asum, xnew=xnew, dout_ap=dout_ap, w=w: e.scalar_tensor_tensor(out=dout_ap, in0=asum, scalar=1.0 / w, in1=xnew,
                                                                                                      op0=ALU.mult, op1=ALU.subtract),
                     reads=[aB, xeB[q]], writes=[xmB])
                if first:
                    t15 = tab[(g + 1) % 2][:, 0:15]
                    S.op("dve", lambda e, a=a, t15=t15, g=g: e.tensor_tensor(out=t15, in0=a[:, 15 + 256:15 + 271], in1=cc[:, CC_INV + 15 * g:CC_INV + 15 * g + 15], op=ALU.mult),
                         reads=[aB, ccB], writes=[tabB[(g + 1) % 2]])
                    S.op("dve", lambda e, X=X, t15=t15, c=c: e.tensor_tensor(out=xm[:, c, 256:271], in0=t15, in1=X[:, 15 + 256:15 + 271], op=ALU.subtract),
                         reads=[tabB[(g + 1) % 2], xeB[q]], writes=[xmB])
            if kind == "s":
                for hb in range(2):
                    hp, hpb, hk = hold[hb]
                    copy_op(evac_eng(), ost[0:64, hb * 512:(hb + 1) * 512], hp[0:64, :], [hpb], [ostB])
                    release(hk)
                S.dma("sp", [(pools_o[i, :, 11 + t, :], ost[16 * t:16 * t + 16, :]) for t in range(4)], reads=[ostB], sem="ost", final=True)
            M = N
            for g in range(4):
                for mo in range(2):
                    pt, pbk, _ = bank()
                    for kc in range(2):
                        S.op("pe", lambda e, pt=pt, g=g, mo=mo, kc=kc: e.matmul(pt[:, 0:M], lhsT=wpl[:, g, kc, mo * 128:(mo + 1) * 128], rhs=xm[:, 2 * g + kc, 0:M],
                                                                               start=(kc == 0), stop=(kc == 1)),
                             reads=[wplB, xmB], writes=[pbk], signal=(kc == 1))
                    m = 2 * g + mo
                    S.op("dve", lambda e, pt=pt, m=m: e.scalar_tensor_tensor(out=hT[:, m, c0:c0 + M], in0=pt[:, 0:M], scalar=col(C_PSC + i * 8 + m),
                                                                            in1=hT[:, m, c0:c0 + M], op0=ALU.mult, op1=ALU.add),
                         reads=[pbk, hB[(m, b)], colsB], writes=[hB[(m, b)]])

        def qk_norm(pt, pbk, N, gcol, out_bf, out_bf_B, out_f32=None, out_f32_B=None, k=1):
            rstd_of(0, N, 0, 1.0 / 64, blk_b[:], blkB, lambda c: pt[:, 0:N], lambda c: [pbk], 1, k)
            S.op("dve", lambda e: e.scalar_tensor_tensor(out=out_bf, in0=pt[:, 0:N], scalar=col(gcol), in1=nr[k][:, 0:N], op0=ALU.mult, op1=ALU.mult),
                 reads=[pbk, nrB[k], colsB], writes=[out_bf_B])
            if out_f32 is not None:
                S.op("dve", lambda e: e.scalar_tensor_tensor(out=out_f32, in0=pt[:, 0:N], scalar=col(gcol), in1=nr[k][:, 0:N], op0=ALU.mult, op1=ALU.mult),
                     reads=[pbk, nrB[k], colsB], writes=[out_f32_B])

        def kv_block(sbi, blk, b, last_prompt):
            c0, N, kind = blk
            rms_block(blk, b, C_KVN, lambda c: xm[:, c, 0:N], xmB, k=0)
            want_cache = (sbi == 1) and (kind == "s" or last_prompt) and ("kvout" not in SKIP)
            for kp in range(2):
                if kind == "s" and "skv_k" in SKIP:
                    continue
                pt, pbk, _ = bank()
                for k in range(NCH):
                    S.op("pe", lambda e, pt=pt, k=k, kp=kp: e.matmul(pt[:, 0:N], lhsT=wk[:, k, kp * 128:(kp + 1) * 128], rhs=xm[:, k, 0:N], start=(k == 0), stop=(k == 7)),
                         reads=[wkB, xmB], writes=[pbk], signal=(k == 7))
                if kind == "p":
                    qk_norm(pt, pbk, N, C_KN, KT[:, kp, 128 + c0:128 + c0 + N], KTB[b],
                            knf[:, 0:N] if want_cache else None, knfB)
                else:
                    qk_norm(pt, pbk, N, C_KN, KTs[:, kp, 0:64], KTsB, knf[:, 0:64], knfB)
                if want_cache:
                    rows = 128 if kind == "p" else 64
                    p2, p2b, _ = bank()
                    src = knf[:, N - 128:N] if kind == "p" else knf[:, 0:64]
                    S.op("pe", lambda e, p2=p2, src=src, rows=rows: e.transpose(out=p2[0:rows, 0:128], in_=src, identity=ident_f[:]),
                         reads=[knfB, idfB], writes=[p2b])
                    copy_op(evac_eng(), ost[0:rows, kp * 128:(kp + 1) * 128], p2[0:rows, 0:128], [p2b], [ostB])
            if want_cache:
                if kind == "p":
                    S.dma("sp", [(ckp_o, ost[:, 0:256])], reads=[ostB], sem="ost", final=True)
                else:
                    S.dma("sp", [(cks_o[:, 124 + t, :], ost[16 * t:16 * t + 16, 0:256]) for t in range(4)], reads=[ostB], sem="ost", final=True)
            if kind == "p":
                for t in range(N // 128):
                    pt, pbk, _ = bank()
                    for k in range(NCH):
                        S.op("pe", lambda e, pt=pt, k=k, t=t: e.matmul(pt[:, 0:256], lhsT=xm[:, k, 128 * t:128 * t + 128], rhs=wv[:, k, :], start=(k == 0), stop=(k == 7)),
                             reads=[wvB, xmB], writes=[pbk], signal=(k == 7))
                    gt = c0 // 128 + t
                    copy_op(evac_eng(), Vt[:, 1 + gt, :], pt[:, 0:256], [pbk], [VtB[b]])
                    if want_cache and t == N // 128 - 1:
                        copy_op("act", ost[:, 256:512], pt[:, 0:256], [pbk], [ostB])
                        S.dma("sp", [(cvp_o, ost[:, 256:512])], reads=[ostB], sem="ost", final=True)
            elif "skv_v" not in SKIP:
                pt, pbk, _ = bank()
                for k in range(NCH):
                    S.op("pe", lambda e, pt=pt, k=k: e.matmul(pt[0:64, 0:256], lhsT=xm[:, k, 0:64], rhs=wv[:, k, :], start=(k == 0), stop=(k == 7)),
                         reads=[wvB, xmB], writes=[pbk], signal=(k == 7))
                copy_op("dve", Vsm[:, :], pt[0:64, 0:256], [pbk], [VsmB])
                if "skv_o" not in SKIP:
                    copy_op("act", ost[0:64, 256:512], pt[0:64, 0:256], [pbk], [ostB])
                    S.dma("sp", [(cvs_o[:, 124 + t, :], ost[16 * t:16 * t + 16, 256:512]) for t in range(4)], reads=[ostB], sem="ost", final=True)

        sm_ctr = [0]

        def attn_stage_a(items):
            gsl = (sm_ctr[0] // 4) % 2
            for idx, it in enumerate(items):
                it["k"] = sm_ctr[0] % 8
                it["k4"] = sm_ctr[0] % 4
                sm_ctr[0] += 1
                it["sm"] = small[0:it["P"], 32 * gsl + 8 * idx:32 * gsl + 8 * idx + 8]
                it["smB"] = smallB[4 * gsl + idx]
                it["gsm"] = small[0:it["P"], 32 * gsl:32 * gsl + 32].rearrange("p (i c) -> p i c", c=8)
            for it in items:
                k, P = it["k"], it["P"]
                S.op("dve", lambda e, it=it, k=k, P=P: e.scalar_tensor_tensor(out=scb[k][0:P, :], in0=it["pss"], scalar=0.125, in1=it["bias"], op0=ALU.mult, op1=ALU.add),
                     reads=[it["pssB"], it["biasB"]], writes=[scB[k]])
                if it["mask"]:
                    S.op("dve", lambda e, k=k: e.tensor_tensor(out=scb[k][:, 0:128], in0=scb[k][:, 0:128], in1=cc[:, CC_FM:CC_FM + 128], op=ALU.add),
                         reads=[scB[k], ccB], writes=[scB[k]])
            for it in items:
                k, P, sm, smB = it["k"], it["P"], it["sm"], it["smB"]
                S.op("dve", lambda e, k=k, P=P, sm=sm: e.reduce_max(out=sm[:, 0:1], in_=scb[k][0:P, :], axis=AX.X), reads=[scB[k]], writes=[smB])
                S.op("dve", lambda e, it=it, sm=sm: e.tensor_scalar(out=sm[:, 1:2], in0=sm[:, 0:1], scalar1=it["sink"], scalar2=-1.0, op0=ALU.max, op1=ALU.mult),
                     reads=[smB, it["sinkB"]], writes=[smB])
            for it in items:
                k, P, sm, smB = it["k"], it["P"], it["sm"], it["smB"]
                S.op("act", lambda e, k=k, P=P, sm=sm: e.activation(out=eb[k][0:P, :], in_=scb[k][0:P, :], func=AF.Exp, bias=sm[:, 1:2], accum_out=sm[:, 2:3]),
                     reads=[scB[k], smB], writes=[ebB[k], smB])
                S.op("act", lambda e, it=it, sm=sm: e.activation(out=sm[:, 3:4], in_=it["sink"], func=AF.Exp, bias=sm[:, 1:2]),
                     reads=[smB, it["sinkB"]], writes=[smB])

        def attn_stage_b(items):
            ptt, pttB, _ = bank()
            ptb = ptt[:, :].bitcast(BF16)
            g0 = items[0]
            gsm, gB = g0["gsm"], [it["smB"] for it in items]
            S.op("dve", lambda e, gsm=gsm: e.tensor_tensor(out=gsm[:, :, 4], in0=gsm[:, :, 2], in1=gsm[:, :, 3], op=ALU.add), reads=gB, writes=gB)
            S.op("dve", lambda e, gsm=gsm: e.reciprocal(out=gsm[:, :, 5], in_=gsm[:, :, 4]), reads=gB, writes=gB)
            for it in items:
                k, k4, P, sm, smB = it["k"], it["k4"], it["P"], it["sm"], it["smB"]
                S.op("dve", lambda e, k=k, k4=k4, P=P, sm=sm: e.tensor_scalar(out=pb[k4][0:P, :], in0=eb[k][0:P, :], scalar1=sm[:, 5:6], scalar2=None, op0=ALU.mult),
                     reads=[ebB[k], smB], writes=[pbB[k4]])
            for idx, it in enumerate(items):
                k4, P = it["k4"], it["P"]
                base = idx * 2 * P
                for hh in range(2):
                    S.op("pe", lambda e, k4=k4, P=P, hh=hh, base=base: e.transpose(out=ptb[:, base + hh * P:base + (hh + 1) * P], in_=pb[k4][0:P, hh * 128:(hh + 1) * 128], identity=ident_b[0:P, 0:P]),
                         reads=[pbB[k4], idbB], writes=[pttB], signal=(hh == 1))
            for idx, it in enumerate(items):
                k4, P = it["k4"], it["P"]
                base = idx * 2 * P
                copy_op("act", pTs[k4][:, 0:2 * P], ptb[:, base:base + 2 * P], [pttB], [pTsB[k4]])
            for it in items:
                k4, P = it["k4"], it["P"]
                for hh in range(2):
                    vap, vB = it["v"][hh]
                    S.op("pe", lambda e, it=it, k4=k4, P=P, hh=hh, vap=vap: e.matmul(it["po"], lhsT=vap, rhs=pTs[k4][:, hh * P:(hh + 1) * P], start=(hh == 0), stop=(hh == 1)),
                         reads=[pTsB[k4], vB], writes=[it["poB"]], signal=(hh == 1))

        pendg = [None]

        def attn_block(i, sbi, blk, b):
            j = i - 2
            c0, N, kind = blk
            rms_block(blk, b, C_NM + i * 8, lambda c: xm[:, c, 0:N], xmB, k=0)
            for cp in range(NCH):
                pt, pbk, _ = bank()
                for k in range(NCH):
                    S.op("pe", lambda e, pt=pt, k=k, cp=cp: e.matmul(pt[:, 0:N], lhsT=wq[:, k, cp * 128:(cp + 1) * 128], rhs=xm[:, k, 0:N], start=(k == 0), stop=(k == 7)),
                         reads=[wqB, xmB], writes=[pbk], signal=(k == 7))
                qk_norm(pt, pbk, N, C_QN + j, qT[:, cp, 0:N], qTB, k=1)
            if kind == "p":
                groups = [(t, cpp) for t in range(N // 128) for cpp in range(0, NCH, 2)]

                def emit_scores(t, cpp):
                    tcol = c0 + 128 * t
                    gt = tcol // 128
                    kb_prev = (KTcB, VtcB) if gt == 0 else (KTB[(tcol - 128) // 512], VtB[(tcol - 128) // 512])
                    items = []
                    for cp in (cpp, cpp + 1):
                        pss, pssB, _ = bank()
                        kp = cp // 4
                        for hf in range(2):
                            kv = 2 * kp + hf
                            head = 4 * kv + cp % 4
                            r0, r1 = 64 * hf, 64 * hf + 64
                            S.op("pe", lambda e, pss=pss, hf=hf, r0=r0, r1=r1, cp=cp, t=t, kp=kp, tcol=tcol: e.matmul(
                                pss[:, 256 * hf:256 * hf + 256], lhsT=qT[r0:r1, cp, 128 * t:128 * t + 128], rhs=KT[r0:r1, kp, tcol:tcol + 256], start=True, stop=True),
                                reads=[qTB, KTB[b], kb_prev[0]], writes=[pssB])
                            items.append(dict(pss=pss[:, 256 * hf:256 * hf + 256], pssB=pssB, bias=Tb[:, head, :], biasB=TbB,
                                              sink=skb[:, 16 * j + head:16 * j + head + 1], sinkB=skbB, mask=(sbi == 0 and gt == 2),
                                              v=[(Vt[:, gt, kv * 64:(kv + 1) * 64], kb_prev[1]), (Vt[:, gt + 1, kv * 64:(kv + 1) * 64], VtB[b])],
                                              P=128, cp=cp, rows=(r0, r1)))
                    return items

                def fin_p(items, t):
                    outs = {}
                    for it in items:
                        if it["cp"] not in outs:
                            outs[it["cp"]] = bank()
                        psO, psOB, _ = outs[it["cp"]]
                        it["po"] = psO[it["rows"][0]:it["rows"][1], 0:128]
                        it["poB"] = psOB
                    attn_stage_b(items)

                    def cp_out(outs=outs, t=t):
                        for cp, (psO, psOB, _) in outs.items():
                            copy_op("act", OT[:, cp, 128 * t:128 * t + 128], psO[:, 0:128], [psOB], [OTB])
                    return cp_out

                sc_items = [None] * len(groups)
                sc_items[0] = emit_scores(*groups[0])
                prev = None
                late = None
                for gi, (t, cpp) in enumerate(groups):
                    if gi + 1 < len(groups):
                        sc_items[gi + 1] = emit_scores(*groups[gi + 1])
                    if late is not None:
                        late()
                        late = None
                    attn_stage_a(sc_items[gi])
                    if prev is not None:
                        late = fin_p(*prev)
                    prev = (sc_items[gi], t)
                if late is not None:
                    late()
                if prev is not None:
                    fin_p(*prev)()
            elif "sattn" in SKIP:
                S.op("dve", lambda e: e.memset(OT[:, :, 0:64], 0.0), writes=[OTB])
            else:
                for cp in range(NCH):
                    S.op("dve", lambda e, cp=cp: e.tensor_copy(out=qs2[:, cp // 4, :, 4 * (cp % 4):4 * (cp % 4) + 4],
                                                             in_=qT[:, cp, 0:64].rearrange("p (t s) -> p s t", s=16)),
                         reads=[qTB], writes=[qs2B])
                for s in range(16):
                    q = s % 2
                    S.dma("sp", [(ckst[q][:, :], ckin[s])], writes=[ckstB[q]], sem=f"ckst{q}")
                    S.dma("sp", [(cvst[q][:, :], cvin[s])], writes=[cvstB[q]], sem=f"cvst{q}")
                    if j == 0:
                        S.dma("sp", [(cks_o[s, 0:124, :], ckst[q][4:128, :]), (cvs_o[s, 0:124, :], cvst[q][4:128, :])],
                              reads=[ckstB[q], cvstB[q]], sem=f"cko{q}", final=True)
                    pt, pbk, _ = bank()
                    for kp in range(2):
                        S.op("pe", lambda e, pt=pt, kp=kp, q=q: e.transpose(out=pt[:, kp * 128:(kp + 1) * 128], in_=ckst[q][:, kp * 128:(kp + 1) * 128], identity=ident_f[:]),
                             reads=[ckstB[q], idfB], writes=[pbk], signal=(kp == 1))
                    copy_op("act", Ks[q][:, :, 0:128], pt[:, 0:256].rearrange("p (a b) -> p a b", a=2), [pbk], [KsB[q]])
                    S.op("dve", lambda e, q=q, s=s: e.tensor_copy(out=Ks[q][:, :, 128:132], in_=KTs[:, :, s:64:16]), reads=[KTsB], writes=[KsB[q]])
                    copy_op("dve", Vs[q][:, :], cvst[q][:, :], [cvstB[q]], [VsB[q]])
                    S.dma("sp", [(Vp[q][t:t + 1, :], Vsm[16 * t + s:16 * t + s + 1, :]) for t in range(4)], reads=[VsmB], writes=[VpB[q]], sem=f"vp{q}")
                    items = []
                    for kp in range(2):
                        pss, pssB, _ = bank()
                        for hf in range(2):
                            kv = 2 * kp + hf
                            r0, r1 = 64 * hf, 64 * hf + 64
                            S.op("pe", lambda e, pss=pss, hf=hf, r0=r0, r1=r1, kp=kp, q=q, s=s: e.matmul(
                                pss[0:16, 256 * hf:256 * hf + 256], lhsT=qs2[r0:r1, kp, s, :], rhs=Ks[q][r0:r1, kp, :], start=True, stop=True),
                                reads=[qs2B, KsB[q]], writes=[pssB])
                            items.append(dict(pss=pss[0:16, 256 * hf:256 * hf + 256], pssB=pssB, bias=Ts[0:16, kv, :], biasB=TsB,
                                              sink=sks[0:16, j, kv:kv + 1], sinkB=sksB, mask=False,
                                              v=[(Vs[q][:, kv * 64:(kv + 1) * 64], VsB[q]), (Vp[q][:, kv * 64:(kv + 1) * 64], VpB[q])],
                                              P=16, cp=kp, rows=(r0, r1)))
                    attn_stage_a(items)
                    if pendg[0] is not None:
                        pendg[0]()

                    def fin_s(items=items, s=s):
                        outs = {}
                        for it in items:
                            if it["cp"] not in outs:
                                outs[it["cp"]] = bank()
                            psO, psOB, _ = outs[it["cp"]]
                            it["po"] = psO[it["rows"][0]:it["rows"][1], 0:16]
                            it["poB"] = psOB
                        attn_stage_b(items)
                        for kp, (psO, psOB, _) in outs.items():
                            for hf in range(2):
                                r0, r1 = 64 * hf, 64 * hf + 64
                                src = psO[r0:r1, 0:16].rearrange("p (g t) -> p g t", t=4)
                                copy_op(evac_eng(), OT[r0:r1, 4 * kp:4 * kp + 4, s:64:16], src, [psOB], [OTB])
                    pendg[0] = fin_s
                if pendg[0] is not None:
                    pendg[0]()
                    pendg[0] = None
            for m in range(NCH):
                pt, pbk, _ = bank()
                for cp in range(NCH):
                    S.op("pe", lambda e, pt=pt, cp=cp, m=m: e.matmul(pt[:, 0:N], lhsT=wo[:, cp, m * 128:(m + 1) * 128], rhs=OT[:, cp, 0:N], start=(cp == 0), stop=(cp == 7)),
                         reads=[woB, OTB], writes=[pbk], signal=(cp == 7))
                S.op("dve", lambda e, pt=pt, m=m: e.tensor_tensor(out=hT[:, m, c0:c0 + N], in0=hT[:, m, c0:c0 + N], in1=pt[:, 0:N], op=ALU.add),
                     reads=[pbk, hB[(m, b)]], writes=[hB[(m, b)]])

        def load_cprev(i):
            for pc in range(4):
                S.dma("sp", [(scst[0:32, :], sconv[i, :, :, pc * 1408:(pc + 1) * 1408].rearrange("s r n -> (s r) n"))], writes=[scstB], sem="scst")
                pt, pbk, _ = bank()
                for x in range(11):
                    S.op("pe", lambda e, pt=pt, x=x: e.transpose(out=pt[:, x * 32:(x + 1) * 32], in_=scst[0:32, x * 128:(x + 1) * 128], identity=ident_f[0:32, 0:32]),
                         reads=[scstB, idfB], writes=[pbk], signal=(x == 10))
                copy_op(evac_eng(), cprev[:, pc * 11:(pc + 1) * 11, :], pt[:, 0:352].rearrange("p (a b) -> p a b", a=11), [pbk],
                        [cprevB[ch] for ch in range(pc * 11, (pc + 1) * 11)])

        ws_ctr = [0]
        PROD_ENG = os.environ.get("KPROD", "pool")

        def ffn(i, sbi, blocks):
            for b, blk in enumerate(blocks):
                c0, N, kind = blk
                rms_block(blk, b, C_NF + i * 8, lambda c, c0=c0, N=N: xnf[:, c, c0:c0 + N], xnfB[b], k=b % 2)
            pend = [None]
            for pi, (j0, j1) in enumerate(PARTS):
                for j in range(j0, j1):
                    if j + 2 < NJ:
                        load_wup(i, j + 2)
                    wu, wuB = wup[j % 3], wupB[j % 3]
                    jj = j - j0
                    for b, (c0, N, kind) in enumerate(blocks):
                        st = ws_ctr[0] % 3
                        ws_ctr[0] += 1
                        pss = [bank(), bank()]
                        for hf in range(2):
                            pt, pbk, _ = pss[hf]
                            for k in range(NCH):
                                S.op("pe", lambda e, pt=pt, hf=hf, k=k, wu=wu, c0=c0, N=N: e.matmul(pt[:, 0:N], lhsT=wu[:, hf, k, :], rhs=xnf[:, k, c0:c0 + N], start=(k == 0), stop=(k == 7)),
                                     reads=[wuB, xnfB[b]], writes=[pbk], signal=(k == 7))
                        W = []
                        for hf in range(2):
                            ch = hf * NJ + j
                            W.append((col(C_CW + (i * 3 + 0) * 44 + ch), col(C_CW + (i * 3 + 1) * 44 + ch), col(C_CW + (i * 3 + 2) * 44 + ch), col(C_CB + i * 44 + ch), ch))
                        if kind == "p":
                            gb = sbi * 3 + b
                            rp, wp = gb % 2, (gb + 1) % 2
                            if sbi == 0 and b == 0:
                                for hf in range(2):
                                    pt, pbk, _ = pss[hf]
                                    S.op("dve", lambda e, pt=pt: e.tensor_scalar(out=pt[:, 254:256], in0=pt[:, 254:256], scalar1=cc[:, CC_M:CC_M + 1], scalar2=None, op0=ALU.mult),
                                         reads=[pbk, ccB], writes=[pbk])
                            for hf in range(2):
                                pt, pbk, _ = pss[hf]
                                w0, w1, w2, bcol, ch = W[hf]
                                Cb, CbB = cbuf[st][hf], cbB[st][hf]
                                S.op("act", lambda e, Cb=Cb, pt=pt, w2=w2, bcol=bcol, N=N: e.activation(out=Cb[:, 0:N], in_=pt[:, 0:N], func=AF.Identity, bias=bcol, scale=w2),
                                     reads=[pbk, colsB], writes=[CbB])
                                S.op("act", lambda e, pt=pt, ch=ch, N=N, wp=wp: e.activation(out=ccarry[:, wp, i, :, ch], in_=pt[:, N - 2:N], func=AF.Copy),
                                     reads=[pbk], writes=[ccB2[(wp, i, ch)]])
                            for hf in range(2):
                                pt, pbk, _ = pss[hf]
                                w0, w1, w2, bcol, ch = W[hf]
                                Cb, CbB = cbuf[st][hf], cbB[st][hf]
                                S.op("dve", lambda e, Cb=Cb, pt=pt, w1=w1, N=N: e.scalar_tensor_tensor(out=Cb[:, 1:N], in0=pt[:, 0:N - 1], scalar=w1, in1=Cb[:, 1:N], op0=ALU.mult, op1=ALU.add),
                                     reads=[pbk, CbB, colsB], writes=[CbB])
                            for hf in range(2):
                                pt, pbk, _ = pss[hf]
                                w0, w1, w2, bcol, ch = W[hf]
                                Cb, CbB = cbuf[st][hf], cbB[st][hf]
                                S.op("dve", lambda e, Cb=Cb, pt=pt, w0=w0, N=N: e.scalar_tensor_tensor(out=Cb[:, 2:N], in0=pt[:, 0:N - 2], scalar=w0, in1=Cb[:, 2:N], op0=ALU.mult, op1=ALU.add),
                                     reads=[pbk, CbB, colsB], writes=[CbB])
                            for hf in range(2):
                                pt, pbk, _ = pss[hf]
                                w0, w1, w2, bcol, ch = W[hf]
                                Cb, CbB = cbuf[st][hf], cbB[st][hf]
                                S.op("dve", lambda e, Cb=Cb, w1=w1, ch=ch, rp=rp: e.scalar_tensor_tensor(out=Cb[:, 0:1], in0=ccarry[:, rp, i, 1:2, ch], scalar=w1, in1=Cb[:, 0:1], op0=ALU.mult, op1=ALU.add),
                                     reads=[ccB2[(rp, i, ch)], CbB, colsB], writes=[CbB])
                                S.op("dve", lambda e, Cb=Cb, w0=w0, ch=ch, rp=rp: e.scalar_tensor_tensor(out=Cb[:, 0:2], in0=ccarry[:, rp, i, :, ch], scalar=w0, in1=Cb[:, 0:2], op0=ALU.mult, op1=ALU.add),
                                     reads=[ccB2[(rp, i, ch)], CbB, colsB], writes=[CbB])
                        else:
                            for hf in range(2):
                                pt, pbk, _ = pss[hf]
                                w0, w1, w2, bcol, ch = W[hf]
                                Cb, CbB = cbuf[st][hf], cbB[st][hf]
                                U3 = ugs[hf][:, 0:96].rearrange("p (s k) -> p s k", k=6)
                                UB = ugsB[hf]
                                S.op("act", lambda e, U3=U3, pt=pt: e.activation(out=U3[:, :, 2:6], in_=pt[:, 0:64].rearrange("p (t s) -> p s t", s=16), func=AF.Copy),
                                     reads=[pbk], writes=[UB])
                                S.op("dve", lambda e, U3=U3, ch=ch: e.tensor_copy(out=U3[:, :, 0:2], in_=cprev[:, ch, :].rearrange("p (s r) -> p s r", r=2)),
                                     reads=[cprevB[ch]], writes=[UB])
                                S.op("dve", lambda e, U3=U3, ch=ch: e.tensor_copy(out=cprev[:, ch, :].rearrange("p (s r) -> p s r", r=2), in_=U3[:, :, 4:6]),
                                     reads=[UB], writes=[cprevB[ch]])
                                t0, t1, t2 = U3[:, :, 0:4], U3[:, :, 1:5], U3[:, :, 2:6]
                                cv_ = Cb[:, 0:64].rearrange("p (t s) -> p s t", s=16)
                                S.op("act", lambda e, cv_=cv_, t0=t0, w0=w0, bcol=bcol: e.activation(out=cv_, in_=t0, func=AF.Identity, bias=bcol, scale=w0),
                                     reads=[UB, colsB], writes=[CbB])
                                S.op("dve", lambda e, cv_=cv_, t1=t1, w1=w1: e.scalar_tensor_tensor(out=cv_, in0=t1, scalar=w1, in1=cv_, op0=ALU.mult, op1=ALU.add),
                                     reads=[UB, CbB, colsB], writes=[CbB])
                                S.op("dve", lambda e, cv_=cv_, t2=t2, w2=w2: e.scalar_tensor_tensor(out=cv_, in0=t2, scalar=w2, in1=cv_, op0=ALU.mult, op1=ALU.add),
                                     reads=[UB, CbB, colsB], writes=[CbB])
                        if pend[0] is not None:
                            pend[0]()
                        Cg, Cv = cbuf[st][0], cbuf[st][1]

                        def fin(Cg=Cg, Cv=Cv, st=st, jj=jj, b=b, c0=c0, N=N):
                            S.op("act", lambda e: e.activation(out=Cg[:, 0:N], in_=Cg[:, 0:N], func=AF.Gelu_apprx_tanh), reads=[cbB[st][0]], writes=[cbB[st][0]])
                            S.op(PROD_ENG, lambda e: e.tensor_tensor(out=actT[:, jj, c0:c0 + N], in0=Cg[:, 0:N], in1=Cv[:, 0:N], op=ALU.mult),
                                 reads=[cbB[st][0], cbB[st][1]], writes=[actB[(jj, b)]])
                        pend[0] = fin
                if pend[0] is not None:
                    pend[0]()
                    pend[0] = None
                if pi + 1 < len(PARTS):
                    load_wdn(i, pi + 1)
                wd, wdB = wdn[pi % 2], wdnB[pi % 2]
                nj = j1 - j0
                for b, (c0, N, kind) in enumerate(blocks):
                    for m in range(NCH):
                        pt, pbk, _ = bank()
                        for jj in range(nj):
                            S.op("pe", lambda e, pt=pt, jj=jj, m=m, wd=wd, c0=c0, N=N: e.matmul(pt[:, 0:N], lhsT=wd[:, jj, m * 128:(m + 1) * 128], rhs=actT[:, jj, c0:c0 + N], start=(jj == 0), stop=(jj == nj - 1)),
                                 reads=[wdB, actB[(jj, b)]], writes=[pbk], signal=(jj == nj - 1))
                        S.op("dve", lambda e, pt=pt, m=m, c0=c0, N=N: e.tensor_tensor(out=hT[:, m, c0:c0 + N], in0=hT[:, m, c0:c0 + N], in1=pt[:, 0:N], op=ALU.add),
                             reads=[pbk, hB[(m, b)]], writes=[hB[(m, b)]])

        def conv_state_out(i, sbi):
            if sbi != 1:
                return
            pt, pbk, _ = bank()
            S.op("pe", lambda e, pt=pt: e.transpose(out=pt[0:88, 0:128], in_=ccarry[:, 1, i, :, :], identity=ident_f[:]),
                 reads=[ccB2[(1, i, ch)] for ch in range(44)] + [idfB], writes=[pbk])
            copy_op(evac_eng(), ost[0:88, 0:128], pt[0:88, 0:128], [pbk], [ostB])
            S.dma("sp", [(convp_o[i, r, :].rearrange("(j p) -> j p", p=128), ost[44 * r:44 * r + 44, 0:128]) for r in range(2)], reads=[ostB], sem="ost", final=True)
            for x in range(11):
                pt, pbk, _ = bank()
                S.op("pe", lambda e, pt=pt, x=x: e.transpose(out=pt[:, 0:128], in_=cprev[:, 4 * x:4 * x + 4, :], identity=ident_f[:]),
                     reads=[cprevB[ch] for ch in range(4 * x, 4 * x + 4)] + [idfB], writes=[pbk])
                copy_op(evac_eng(), ost[:, 0:128], pt[:, 0:128], [pbk], [ostB])
                S.dma("sp", [(convs_o[i, :, :, (4 * x + j4) * 128:(4 * x + j4 + 1) * 128].rearrange("s r p -> (s r) p"), ost[32 * j4:32 * j4 + 32, 0:128]) for j4 in range(4)],
                      reads=[ostB], sem="ost", final=True)

        def ple_block(i, sbi, blk, b):
            c0, N, kind = blk
            row0 = 0 if sbi == 0 else TA
            rms_block(blk, b, C_NP + i * 8, lambda c: xm[:, c, 0:N], xmB, k=0)
            if kind == "p":
                tiles = [(row0 + c0 + 128 * t, 128 * t, 128) for t in range(N // 128)]
            else:
                tiles = [(NPROMPT, 0, 64)]
            for ti, (r0, lc, rows) in enumerate(tiles):
                q = ti % 2
                S.dma("sp", [(pst[q][0:rows, :], pin[i, r0:r0 + rows, :])], writes=[pstB[q]], sem=f"pst{q}")
                pt, pbk, _ = bank()
                for kc in range(2):
                    S.op("pe", lambda e, pt=pt, kc=kc, q=q, rows=rows: e.transpose(out=pt[:, kc * 128:kc * 128 + rows], in_=pst[q][0:rows, kc * 128:(kc + 1) * 128], identity=ident_f[0:rows, 0:rows]),
                         reads=[pstB[q], idfB], writes=[pbk], signal=(kc == 1))
                copy_op(evac_eng(), pT[:, :, lc:lc + rows], pt[:, 0:256].rearrange("p (a b) -> p a b", a=2)[:, :, 0:rows], [pbk], [pTB])
            for m in range(NCH):
                q = m % 2
                pg, pgB, _ = bank()
                for k in range(NCH):
                    S.op("pe", lambda e, pg=pg, k=k, m=m: e.matmul(pg[:, 0:N], lhsT=wg[:, k, m * 128:(m + 1) * 128], rhs=xm[:, k, 0:N], start=(k == 0), stop=(k == 7)),
                         reads=[wgB, xmB], writes=[pgB], signal=(k == 7))
                pp, ppB, _ = bank()
                for k in range(2):
                    S.op("pe", lambda e, pp=pp, k=k, m=m: e.matmul(pp[:, 0:N], lhsT=wpr[:, k, m * 128:(m + 1) * 128], rhs=pT[:, k, 0:N], start=(k == 0), stop=(k == 1)),
                         reads=[wprB, pTB], writes=[ppB], signal=(k == 1))
                S.op("act", lambda e, pg=pg, q=q: e.activation(out=sg[q][:, 0:N], in_=pg[:, 0:N], func=AF.Sigmoid), reads=[pgB], writes=[sgB[q]])
                S.op("dve", lambda e, pp=pp, q=q: e.tensor_tensor(out=tmpb[q][:, 0:N], in0=sg[q][:, 0:N], in1=pp[:, 0:N], op=ALU.mult),
                     reads=[sgB[q], ppB], writes=[tmpB[q]])
                S.op("dve", lambda e, q=q, m=m: e.tensor_tensor(out=hT[:, m, c0:c0 + N], in0=hT[:, m, c0:c0 + N], in1=tmpb[q][:, 0:N], op=ALU.add),
                     reads=[tmpB[q], hB[(m, b)]], writes=[hB[(m, b)]])

        SBS = [
            [(0, 512, "p"), (512, 512, "p"), (1024, 256, "p")],
            [(0, 512, "p"), (512, 512, "p"), (1024, 64, "s")],
        ]
        import os
        STAGE = int(os.environ.get("KSTAGE", "9"))
        if STAGE >= 3:
            load_mixer(0)
        for sbi in range(2 if STAGE >= 5 else (1 if STAGE >= 2 else 0)):
            blocks = SBS[sbi]
            load_x(sbi, blocks)
            KLB = int(os.environ.get("KLB", "4"))
            KREP = int(os.environ.get("KREP", "0"))
            for i in (list(range(4 if STAGE >= 4 else (1 if STAGE >= 3 else 0))) if sbi == 0 else list(range(KLB)) + [1] * KREP):
                load_wup(i, 0)
                load_wup(i, 1)
                load_wdn(i, 0)
                if i == 2:
                    load_w(wk, wkd.rearrange("k p n -> p k n"), wkB, "wk")
                    load_w(wv, wvd.rearrange("k p n -> p k n"), wvB, "wv")
                    for b, blk in enumerate(blocks):
                        last_prompt = (b == 1)
                        if blk[2] == "s" and "skv" in SKIP:
                            continue
                        kv_block(sbi, blk, b, last_prompt)
                if sbi == 1:
                    load_cprev(i)
                for b, blk in enumerate(blocks):
                    if i < 2:
                        pool_block(i, sbi, blk, b)
                    else:
                        attn_block(i, sbi, blk, b)
                load_ple(i)
                ffn(i, sbi, blocks)
                conv_state_out(i, sbi)
                nxt = (sbi, i + 1) if i < 3 else ((sbi + 1, 0) if sbi == 0 else None)
                if STAGE == 3 or (STAGE == 4 and i == 3):
                    nxt = None
                if sbi == 0 and i == 3 and KLB == 0:
                    nxt = None
                if sbi == 1 and i == KLB - 1:
                    nxt = None
                if nxt is not None:
                    load_mixer(nxt[1])
                for b, blk in enumerate(blocks):
                    ple_block(i, sbi, blk, b)
            store_y(sbi, blocks)
            if sbi == 0:
                S.op("act", lambda e: e.activation(out=KT[:, :, 0:128], in_=KT[:, :, TA:TA + 128], func=AF.Copy), reads=[KTB[2]], writes=[KTcB])
                S.op("act", lambda e: e.activation(out=Vt[:, 0, :], in_=Vt[:, 10, :], func=AF.Copy), reads=[VtB[2]], writes=[VtcB])
            elif KLB >= 2:
                for i in range(2):
                    for hb in range(2):
                        pt, pbk, _ = bank()
                        for cq in range(4):
                            c = hb * 4 + cq
                            S.op("pe", lambda e, pt=pt, c=c, cq=cq, i=i: e.transpose(out=pt[0:15, cq * 128:(cq + 1) * 128], in_=pcarry[:, i, c, :], identity=ident_f[:]),
                                 reads=[pcB[(i, c)], idfB], writes=[pbk], signal=(cq == 3))
                        copy_op(evac_eng(), ost[0:15, hb * 512:(hb + 1) * 512], pt[0:15, :], [pbk], [ostB])
                    S.dma("sp", [(poolp_o[i], ost[0:15, :])], reads=[ostB], sem="ost", final=True)
        S.finish()
        S.run()
    return nc


_PROG = {}


def _host_inputs(inp):
    f32 = np.float32
    x_prompt, x_sample = inp["x_prompt"], inp["x_sample"]
    p_prompt, p_sample = inp["p_prompt"], inp["p_sample"]
    perm = np.zeros(1024, np.int64)
    for cp in range(8):
        for hf in range(2):
            head = 4 * (2 * (cp // 4) + hf) + cp % 4
            perm[cp * 128 + hf * 64: cp * 128 + hf * 64 + 64] = head * 64 + np.arange(64)
    wq = np.ascontiguousarray(inp["w_q"][:, :, perm]).reshape(2, 8, 128, 1024)
    wo = np.ascontiguousarray(inp["w_o"][:, perm, :]).reshape(2, 8, 128, 1024)
    wup = np.ascontiguousarray(inp["w_up"].reshape(4, 8, 128, 2, NJ, 128).transpose(0, 4, 2, 3, 1, 5)).reshape(4, NJ, 128, 2048)
    wdn = np.ascontiguousarray(inp["w_down"]).reshape(4, NJ, 128, 1024)
    wg = np.ascontiguousarray(inp["w_ple_gate"]).reshape(4, 8, 128, 1024)
    wpr = np.ascontiguousarray(inp["w_ple_proj"]).reshape(4, 2, 128, 1024)
    wpl = np.ascontiguousarray(inp["w_pool"]).reshape(2, 4, 2, 128, 256)
    wk = np.ascontiguousarray(inp["w_k"]).reshape(8, 128, 256)
    wv = np.ascontiguousarray(inp["w_v"]).reshape(8, 128, 256)

    cols = np.zeros((128, NCOLS), f32)

    def colmajor(v):
        return np.ascontiguousarray(v.reshape(-1, 128).T)
    for i in range(4):
        cols[:, C_NM + 8 * i:C_NM + 8 * i + 8] = colmajor(inp["norm_mix"][i])
        cols[:, C_NF + 8 * i:C_NF + 8 * i + 8] = colmajor(inp["norm_ffn"][i])
        cols[:, C_NP + 8 * i:C_NP + 8 * i + 8] = colmajor(inp["norm_ple"][i])
        for tap in range(3):
            cols[:, C_CW + (i * 3 + tap) * 44:C_CW + (i * 3 + tap + 1) * 44] = colmajor(inp["conv_w"][i, tap])
        cols[:, C_CB + i * 44:C_CB + (i + 1) * 44] = colmajor(inp["conv_b"][i])
    cols[:, C_KVN:C_KVN + 8] = colmajor(inp["kv_norm"])
    for i in range(2):
        cols[:, C_PSC + 8 * i:C_PSC + 8 * i + 8] = colmajor(inp["pool_scale"][i])
        cols[:, C_QN + i] = np.tile(inp["q_norm"][i], 2)
    cols[:, C_KN] = np.tile(inp["k_norm"], 2)
    cols[:, C_EPS] = EPS

    oh = np.zeros((33, 384), f32)
    ii = np.arange(384)
    dist = ii - 127
    valid = (dist >= 0) & (dist < 128)
    bk = t5_bucket_np(np.clip(dist, 0, 127))
    oh[bk[valid], ii[valid]] = 1.0
    oh[32, ~valid] = NEG

    shared = dict(cols=cols, relb=np.ascontiguousarray(inp["rel_bias"], f32), oh=oh, sinks=np.ascontiguousarray(inp["sinks"], f32),
                  wup=wup, wdn=wdn, wq=wq, wo=wo, wg=wg, wpr=wpr, wpl=wpl, wk=wk, wv=wv)
    maps = []
    for core in range(NCORES):
        bi, ch = core // 4, core % 4
        s = ch * CHUNK
        xin = np.zeros((NROWS, D), f32)
        pin = np.zeros((4, NROWS, 256), f32)
        lo = s - HALO
        if lo >= 0:
            xin[0:NPROMPT] = x_prompt[bi, lo:s + CHUNK]
            pin[:, 0:NPROMPT] = p_prompt[:, bi, lo:s + CHUNK]
        else:
            xin[HALO:NPROMPT] = x_prompt[bi, 0:CHUNK]
            pin[:, HALO:NPROMPT] = p_prompt[:, bi, 0:CHUNK]
        sq = slice(16 * core, 16 * core + 16)
        xin[NPROMPT:] = x_sample[sq].transpose(1, 0, 2).reshape(64, D)
        pin[:, NPROMPT:] = p_sample[:, sq].transpose(0, 2, 1, 3).reshape(4, 64, 256)
        ccv = np.zeros((128, NCC), f32)
        first = (ch == 0)
        ccv[:, CC_M] = 0.0 if first else 1.0
        for g, w in enumerate(POOL_W):
            pos = np.arange(15)
            cnt = np.minimum(pos + 1, w) if first else np.full(15, w)
            ccv[:, CC_INV + 15 * g:CC_INV + 15 * g + 15] = (1.0 / cnt.astype(np.float64)).astype(f32)[None, :]
        ccv[:, CC_FM:CC_FM + 128] = NEG if first else 0.0
        m = dict(shared)
        m.update(xin=xin, pin=pin, cc=ccv,
                 spool=np.ascontiguousarray(inp["state_pool"][:, sq]),
                 sconv=np.ascontiguousarray(inp["state_conv"][:, sq]),
                 ck=np.ascontiguousarray(inp["cache_k"][sq]).reshape(16, 128, 256),
                 cv=np.ascontiguousarray(inp["cache_v"][sq]).reshape(16, 128, 256))
        maps.append(m)
    return maps


def kernel(**inputs):
    inp = {k: np.asarray(v) for k, v in inputs.items()}
    if "nc" not in _PROG:
        _PROG["nc"] = build_program()
    nc = _PROG["nc"]
    maps = _host_inputs(inp)
    res = run_bass_kernel_spmd(nc, maps, core_ids=list(range(NCORES)))
    R = res.results
    f32 = np.float32
    y_prompt = np.zeros((2, 8192, D), f32)
    y_sample = np.zeros((128, 4, D), f32)
    pool_p = np.zeros((2, 2, 15, D), f32)
    pool_s = np.zeros((2, 128, 15, D), f32)
    conv_p = np.zeros((4, 2, 2, 2 * FF), f32)
    conv_s = np.zeros((4, 128, 2, 2 * FF), f32)
    k_p = np.zeros((2, 128, 4, 64), f32)
    v_p = np.zeros((2, 128, 4, 64), f32)
    k_s = np.zeros((128, 128, 4, 64), f32)
    v_s = np.zeros((128, 128, 4, 64), f32)
    for core in range(NCORES):
        bi, ch = core // 4, core % 4
        r = R[core]
        y = np.asarray(r["y"])
        y_prompt[bi, ch * CHUNK:(ch + 1) * CHUNK] = y[HALO:NPROMPT]
        sq = slice(16 * core, 16 * core + 16)
        y_sample[sq] = y[NPROMPT:].reshape(4, 16, D).transpose(1, 0, 2)
        pool_s[:, sq] = np.asarray(r["pools"])
        conv_s[:, sq] = np.asarray(r["convs"])
        k_s[sq] = np.asarray(r["cks"]).reshape(16, 128, 4, 64)
        v_s[sq] = np.asarray(r["cvs"]).reshape(16, 128, 4, 64)
        if ch == 3:
            pool_p[:, bi] = np.asarray(r["poolp"])
            conv_p[:, bi] = np.asarray(r["convp"])
            k_p[bi] = np.asarray(r["ckp"]).reshape(128, 4, 64)
            v_p[bi] = np.asarray(r["cvp"]).reshape(128, 4, 64)
    return (y_prompt, y_sample, pool_p, pool_s, conv_p, conv_s, k_p, k_s, v_p, v_s)
```

```python
import math
import types
import numpy as np
import concourse.bass as bass
import concourse.mybir as mybir
from concourse.bass_utils import run_bass_kernel_spmd
from contextlib import ExitStack

F32 = mybir.dt.float32
BF16 = mybir.dt.bfloat16
AF = mybir.ActivationFunctionType
ALU = mybir.AluOpType
AX = mybir.AxisListType

NCORES = 8
D = 1024
NCH = 8
FF = 2816
NJ = 22
NEG = -30000.0
EPS = 1e-6
HALO = 256
CHUNK = 2048
NPROMPT = HALO + CHUNK
NSAMP = 64
NROWS = NPROMPT + NSAMP
TA = 1280
POOL_W = (2, 4, 8, 16)

C_NM, C_NF, C_NP, C_KVN, C_PSC, C_CW, C_CB, C_QN, C_KN, C_EPS = 0, 32, 64, 96, 104, 120, 648, 824, 826, 827
NCOLS = 828
CC_M, CC_INV, CC_FM, NCC = 0, 1, 61, 189

PARTS = [(0, 4), (4, 8), (8, 12), (12, 16), (16, 19), (19, 22)]


def _freeze(fn):
    if fn.__closure__ is None:
        return fn
    cells = []
    for c in fn.__closure__:
        try:
            cells.append(types.CellType(c.cell_contents))
        except ValueError:
            cells.append(c)
    return types.FunctionType(fn.__code__, fn.__globals__, fn.__name__, fn.__defaults__, tuple(cells))


class Buf:
    __slots__ = ("name", "w", "r", "alias", "excl")

    def __init__(self, name, excl=False):
        self.name = name
        self.w = None
        self.r = {}
        self.alias = []
        self.excl = excl


class Sched:
    ENG = ("pe", "act", "dve", "pool", "sp")
    SEM_LIMIT = 3500

    def __init__(self, nc, es):
        self.nc = nc
        self.es = es
        self.rec = {e: [] for e in self.ENG}
        self.sem = {e: es.enter_context(nc.semaphore("s_" + e)) for e in ("pe", "act", "dve", "pool")}
        self.cnt = {e: 0 for e in self.sem}
        self.epoch = {e: 0 for e in self.sem}
        self.key = {e: e + "#0" for e in self.sem}
        self.pending = {e: False for e in self.sem}
        self.waited = {e: {} for e in self.ENG}
        self.dsem = {}
        self.final = {}

    def _wait(self, eng, ev):
        if ev is None:
            return
        sem, val, key = ev
        if key == self.key.get(eng) and val > self.cnt[eng]:
            return
        w = self.waited[eng]
        if w.get(key, 0) >= val:
            return
        w[key] = val
        self.rec[eng].append(lambda e, sem=sem, val=val: e.wait_ge(sem, val))

    def _deps(self, eng, reads, writes):
        for b in reads:
            self._wait(eng, b.w)
            if b.excl:
                mine = self.key.get(eng, eng).split("#")[0]
                for k, ev in list(b.r.items()):
                    if k.split("#")[0] != mine:
                        self._wait(eng, ev)
        for b in writes:
            self._wait(eng, b.w)
            for ev in list(b.r.values()):
                self._wait(eng, ev)
            for a in b.alias:
                self._wait(eng, a.w)
                for ev in list(a.r.values()):
                    self._wait(eng, ev)

    def _mark(self, ev, reads, writes):
        for b in reads:
            old = b.r.get(ev[2])
            if old is None or old[1] < ev[1]:
                b.r[ev[2]] = ev
        for b in writes:
            b.w = ev
            b.r = {}
            for a in b.alias:
                a.w = None
                a.r = {}

    def op(self, eng, fn, reads=(), writes=(), signal=True):
        fn = _freeze(fn)
        if self.cnt[eng] >= self.SEM_LIMIT and not self.pending[eng]:
            self.epoch[eng] += 1
            self.sem[eng] = self.es.enter_context(self.nc.semaphore(f"s_{eng}_{self.epoch[eng]}"))
            self.cnt[eng] = 0
            self.key[eng] = f"{eng}#{self.epoch[eng]}"
        self._deps(eng, reads, writes)
        s = self.sem[eng]
        if signal:
            self.cnt[eng] += 1
            ev = (s, self.cnt[eng], self.key[eng])
            self.rec[eng].append(lambda e, fn=fn, s=s: fn(e).then_inc(s, 1))
            self.pending[eng] = False
        else:
            ev = (s, self.cnt[eng] + 1, self.key[eng])
            self.rec[eng].append(lambda e, fn=fn: fn(e))
            self.pending[eng] = True
        self._mark(ev, reads, writes)
        return ev

    def dma(self, q, pairs, reads=(), writes=(), sem=None, final=False, **kw):
        self._deps(q, reads, writes)
        if sem not in self.dsem:
            self.dsem[sem] = [self.es.enter_context(self.nc.semaphore("d_" + sem)), 0]
        ent = self.dsem[sem]
        for (o, i) in pairs:
            ent[1] += 16
            s = ent[0]
            self.rec[q].append(lambda e, o=o, i=i, s=s, kw=kw: e.dma_start(out=o, in_=i, **kw).then_inc(s, 16))
        ev = (ent[0], ent[1], "dma_" + sem)
        self._mark(ev, reads, writes)
        if final:
            self.final[sem] = ev
        return ev

    def finish(self):
        for name, ev in self.final.items():
            self._wait("sp", ev)
        for e in ("pe", "act", "dve"):
            if self.cnt[e]:
                self._wait("sp", (self.sem[e], self.cnt[e], self.key[e]))

    def run(self):
        nc = self.nc
        engs = {"pe": "tensor", "act": "scalar", "dve": "vector", "pool": "gpsimd", "sp": "sync"}
        with nc.Block() as block:
            for k, attr in engs.items():
                lst = self.rec[k]
                if not lst:
                    continue

                def body(e, lst=lst):
                    for f in lst:
                        f(e)
                getattr(block, attr)(body)


class Arena:
    def __init__(self, nc, es, name, nbytes):
        self.name = name
        self.nbytes = nbytes
        self.t = es.enter_context(nc.sbuf_tensor(name, [128, nbytes // 4], F32))
        self.regs = []

    def ap(self, off, shape, dtype):
        isz = 4 if dtype == F32 else 2
        n = 1
        for s in shape:
            n *= s
        nb = n * isz
        assert off % 4 == 0 and nb % 4 == 0 and off + nb <= self.nbytes, (self.name, off, nb)
        a = self.t[:, off // 4:(off + nb) // 4]
        if dtype != F32:
            a = a.bitcast(dtype)
        if len(shape) == 2:
            a = a.rearrange("p (a b) -> p a b", a=shape[0])
        elif len(shape) == 3:
            a = a.rearrange("p (a b c) -> p a b c", a=shape[0], b=shape[1])
        return a

    def bufs(self, name, off, nbytes, n=1):
        assert off + nbytes <= self.nbytes, (self.name, name, off, nbytes)
        new = [Buf(f"{name}{k}") for k in range(n)]
        for (lo, hi, bs) in self.regs:
            if lo < off + nbytes and off < hi:
                for b in new:
                    for o in bs:
                        b.alias.append(o)
                        o.alias.append(b)
        self.regs.append((off, off + nbytes, new))
        return new


def t5_bucket_np(d):
    n = np.maximum(d, 0)
    nf = np.maximum(n, 1).astype(np.float32)
    large = 16 + (np.log(nf / np.float32(16)) / np.float32(math.log(128 / 16)) * np.float32(16)).astype(np.int32)
    large = np.minimum(large, 31)
    return np.where(n < 16, n, large)


def build_program():
    nc = bass.Bass("TRN2", target_bir_lowering=False)

    def din(name, shape):
        return nc.dram_tensor(name, shape, F32, kind="ExternalInput").ap()

    def dout(name, shape):
        return nc.dram_tensor(name, shape, F32, kind="ExternalOutput").ap()

    xin = din("xin", [NROWS, D])
    pin = din("pin", [4, NROWS, 256])
    spool = din("spool", [2, 16, 15, D])
    sconv = din("sconv", [4, 16, 2, 2 * FF])
    ckin = din("ck", [16, 128, 256])
    cvin = din("cv", [16, 128, 256])
    colsd = din("cols", [128, NCOLS])
    ccd = din("cc", [128, NCC])
    relbd = din("relb", [32, 16])
    ohd = din("oh", [33, 384])
    sinksd = din("sinks", [2, 16])
    wupd = din("wup", [4, NJ, 128, 2048])
    wdnd = din("wdn", [4, NJ, 128, 1024])
    wqd = din("wq", [2, 8, 128, 1024])
    wod = din("wo", [2, 8, 128, 1024])
    wgd = din("wg", [4, 8, 128, 1024])
    wprd = din("wpr", [4, 2, 128, 1024])
    wpld = din("wpl", [2, 4, 2, 128, 256])
    wkd = din("wk", [8, 128, 256])
    wvd = din("wv", [8, 128, 256])

    yout = dout("y", [NROWS, D])
    poolp_o = dout("poolp", [2, 15, D])
    pools_o = dout("pools", [2, 16, 15, D])
    convp_o = dout("convp", [4, 2, 2 * FF])
    convs_o = dout("convs", [4, 16, 2, 2 * FF])
    ckp_o = dout("ckp", [128, 256])
    cvp_o = dout("cvp", [128, 256])
    cks_o = dout("cks", [16, 128, 256])
    cvs_o = dout("cvs", [16, 128, 256])
    gdt = nc.dram_tensor("gd", [16, 384], F32, kind="Internal")
    gd = gdt.ap()

    with ExitStack() as es:
        S = Sched(nc, es)

        def sb(name, shape, dt):
            return es.enter_context(nc.sbuf_tensor(name, shape, dt))

        hT = sb("hT", [128, NCH, TA], F32)
        KT = sb("KT", [128, 2, 128 + TA], BF16)
        Vt = sb("Vt", [128, 11, 256], BF16)
        Tb = sb("Tb", [128, 16, 256], F32)
        Ts = sb("Ts", [128, 4, 256], F32)
        skb = sb("skb", [128, 32], F32)
        sks = sb("sks", [128, 2, 4], F32)
        cols = sb("colst", [128, NCOLS], F32)
        cc = sb("cct", [128, NCC], F32)
        ident_f = sb("ident_f", [128, 128], F32)
        ident_b = sb("ident_b", [128, 128], BF16)
        Jm = sb("Jm", [128, 128], F32)
        ones_b = sb("ones_b", [128, 128], BF16)
        blk_b = sb("blk_b", [128, 128], BF16)
        pcarry = sb("pcarry", [128, 2, NCH, 15], F32)
        ccarry = sb("ccarry", [128, 2, 4, 2, 44], F32)
        wup = [sb(f"wup{k}", [128, 2, 8, 128], BF16) for k in range(3)]
        wdn = [sb(f"wdn{k}", [128, 4, 1024], BF16) for k in range(2)]
        xm = sb("xm", [128, NCH, 512], BF16)
        pT = sb("pT", [128, 2, 512], BF16)
        pst = [sb(f"pst{k}", [128, 256], F32) for k in range(2)]
        sqb = [sb(f"sqb{k}", [128, 512], BF16) for k in range(2)]
        cprev = sb("cprev", [128, 44, 32], F32)
        KTs = sb("KTs", [128, 2, 64], BF16)
        Vsm = sb("Vsm", [64, 256], BF16)
        Ks = [sb(f"Ks{k}", [128, 2, 256], BF16) for k in range(2)]
        Vs = [sb(f"Vs{k}", [128, 256], BF16) for k in range(2)]
        Vp = [sb(f"Vp{k}", [128, 256], BF16) for k in range(2)]
        small = sb("small", [128, 64], F32)
        relb_aug = sb("relb_aug", [33, 16], F32)
        xns = sb("xns", [128, 64], F32)
        xnsB = Buf("xns")
        ugs = [sb(f"ugs{hf}", [128, 96], F32) for hf in range(2)]
        ugsB = [Buf(f"ugs{hf}") for hf in range(2)]
        qs2 = sb("qs2", [128, 2, 16, 16], BF16)
        qs2B = Buf("qs2")

        hB = {(c, b): Buf(f"h{c}_{b}") for c in range(NCH) for b in range(3)}
        KTB = [Buf(f"KT{b}") for b in range(3)]
        KTcB = Buf("KTc")
        VtB = [Buf(f"Vt{b}") for b in range(3)]
        VtcB = Buf("Vtc")
        TbB, TsB, skbB, sksB, colsB, ccB = Buf("Tb"), Buf("Ts"), Buf("skb"), Buf("sks"), Buf("cols"), Buf("cc")
        idfB, idbB, JB, onesB, blkB = Buf("idf"), Buf("idb"), Buf("J"), Buf("ones"), Buf("blk")
        pcB = {(i, c): Buf(f"pc{i}_{c}") for i in range(2) for c in range(NCH)}
        ccB2 = {(p_, i, ch): Buf(f"ccar{p_}_{i}_{ch}") for p_ in range(2) for i in range(4) for ch in range(44)}
        wupB = [Buf(f"wup{k}") for k in range(3)]
        wdnB = [Buf(f"wdn{k}") for k in range(2)]
        xmB = Buf("xm")
        pTB = Buf("pT")
        pstB = [Buf(f"pst{k}") for k in range(2)]
        sqB = [Buf(f"sq{k}") for k in range(2)]
        cprevB = [Buf(f"cprev{ch}") for ch in range(44)]
        KTsB, VsmB = Buf("KTs"), Buf("Vsm")
        KsB = [Buf(f"Ks{k}") for k in range(2)]
        VsB = [Buf(f"Vs{k}") for k in range(2)]
        VpB = [Buf(f"Vp{k}") for k in range(2)]
        smallB = [Buf(f"small{k}") for k in range(8)]
        relbB = Buf("relb")
        gdB = Buf("gd")

        AB = Arena(nc, es, "arenaB", 32768)
        AC = Arena(nc, es, "arenaC", 20480)
        AF_ = Arena(nc, es, "arenaF", 18432)

        xnf = AB.ap(0, (NCH, TA), BF16)
        xnfB = AB.bufs("xnf", 0, NCH * TA * 2, 3)
        actT = AB.ap(20480, (4, TA), BF16)
        _actl = AB.bufs("act", 20480, 4 * TA * 2, 12)
        actB = {(jj, b): _actl[jj * 3 + b] for jj in range(4) for b in range(3)}
        wq = AB.ap(0, (8, 1024), BF16)
        wqB = AB.bufs("wq", 0, 16384)[0]
        wo = AB.ap(16384, (8, 1024), BF16)
        woB = AB.bufs("wo", 16384, 16384)[0]
        wpl = AB.ap(0, (4, 2, 256), BF16)
        wplB = AB.bufs("wpl", 0, 4096)[0]
        Hk = AB.ap(0, (16, 256), F32)
        HkB = AB.bufs("Hk", 0, 16384)[0]

        qT = AC.ap(0, (NCH, 512), BF16)
        qTB = AC.bufs("qT", 0, 8192)[0]
        OT = AC.ap(8192, (NCH, 512), BF16)
        OTB = AC.bufs("OT", 8192, 8192)[0]
        ckst = [AC.ap(16384 + 1024 * k, (256,), F32) for k in range(2)]
        ckstB = [AC.bufs(f"ckst{k}", 16384 + 1024 * k, 1024)[0] for k in range(2)]
        cvst = [AC.ap(18432 + 1024 * k, (256,), F32) for k in range(2)]
        cvstB = [AC.bufs(f"cvst{k}", 18432 + 1024 * k, 1024)[0] for k in range(2)]
        wg = AC.ap(0, (8, 1024), BF16)
        wgB = AC.bufs("wg", 0, 16384)[0]
        wpr = AC.ap(16384, (2, 1024), BF16)
        wprB = AC.bufs("wpr", 16384, 4096)[0]
        wk = AC.ap(0, (8, 256), BF16)
        wkB = AC.bufs("wk", 0, 4096)[0]
        wv = AC.ap(4096, (8, 256), BF16)
        wvB = AC.bufs("wv", 4096, 4096)[0]
        spst = [AC.ap(4096 * k, (1024,), F32) for k in range(2)]
        spstB = [AC.bufs(f"spst{k}", 4096 * k, 4096)[0] for k in range(2)]

        cbuf = [[AF_.ap((s * 2 + hf) * 2056, (514,), F32) for hf in range(2)] for s in range(3)]
        cbB = [[AF_.bufs(f"cb{s}{hf}", (s * 2 + hf) * 2056, 2056)[0] for hf in range(2)] for s in range(3)]
        xe = [AF_.ap(2108 * k, (527,), F32) for k in range(2)]
        xeB = [AF_.bufs(f"xe{k}", 2108 * k, 2108)[0] for k in range(2)]
        tab = [AF_.ap(2108 * (2 + k), (527,), F32) for k in range(2)]
        tabB = [AF_.bufs(f"tab{k}", 2108 * (2 + k), 2108)[0] for k in range(2)]
        nr = [AF_.ap(8432 + 2048 * k, (512,), F32) for k in range(2)]
        nrB = [AF_.bufs(f"nr{k}", 8432 + 2048 * k, 2048)[0] for k in range(2)]
        scb = [AF_.ap(1024 * k, (256,), F32) for k in range(8)]
        scB = [AF_.bufs(f"sc{k}", 1024 * k, 1024)[0] for k in range(8)]
        eb = [AF_.ap(8192 + 512 * k, (256,), BF16) for k in range(8)]
        ebB = [AF_.bufs(f"e{k}", 8192 + 512 * k, 512)[0] for k in range(8)]
        pTs = [AF_.ap(12528 + 512 * k, (256,), BF16) for k in range(4)]
        pTsB = [AF_.bufs(f"pTs{k}", 12528 + 512 * k, 512)[0] for k in range(4)]
        pb = [AF_.ap(14576 + 512 * k, (256,), BF16) for k in range(4)]
        pbB = [AF_.bufs(f"p{k}", 14576 + 512 * k, 512)[0] for k in range(4)]
        knf = AF_.ap(5120, (512,), F32)
        knfB = AF_.bufs("knf", 5120, 2048)[0]
        sg = [AF_.ap(2048 * k, (512,), F32) for k in range(2)]
        sgB = [AF_.bufs(f"sg{k}", 2048 * k, 2048)[0] for k in range(2)]
        tmpb = [AF_.ap(4096 + 2048 * k, (512,), F32) for k in range(2)]
        tmpB = [AF_.bufs(f"tmp{k}", 4096 + 2048 * k, 2048)[0] for k in range(2)]
        xst = [AF_.ap(4096 * k, (1024,), F32) for k in range(2)]
        xstB = [AF_.bufs(f"xst{k}", 4096 * k, 4096)[0] for k in range(2)]
        scst = AF_.ap(12528, (1408,), F32)
        scstB = AF_.bufs("scst", 12528, 5632)[0]
        ost = AF_.ap(12528, (1024,), F32)
        ostB = AF_.bufs("ost", 12528, 4096)[0]
        gsb = AF_.ap(0, (384,), F32)
        gsbB = AF_.bufs("gsb", 0, 1536)[0]
        oht = AF_.ap(2048, (384,), F32)
        ohB = AF_.bufs("oht", 2048, 1536)[0]

        pst_t = [es.enter_context(nc.psum_tensor(f"ps{k}", [128, 512], F32)) for k in range(8)]
        psB = [Buf(f"ps{k}", excl=True) for k in range(8)]
        free = list(range(8))

        def bank(hold=False):
            k = free.pop(0)
            if not hold:
                free.append(k)
            return pst_t[k], psB[k], k

        def release(k):
            free.append(k)

        flip = [0]

        def evac_eng():
            flip[0] ^= 1
            return "act" if flip[0] else "dve"

        def copy_op(eng, out, in_, reads, writes):
            if eng == "act":
                S.op("act", lambda e: e.activation(out=out, in_=in_, func=AF.Copy), reads=reads, writes=writes)
            else:
                S.op("dve", lambda e: e.tensor_copy(out=out, in_=in_), reads=reads, writes=writes)

        def col(k):
            return cols[:, k:k + 1]

        S.dma("sp", [(cols[:], colsd)], writes=[colsB], sem="cols")
        S.dma("sp", [(cc[:], ccd)], writes=[ccB], sem="cc")
        S.op("pool", lambda e: e.memset(ident_f[:], 0.0), writes=[idfB])
        S.op("pool", lambda e: e.affine_select(out=ident_f[:], in_=ident_f[:], compare_op=ALU.not_equal, fill=1.0,
                                                base=0, pattern=[[-1, 128]], channel_multiplier=1),
             reads=[idfB], writes=[idfB])
        S.op("pool", lambda e: e.memset(Jm[:], 0.0), writes=[JB])
        S.op("pool", lambda e: e.affine_select(out=Jm[:], in_=Jm[:], compare_op=ALU.not_equal, fill=1.0,
                                                base=-127, pattern=[[1, 128]], channel_multiplier=1),
             reads=[JB], writes=[JB])
        S.op("dve", lambda e: e.tensor_copy(out=ident_b[:], in_=ident_f[:]), reads=[idfB], writes=[idbB])
        S.op("dve", lambda e: e.memset(ones_b[:], 1.0), writes=[onesB])
        S.op("dve", lambda e: e.memset(blk_b[:], 0.0), writes=[blkB])
        S.op("dve", lambda e: e.memset(blk_b[0:64, 0:64], 1.0), writes=[blkB])
        S.op("dve", lambda e: e.memset(blk_b[64:128, 64:128], 1.0), writes=[blkB])
        S.op("dve", lambda e: e.memset(KT[:, :, 0:128], 0.0), writes=[KTcB])
        S.op("dve", lambda e: e.memset(Vt[:, 0, :], 0.0), writes=[VtcB])
        S.op("dve", lambda e: e.memset(pcarry[:], 0.0), writes=list(pcB.values()))
        S.op("dve", lambda e: e.memset(ccarry[:], 0.0), writes=list(ccB2.values()))
        for k in range(2):
            S.op("dve", lambda e, k=k: e.memset(Ks[k][:], 0.0), writes=[KsB[k]])
            S.op("dve", lambda e, k=k: e.memset(Vp[k][:], 0.0), writes=[VpB[k]])
        S.op("dve", lambda e: e.memset(Ts[:], 0.0), writes=[TsB])
        S.op("dve", lambda e: e.memset(sks[:], 0.0), writes=[sksB])
        S.op("dve", lambda e: e.memset(small[:], 0.0), writes=smallB)

        S.op("dve", lambda e: e.memset(relb_aug[:], 1.0), writes=[relbB])
        S.dma("sp", [(relb_aug[0:32, :], relbd)], writes=[relbB], sem="relb")
        S.dma("sp", [(oht[0:33, :], ohd)], writes=[ohB], sem="oh")
        pt, pbk, _ = bank()
        S.op("pe", lambda e: e.matmul(pt[0:16, 0:384], lhsT=relb_aug[0:33, 0:16], rhs=oht[0:33, 0:384], start=True, stop=True),
             reads=[relbB, ohB], writes=[pbk])
        S.op("act", lambda e: e.activation(out=gsb[0:16, :], in_=pt[0:16, 0:384], func=AF.Copy), reads=[pbk], writes=[gsbB])
        S.dma("sp", [(gd, gsb[0:16, :])], reads=[gsbB], writes=[gdB], sem="gd")
        S.dma("sp", [(Hk, bass.AP(gdt, 0, [[1, 128], [384, 16], [1, 256]]))], reads=[gdB], writes=[HkB], sem="hk")
        for h in range(16):
            pt, pbk, _ = bank()
            S.op("pe", lambda e, pt=pt, h=h: e.matmul(pt[:, 128:256], lhsT=Hk[:, h, 0:128], rhs=Jm[:], start=True, stop=True),
                 reads=[HkB, JB], writes=[pbk], signal=False)
            S.op("pe", lambda e, pt=pt, h=h: e.matmul(pt[:, 0:128], lhsT=Hk[:, h, 128:256], rhs=Jm[:], start=True, stop=True),
                 reads=[HkB, JB], writes=[pbk])
            copy_op(evac_eng(), Tb[:, h, :], pt[:, 0:256], [pbk], [TbB])
        for kv in range(4):
            for g in range(4):
                S.dma("sp", [(Ts[4 * g:4 * g + 4, kv, :], Tb[0:4, 4 * kv + g, :])], reads=[TbB], writes=[TsB], sem="ts")
        S.dma("sp", [(skb[:], sinksd.rearrange("a b -> (a b)").partition_broadcast(128))], writes=[skbB], sem="skb")
        for j in range(2):
            for kv in range(4):
                for g in range(4):
                    S.dma("sp", [(sks[4 * g:4 * g + 4, j, kv:kv + 1],
                                  sinksd[j, 4 * kv + g:4 * kv + g + 1].partition_broadcast(4))],
                          writes=[sksB], sem="sks")

        def load_w(dst, src, buf, sem):
            S.dma("pool", [(dst, src)], writes=[buf], sem=sem)

        def load_wup(i, j):
            k = j % 3
            load_w(wup[k][:].rearrange("p h k c -> p (h k c)"), wupd[i, j], wupB[k], f"wup{k}")

        def load_wdn(i, pi):
            j0, j1 = PARTS[pi]
            k = pi % 2
            load_w(wdn[k][:, 0:j1 - j0, :], wdnd[i, j0:j1].rearrange("j p n -> p j n"), wdnB[k], f"wdn{k}")

        def load_mixer(i):
            if i < 2:
                load_w(wpl, wpld[i].rearrange("g k p n -> p g k n"), wplB, "wpl")
            else:
                load_w(wq, wqd[i - 2].rearrange("k p n -> p k n"), wqB, "wq")
                load_w(wo, wod[i - 2].rearrange("k p n -> p k n"), woB, "wo")

        def load_ple(i):
            load_w(wg, wgd[i].rearrange("k p n -> p k n"), wgB, "wg")
            load_w(wpr, wprd[i].rearrange("k p n -> p k n"), wprB, "wpr")

        import os
        SKIP = os.environ.get("KSKIP", "").split(",")

        def hreads(b):
            return [hB[(c, b)] for c in range(NCH)]

        def rstd_of(c0, N, b, scale, lhs, lhsB, src_fn, src_reads, nk, k):
            pt, pbk, _ = bank()
            for c in range(nk):
                q = c % 2
                S.op("act", lambda e, c=c, q=q: e.activation(out=sqb[q][:, 0:N], in_=src_fn(c), func=AF.Square),
                     reads=src_reads(c), writes=[sqB[q]])
                S.op("pe", lambda e, c=c, q=q, pt=pt: e.matmul(pt[:, 0:N], lhsT=lhs, rhs=sqb[q][:, 0:N], start=(c == 0), stop=(c == nk - 1)),
                     reads=[sqB[q], lhsB], writes=[pbk], signal=True)
            S.op("act", lambda e, pt=pt: e.activation(out=nr[k][:, 0:N], in_=pt[:, 0:N], func=AF.Ln, bias=col(C_EPS), scale=scale),
                 reads=[pbk, colsB], writes=[nrB[k]])
            S.op("act", lambda e: e.activation(out=nr[k][:, 0:N], in_=nr[k][:, 0:N], func=AF.Exp, scale=-0.5), reads=[nrB[k]], writes=[nrB[k]])

        def rms_block(blk, b, gbase, out, outB, k=0):
            c0, N, kind = blk
            rstd_of(c0, N, b, 1.0 / D, ones_b[:], onesB, lambda c: hT[:, c, c0:c0 + N], lambda c: [hB[(c, b)]], NCH, k)
            for c in range(NCH):
                S.op("dve", lambda e, c=c: e.scalar_tensor_tensor(out=out(c), in0=hT[:, c, c0:c0 + N], scalar=col(gbase + c),
                                                                  in1=nr[k][:, 0:N], op0=ALU.mult, op1=ALU.mult),
                     reads=[hB[(c, b)], nrB[k], colsB], writes=[outB])

        def load_x(sbi, blocks):
            row0 = 0 if sbi == 0 else TA
            tiles = []
            for (c0, N, kind) in blocks:
                if kind == "p":
                    for t in range(N // 128):
                        tiles.append((row0 + c0 + 128 * t, c0 + 128 * t, 128))
                else:
                    tiles.append((NPROMPT, c0, 64))
            for ti, (r0, cc0, rows) in enumerate(tiles):
                q = ti % 2
                b = cc0 // 512
                S.dma("sp", [(xst[q][0:rows, :], xin[r0:r0 + rows, :])], writes=[xstB[q]], sem=f"xst{q}")
                for hb in range(2):
                    pt, pbk, _ = bank()
                    for cq in range(4):
                        c = hb * 4 + cq
                        S.op("pe", lambda e, pt=pt, c=c, cq=cq, q=q, rows=rows: e.transpose(
                            out=pt[:, cq * 128:cq * 128 + rows], in_=xst[q][0:rows, c * 128:(c + 1) * 128], identity=ident_f[0:rows, 0:rows]),
                            reads=[xstB[q], idfB], writes=[pbk], signal=(cq == 3))
                    src = pt[:, :].rearrange("p (a b) -> p a b", a=4)[:, :, 0:rows]
                    copy_op(evac_eng(), hT[:, hb * 4:hb * 4 + 4, cc0:cc0 + rows], src, [pbk], [hB[(c, b)] for c in range(hb * 4, hb * 4 + 4)])

        def store_y(sbi, blocks):
            row0 = 0 if sbi == 0 else TA
            tiles = []
            for (c0, N, kind) in blocks:
                if kind == "p":
                    for t in range(N // 128):
                        tiles.append((row0 + c0 + 128 * t, c0 + 128 * t, 128))
                else:
                    tiles.append((NPROMPT, c0, 64))
            for ti, (r0, cc0, rows) in enumerate(tiles):
                q = ti % 2
                b = cc0 // 512
                for hb in range(2):
                    pt, pbk, _ = bank()
                    for cq in range(4):
                        c = hb * 4 + cq
                        S.op("pe", lambda e, pt=pt, c=c, cq=cq, rows=rows, cc0=cc0: e.transpose(
                            out=pt[0:rows, cq * 128:(cq + 1) * 128], in_=hT[:, c, cc0:cc0 + rows], identity=ident_f[:]),
                            reads=[hB[(c, b)], idfB], writes=[pbk], signal=(cq == 3))
                    copy_op(evac_eng(), xst[q][0:rows, hb * 512:(hb + 1) * 512], pt[0:rows, :], [pbk], [xstB[q]])
                S.dma("sp", [(yout[r0:r0 + rows, :], xst[q][0:rows, :])], reads=[xstB[q]], sem=f"xst{q}", final=True)

        def pool_block(i, sbi, blk, b):
            c0, N, kind = blk
            rstd_of(c0, N, b, 1.0 / D, ones_b[:], onesB, lambda c: hT[:, c, c0:c0 + N], lambda c: [hB[(c, b)]], NCH, 0)
            first = (sbi == 0 and b == 0)
            if kind == "s":
                for hf in range(2):
                    S.dma("sp", [(spst[hf][0:120, :], spool[i, 8 * hf:8 * hf + 8].rearrange("s r d -> (s r) d"))],
                          writes=[spstB[hf]], sem=f"spst{hf}")
                for hf in range(2):
                    S.dma("sp", [(pools_o[i, 8 * hf + s_, 0:11, :], spst[hf][15 * s_ + 4:15 * s_ + 15, :]) for s_ in range(8)],
                          reads=[spstB[hf]], sem=f"spo{hf}", final=True)
                hold = [bank(hold=True), bank(hold=True)]
            for c in range(NCH):
                g = c // 2
                w = POOL_W[g]
                q = c % 2
                X = xe[q]
                if kind == "p":
                    L = 15 + N
                    S.op("act", lambda e, X=X, c=c: e.activation(out=X[:, 0:15], in_=pcarry[:, i, c, :], func=AF.Copy),
                         reads=[pcB[(i, c)]], writes=[xeB[q]])
                    S.op("dve", lambda e, X=X, c=c: e.scalar_tensor_tensor(out=X[:, 15:15 + N], in0=hT[:, c, c0:c0 + N], scalar=col(C_NM + i * 8 + c),
                                                                           in1=nr[0][:, 0:N], op0=ALU.mult, op1=ALU.mult),
                         reads=[hB[(c, b)], nrB[0], colsB], writes=[xeB[q]])
                    if first:
                        S.op("dve", lambda e, X=X: e.tensor_scalar(out=X[:, 15 + 241:15 + 256], in0=X[:, 15 + 241:15 + 256],
                                                                   scalar1=cc[:, CC_M:CC_M + 1], scalar2=None, op0=ALU.mult),
                             reads=[xeB[q], ccB], writes=[xeB[q]])
                    S.op("act", lambda e, X=X, c=c: e.activation(out=pcarry[:, i, c, :], in_=X[:, N:N + 15], func=AF.Copy),
                         reads=[xeB[q]], writes=[pcB[(i, c)]])
                    xnew = X[:, 15:15 + N]
                    dout_ap = xm[:, c, 0:N]
                else:
                    L = 304
                    X3 = X[:, 0:304].rearrange("p (s k) -> p s k", k=19)
                    pt, pbk, _ = bank()
                    for hf in range(2):
                        S.op("pe", lambda e, pt=pt, hf=hf, c=c: e.transpose(out=pt[:, hf * 120:hf * 120 + 120], in_=spst[hf][0:120, c * 128:(c + 1) * 128],
                                                                            identity=ident_f[0:120, 0:120]),
                             reads=[spstB[hf], idfB], writes=[pbk], signal=(hf == 1))
                    copy_op("act", X3[:, :, 0:15], pt[:, 0:240].rearrange("p (s k) -> p s k", k=15), [pbk], [xeB[q]])
                    hv = hT[:, c, c0:c0 + 64].rearrange("p (t s) -> p s t", s=16)
                    nv = nr[0][:, 0:64].rearrange("p (t s) -> p s t", s=16)
                    S.op("dve", lambda e, c=c: e.scalar_tensor_tensor(out=xns[:, :], in0=hT[:, c, c0:c0 + 64], scalar=col(C_NM + i * 8 + c),
                                                                      in1=nr[0][:, 0:64], op0=ALU.mult, op1=ALU.mult),
                         reads=[hB[(c, b)], nrB[0], colsB], writes=[xnsB])
                    S.op("act", lambda e, X3=X3: e.activation(out=X3[:, :, 15:19], in_=xns[:, :].rearrange("p (t s) -> p s t", s=16), func=AF.Copy),
                         reads=[xnsB], writes=[xeB[q]])
                    hp, hpb, _ = hold[c // 4]
                    S.op("pe", lambda e, hp=hp, c=c: e.transpose(out=hp[0:64, (c % 4) * 128:(c % 4 + 1) * 128], in_=xns[:, :], identity=ident_f[:]),
                         reads=[xnsB, idfB], writes=[hpb])
                    xnew = X3[:, :, 15:19]
                    dout_ap = xm[:, c, 0:64].rearrange("p (t s) -> p s t", s=16)
                a, aB = X, xeB[q]
                lo = 0
                for si, s in enumerate((1, 2, 4, 8)[:g + 1]):
                    lo2 = lo + s
                    tb_, tbB_ = tab[si % 2], tabB[si % 2]
                    S.op("dve", lambda e, a=a, tb_=tb_, lo2=lo2, s=s, L=L: e.tensor_tensor(out=tb_[:, lo2:L], in0=a[:, lo2:L], in1=a[:, lo2 - s:L - s], op=ALU.add),
                         reads=[aB], writes=[tbB_])
                    a, aB, lo = tb_, tbB_, lo2
                if kind == "p":
                    asum = a[:, 15:15 + N]
                else:
                    asum = a[:, 0:304].rearrange("p (s k) -> p s k", k=19)[:, :, 15:19]
                S.op("dve", lambda e, asum=asum, xnew=xnew, dout_ap=dout_ap, w=w: e.scalar_tensor_tensor(out=dout_ap, in0=asum, scalar=1.0 / w, in1=xnew,
                                                                                                      op0=ALU.mult, op1=ALU.subtract),
                     reads=[aB, xeB[q]], writes=[xmB])
                if first:
                    t15 = tab[(g + 1) % 2][:, 0:15]
                    S.op("dve", lambda e, a=a, t15=t15, g=g: e.tensor_tensor(out=t15, in0=a[:, 15 + 256:15 + 271], in1=cc[:, CC_INV + 15 * g:CC_INV + 15 * g + 15], op=ALU.mult),
                         reads=[aB, ccB], writes=[tabB[(g + 1) % 2]])
                    S.op("dve", lambda e, X=X, t15=t15, c=c: e.tensor_tensor(out=xm[:, c, 256:271], in0=t15, in1=X[:, 15 + 256:15 + 271], op=ALU.subtract),
                         reads=[tabB[(g + 1) % 2], xeB[q]], writes=[xmB])
            if kind == "s":
                for hb in range(2):
                    hp, hpb, hk = hold[hb]
                    copy_op(evac_eng(), ost[0:64, hb * 512:(hb + 1) * 512], hp[0:64, :], [hpb], [ostB])
                    release(hk)
                S.dma("sp", [(pools_o[i, :, 11 + t, :], ost[16 * t:16 * t + 16, :]) for t in range(4)], reads=[ostB], sem="ost", final=True)
            M = N
            for g in range(4):
                for mo in range(2):
                    pt, pbk, _ = bank()
                    for kc in range(2):
                        S.op("pe", lambda e, pt=pt, g=g, mo=mo, kc=kc: e.matmul(pt[:, 0:M], lhsT=wpl[:, g, kc, mo * 128:(mo + 1) * 128], rhs=xm[:, 2 * g + kc, 0:M],
                                                                               start=(kc == 0), stop=(kc == 1)),
                             reads=[wplB, xmB], writes=[pbk], signal=(kc == 1))
                    m = 2 * g + mo
                    S.op("dve", lambda e, pt=pt, m=m: e.scalar_tensor_tensor(out=hT[:, m, c0:c0 + M], in0=pt[:, 0:M], scalar=col(C_PSC + i * 8 + m),
                                                                            in1=hT[:, m, c0:c0 + M], op0=ALU.mult, op1=ALU.add),
                         reads=[pbk, hB[(m, b)], colsB], writes=[hB[(m, b)]])

        def qk_norm(pt, pbk, N, gcol, out_bf, out_bf_B, out_f32=None, out_f32_B=None, k=1):
            rstd_of(0, N, 0, 1.0 / 64, blk_b[:], blkB, lambda c: pt[:, 0:N], lambda c: [pbk], 1, k)
            S.op("dve", lambda e: e.scalar_tensor_tensor(out=out_bf, in0=pt[:, 0:N], scalar=col(gcol), in1=nr[k][:, 0:N], op0=ALU.mult, op1=ALU.mult),
                 reads=[pbk, nrB[k], colsB], writes=[out_bf_B])
            if out_f32 is not None:
                S.op("dve", lambda e: e.scalar_tensor_tensor(out=out_f32, in0=pt[:, 0:N], scalar=col(gcol), in1=nr[k][:, 0:N], op0=ALU.mult, op1=ALU.mult),
                     reads=[pbk, nrB[k], colsB], writes=[out_f32_B])

        def kv_block(sbi, blk, b, last_prompt):
            c0, N, kind = blk
            rms_block(blk, b, C_KVN, lambda c: xm[:, c, 0:N], xmB, k=0)
            want_cache = (sbi == 1) and (kind == "s" or last_prompt) and ("kvout" not in SKIP)
            for kp in range(2):
                if kind == "s" and "skv_k" in SKIP:
                    continue
                pt, pbk, _ = bank()
                for k in range(NCH):
                    S.op("pe", lambda e, pt=pt, k=k, kp=kp: e.matmul(pt[:, 0:N], lhsT=wk[:, k, kp * 128:(kp + 1) * 128], rhs=xm[:, k, 0:N], start=(k == 0), stop=(k == 7)),
                         reads=[wkB, xmB], writes=[pbk], signal=(k == 7))
                if kind == "p":
                    qk_norm(pt, pbk, N, C_KN, KT[:, kp, 128 + c0:128 + c0 + N], KTB[b],
                            knf[:, 0:N] if want_cache else None, knfB)
                else:
                    qk_norm(pt, pbk, N, C_KN, KTs[:, kp, 0:64], KTsB, knf[:, 0:64], knfB)
                if want_cache:
                    rows = 128 if kind == "p" else 64
                    p2, p2b, _ = bank()
                    src = knf[:, N - 128:N] if kind == "p" else knf[:, 0:64]
                    S.op("pe", lambda e, p2=p2, src=src, rows=rows: e.transpose(out=p2[0:rows, 0:128], in_=src, identity=ident_f[:]),
                         reads=[knfB, idfB], writes=[p2b])
                    copy_op(evac_eng(), ost[0:rows, kp * 128:(kp + 1) * 128], p2[0:rows, 0:128], [p2b], [ostB])
            if want_cache:
                if kind == "p":
                    S.dma("sp", [(ckp_o, ost[:, 0:256])], reads=[ostB], sem="ost", final=True)
                else:
                    S.dma("sp", [(cks_o[:, 124 + t, :], ost[16 * t:16 * t + 16, 0:256]) for t in range(4)], reads=[ostB], sem="ost", final=True)
            if kind == "p":
                for t in range(N // 128):
                    pt, pbk, _ = bank()
                    for k in range(NCH):
                        S.op("pe", lambda e, pt=pt, k=k, t=t: e.matmul(pt[:, 0:256], lhsT=xm[:, k, 128 * t:128 * t + 128], rhs=wv[:, k, :], start=(k == 0), stop=(k == 7)),
                             reads=[wvB, xmB], writes=[pbk], signal=(k == 7))
                    gt = c0 // 128 + t
                    copy_op(evac_eng(), Vt[:, 1 + gt, :], pt[:, 0:256], [pbk], [VtB[b]])
                    if want_cache and t == N // 128 - 1:
                        copy_op("act", ost[:, 256:512], pt[:, 0:256], [pbk], [ostB])
                        S.dma("sp", [(cvp_o, ost[:, 256:512])], reads=[ostB], sem="ost", final=True)
            elif "skv_v" not in SKIP:
                pt, pbk, _ = bank()
                for k in range(NCH):
                    S.op("pe", lambda e, pt=pt, k=k: e.matmul(pt[0:64, 0:256], lhsT=xm[:, k, 0:64], rhs=wv[:, k, :], start=(k == 0), stop=(k == 7)),
                         reads=[wvB, xmB], writes=[pbk], signal=(k == 7))
                copy_op("dve", Vsm[:, :], pt[0:64, 0:256], [pbk], [VsmB])
                if "skv_o" not in SKIP:
                    copy_op("act", ost[0:64, 256:512], pt[0:64, 0:256], [pbk], [ostB])
                    S.dma("sp", [(cvs_o[:, 124 + t, :], ost[16 * t:16 * t + 16, 256:512]) for t in range(4)], reads=[ostB], sem="ost", final=True)

        sm_ctr = [0]

        def attn_stage_a(items):
            gsl = (sm_ctr[0] // 4) % 2
            for idx, it in enumerate(items):
                it["k"] = sm_ctr[0] % 8
                it["k4"] = sm_ctr[0] % 4
                sm_ctr[0] += 1
                it["sm"] = small[0:it["P"], 32 * gsl + 8 * idx:32 * gsl + 8 * idx + 8]
                it["smB"] = smallB[4 * gsl + idx]
                it["gsm"] = small[0:it["P"], 32 * gsl:32 * gsl + 32].rearrange("p (i c) -> p i c", c=8)
            for it in items:
                k, P = it["k"], it["P"]
                S.op("dve", lambda e, it=it, k=k, P=P: e.scalar_tensor_tensor(out=scb[k][0:P, :], in0=it["pss"], scalar=0.125, in1=it["bias"], op0=ALU.mult, op1=ALU.add),
                     reads=[it["pssB"], it["biasB"]], writes=[scB[k]])
                if it["mask"]:
                    S.op("dve", lambda e, k=k: e.tensor_tensor(out=scb[k][:, 0:128], in0=scb[k][:, 0:128], in1=cc[:, CC_FM:CC_FM + 128], op=ALU.add),
                         reads=[scB[k], ccB], writes=[scB[k]])
            for it in items:
                k, P, sm, smB = it["k"], it["P"], it["sm"], it["smB"]
                S.op("dve", lambda e, k=k, P=P, sm=sm: e.reduce_max(out=sm[:, 0:1], in_=scb[k][0:P, :], axis=AX.X), reads=[scB[k]], writes=[smB])
                S.op("dve", lambda e, it=it, sm=sm: e.tensor_scalar(out=sm[:, 1:2], in0=sm[:, 0:1], scalar1=it["sink"], scalar2=-1.0, op0=ALU.max, op1=ALU.mult),
                     reads=[smB, it["sinkB"]], writes=[smB])
            for it in items:
                k, P, sm, smB = it["k"], it["P"], it["sm"], it["smB"]
                S.op("act", lambda e, k=k, P=P, sm=sm: e.activation(out=eb[k][0:P, :], in_=scb[k][0:P, :], func=AF.Exp, bias=sm[:, 1:2], accum_out=sm[:, 2:3]),
                     reads=[scB[k], smB], writes=[ebB[k], smB])
                S.op("act", lambda e, it=it, sm=sm: e.activation(out=sm[:, 3:4], in_=it["sink"], func=AF.Exp, bias=sm[:, 1:2]),
                     reads=[smB, it["sinkB"]], writes=[smB])

        def attn_stage_b(items):
            ptt, pttB, _ = bank()
            ptb = ptt[:, :].bitcast(BF16)
            g0 = items[0]
            gsm, gB = g0["gsm"], [it["smB"] for it in items]
            S.op("dve", lambda e, gsm=gsm: e.tensor_tensor(out=gsm[:, :, 4], in0=gsm[:, :, 2], in1=gsm[:, :, 3], op=ALU.add), reads=gB, writes=gB)
            S.op("dve", lambda e, gsm=gsm: e.reciprocal(out=gsm[:, :, 5], in_=gsm[:, :, 4]), reads=gB, writes=gB)
            for it in items:
                k, k4, P, sm, smB = it["k"], it["k4"], it["P"], it["sm"], it["smB"]
                S.op("dve", lambda e, k=k, k4=k4, P=P, sm=sm: e.tensor_scalar(out=pb[k4][0:P, :], in0=eb[k][0:P, :], scalar1=sm[:, 5:6], scalar2=None, op0=ALU.mult),
                     reads=[ebB[k], smB], writes=[pbB[k4]])
            for idx, it in enumerate(items):
                k4, P = it["k4"], it["P"]
                base = idx * 2 * P
                for hh in range(2):
                    S.op("pe", lambda e, k4=k4, P=P, hh=hh, base=base: e.transpose(out=ptb[:, base + hh * P:base + (hh + 1) * P], in_=pb[k4][0:P, hh * 128:(hh + 1) * 128], identity=ident_b[0:P, 0:P]),
                         reads=[pbB[k4], idbB], writes=[pttB], signal=(hh == 1))
            for idx, it in enumerate(items):
                k4, P = it["k4"], it["P"]
                base = idx * 2 * P
                copy_op("act", pTs[k4][:, 0:2 * P], ptb[:, base:base + 2 * P], [pttB], [pTsB[k4]])
            for it in items:
                k4, P = it["k4"], it["P"]
                for hh in range(2):
                    vap, vB = it["v"][hh]
                    S.op("pe", lambda e, it=it, k4=k4, P=P, hh=hh, vap=vap: e.matmul(it["po"], lhsT=vap, rhs=pTs[k4][:, hh * P:(hh + 1) * P], start=(hh == 0), stop=(hh == 1)),
                         reads=[pTsB[k4], vB], writes=[it["poB"]], signal=(hh == 1))

        pendg = [None]

        def attn_block(i, sbi, blk, b):
            j = i - 2
            c0, N, kind = blk
            rms_block(blk, b, C_NM + i * 8, lambda c: xm[:, c, 0:N], xmB, k=0)
            for cp in range(NCH):
                pt, pbk, _ = bank()
                for k in range(NCH):
                    S.op("pe", lambda e, pt=pt, k=k, cp=cp: e.matmul(pt[:, 0:N], lhsT=wq[:, k, cp * 128:(cp + 1) * 128], rhs=xm[:, k, 0:N], start=(k == 0), stop=(k == 7)),
                         reads=[wqB, xmB], writes=[pbk], signal=(k == 7))
                qk_norm(pt, pbk, N, C_QN + j, qT[:, cp, 0:N], qTB, k=1)
            if kind == "p":
                groups = [(t, cpp) for t in range(N // 128) for cpp in range(0, NCH, 2)]

                def emit_scores(t, cpp):
                    tcol = c0 + 128 * t
                    gt = tcol // 128
                    kb_prev = (KTcB, VtcB) if gt == 0 else (KTB[(tcol - 128) // 512], VtB[(tcol - 128) // 512])
                    items = []
                    for cp in (cpp, cpp + 1):
                        pss, pssB, _ = bank()
                        kp = cp // 4
                        for hf in range(2):
                            kv = 2 * kp + hf
                            head = 4 * kv + cp % 4
                            r0, r1 = 64 * hf, 64 * hf + 64
                            S.op("pe", lambda e, pss=pss, hf=hf, r0=r0, r1=r1, cp=cp, t=t, kp=kp, tcol=tcol: e.matmul(
                                pss[:, 256 * hf:256 * hf + 256], lhsT=qT[r0:r1, cp, 128 * t:128 * t + 128], rhs=KT[r0:r1, kp, tcol:tcol + 256], start=True, stop=True),
                                reads=[qTB, KTB[b], kb_prev[0]], writes=[pssB])
                            items.append(dict(pss=pss[:, 256 * hf:256 * hf + 256], pssB=pssB, bias=Tb[:, head, :], biasB=TbB,
                                              sink=skb[:, 16 * j + head:16 * j + head + 1], sinkB=skbB, mask=(sbi == 0 and gt == 2),
                                              v=[(Vt[:, gt, kv * 64:(kv + 1) * 64], kb_prev[1]), (Vt[:, gt + 1, kv * 64:(kv + 1) * 64], VtB[b])],
                                              P=128, cp=cp, rows=(r0, r1)))
                    return items

                def fin_p(items, t):
                    outs = {}
                    for it in items:
                        if it["cp"] not in outs:
                            outs[it["cp"]] = bank()
                        psO, psOB, _ = outs[it["cp"]]
                        it["po"] = psO[it["rows"][0]:it["rows"][1], 0:128]
                        it["poB"] = psOB
                    attn_stage_b(items)

                    def cp_out(outs=outs, t=t):
                        for cp, (psO, psOB, _) in outs.items():
                            copy_op("act", OT[:, cp, 128 * t:128 * t + 128], psO[:, 0:128], [psOB], [OTB])
                    return cp_out

                sc_items = [None] * len(groups)
                sc_items[0] = emit_scores(*groups[0])
                prev = None
                late = None
                for gi, (t, cpp) in enumerate(groups):
                    if gi + 1 < len(groups):
                        sc_items[gi + 1] = emit_scores(*groups[gi + 1])
                    if late is not None:
                        late()
                        late = None
                    attn_stage_a(sc_items[gi])
                    if prev is not None:
                        late = fin_p(*prev)
                    prev = (sc_items[gi], t)
                if late is not None:
                    late()
                if prev is not None:
                    fin_p(*prev)()
            elif "sattn" in SKIP:
                S.op("dve", lambda e: e.memset(OT[:, :, 0:64], 0.0), writes=[OTB])
            else:
                for cp in range(NCH):
                    S.op("dve", lambda e, cp=cp: e.tensor_copy(out=qs2[:, cp // 4, :, 4 * (cp % 4):4 * (cp % 4) + 4],
                                                             in_=qT[:, cp, 0:64].rearrange("p (t s) -> p s t", s=16)),
                         reads=[qTB], writes=[qs2B])
                for s in range(16):
                    q = s % 2
                    S.dma("sp", [(ckst[q][:, :], ckin[s])], writes=[ckstB[q]], sem=f"ckst{q}")
                    S.dma("sp", [(cvst[q][:, :], cvin[s])], writes=[cvstB[q]], sem=f"cvst{q}")
                    if j == 0:
                        S.dma("sp", [(cks_o[s, 0:124, :], ckst[q][4:128, :]), (cvs_o[s, 0:124, :], cvst[q][4:128, :])],
                              reads=[ckstB[q], cvstB[q]], sem=f"cko{q}", final=True)
                    pt, pbk, _ = bank()
                    for kp in range(2):
                        S.op("pe", lambda e, pt=pt, kp=kp, q=q: e.transpose(out=pt[:, kp * 128:(kp + 1) * 128], in_=ckst[q][:, kp * 128:(kp + 1) * 128], identity=ident_f[:]),
                             reads=[ckstB[q], idfB], writes=[pbk], signal=(kp == 1))
                    copy_op("act", Ks[q][:, :, 0:128], pt[:, 0:256].rearrange("p (a b) -> p a b", a=2), [pbk], [KsB[q]])
                    S.op("dve", lambda e, q=q, s=s: e.tensor_copy(out=Ks[q][:, :, 128:132], in_=KTs[:, :, s:64:16]), reads=[KTsB], writes=[KsB[q]])
                    copy_op("dve", Vs[q][:, :], cvst[q][:, :], [cvstB[q]], [VsB[q]])
                    S.dma("sp", [(Vp[q][t:t + 1, :], Vsm[16 * t + s:16 * t + s + 1, :]) for t in range(4)], reads=[VsmB], writes=[VpB[q]], sem=f"vp{q}")
                    items = []
                    for kp in range(2):
                        pss, pssB, _ = bank()
                        for hf in range(2):
                            kv = 2 * kp + hf
                            r0, r1 = 64 * hf, 64 * hf + 64
                            S.op("pe", lambda e, pss=pss, hf=hf, r0=r0, r1=r1, kp=kp, q=q, s=s: e.matmul(
                                pss[0:16, 256 * hf:256 * hf + 256], lhsT=qs2[r0:r1, kp, s, :], rhs=Ks[q][r0:r1, kp, :], start=True, stop=True),
                                reads=[qs2B, KsB[q]], writes=[pssB])
                            items.append(dict(pss=pss[0:16, 256 * hf:256 * hf + 256], pssB=pssB, bias=Ts[0:16, kv, :], biasB=TsB,
                                              sink=sks[0:16, j, kv:kv + 1], sinkB=sksB, mask=False,
                                              v=[(Vs[q][:, kv * 64:(kv + 1) * 64], VsB[q]), (Vp[q][:, kv * 64:(kv + 1) * 64], VpB[q])],
                                              P=16, cp=kp, rows=(r0, r1)))
                    attn_stage_a(items)
                    if pendg[0] is not None:
                        pendg[0]()

                    def fin_s(items=items, s=s):
                        outs = {}
                        for it in items:
                            if it["cp"] not in outs:
                                outs[it["cp"]] = bank()
                            psO, psOB, _ = outs[it["cp"]]
                            it["po"] = psO[it["rows"][0]:it["rows"][1], 0:16]
                            it["poB"] = psOB
                        attn_stage_b(items)
                        for kp, (psO, psOB, _) in outs.items():
                            for hf in range(2):
                                r0, r1 = 64 * hf, 64 * hf + 64
                                src = psO[r0:r1, 0:16].rearrange("p (g t) -> p g t", t=4)
                                copy_op(evac_eng(), OT[r0:r1, 4 * kp:4 * kp + 4, s:64:16], src, [psOB], [OTB])
                    pendg[0] = fin_s
                if pendg[0] is not None:
                    pendg[0]()
                    pendg[0] = None
            for m in range(NCH):
                pt, pbk, _ = bank()
                for cp in range(NCH):
                    S.op("pe", lambda e, pt=pt, cp=cp, m=m: e.matmul(pt[:, 0:N], lhsT=wo[:, cp, m * 128:(m + 1) * 128], rhs=OT[:, cp, 0:N], start=(cp == 0), stop=(cp == 7)),
                         reads=[woB, OTB], writes=[pbk], signal=(cp == 7))
                S.op("dve", lambda e, pt=pt, m=m: e.tensor_tensor(out=hT[:, m, c0:c0 + N], in0=hT[:, m, c0:c0 + N], in1=pt[:, 0:N], op=ALU.add),
                     reads=[pbk, hB[(m, b)]], writes=[hB[(m, b)]])

        def load_cprev(i):
            for pc in range(4):
                S.dma("sp", [(scst[0:32, :], sconv[i, :, :, pc * 1408:(pc + 1) * 1408].rearrange("s r n -> (s r) n"))], writes=[scstB], sem="scst")
                pt, pbk, _ = bank()
                for x in range(11):
                    S.op("pe", lambda e, pt=pt, x=x: e.transpose(out=pt[:, x * 32:(x + 1) * 32], in_=scst[0:32, x * 128:(x + 1) * 128], identity=ident_f[0:32, 0:32]),
                         reads=[scstB, idfB], writes=[pbk], signal=(x == 10))
                copy_op(evac_eng(), cprev[:, pc * 11:(pc + 1) * 11, :], pt[:, 0:352].rearrange("p (a b) -> p a b", a=11), [pbk],
                        [cprevB[ch] for ch in range(pc * 11, (pc + 1) * 11)])

        ws_ctr = [0]
        PROD_ENG = os.environ.get("KPROD", "pool")

        def ffn(i, sbi, blocks):
            for b, blk in enumerate(blocks):
                c0, N, kind = blk
                rms_block(blk, b, C_NF + i * 8, lambda c, c0=c0, N=N: xnf[:, c, c0:c0 + N], xnfB[b], k=b % 2)
            pend = [None]
            for pi, (j0, j1) in enumerate(PARTS):
                for j in range(j0, j1):
                    if j + 2 < NJ:
                        load_wup(i, j + 2)
                    wu, wuB = wup[j % 3], wupB[j % 3]
                    jj = j - j0
                    for b, (c0, N, kind) in enumerate(blocks):
                        st = ws_ctr[0] % 3
                        ws_ctr[0] += 1
                        pss = [bank(), bank()]
                        for hf in range(2):
                            pt, pbk, _ = pss[hf]
                            for k in range(NCH):
                                S.op("pe", lambda e, pt=pt, hf=hf, k=k, wu=wu, c0=c0, N=N: e.matmul(pt[:, 0:N], lhsT=wu[:, hf, k, :], rhs=xnf[:, k, c0:c0 + N], start=(k == 0), stop=(k == 7)),
                                     reads=[wuB, xnfB[b]], writes=[pbk], signal=(k == 7))
                        W = []
                        for hf in range(2):
                            ch = hf * NJ + j
                            W.append((col(C_CW + (i * 3 + 0) * 44 + ch), col(C_CW + (i * 3 + 1) * 44 + ch), col(C_CW + (i * 3 + 2) * 44 + ch), col(C_CB + i * 44 + ch), ch))
                        if kind == "p":
                            gb = sbi * 3 + b
                            rp, wp = gb % 2, (gb + 1) % 2
                            if sbi == 0 and b == 0:
                                for hf in range(2):
                                    pt, pbk, _ = pss[hf]
                                    S.op("dve", lambda e, pt=pt: e.tensor_scalar(out=pt[:, 254:256], in0=pt[:, 254:256], scalar1=cc[:, CC_M:CC_M + 1], scalar2=None, op0=ALU.mult),
                                         reads=[pbk, ccB], writes=[pbk])
                            for hf in range(2):
                                pt, pbk, _ = pss[hf]
                                w0, w1, w2, bcol, ch = W[hf]
                                Cb, CbB = cbuf[st][hf], cbB[st][hf]
                                S.op("act", lambda e, Cb=Cb, pt=pt, w2=w2, bcol=bcol, N=N: e.activation(out=Cb[:, 0:N], in_=pt[:, 0:N], func=AF.Identity, bias=bcol, scale=w2),
                                     reads=[pbk, colsB], writes=[CbB])
                                S.op("act", lambda e, pt=pt, ch=ch, N=N, wp=wp: e.activation(out=ccarry[:, wp, i, :, ch], in_=pt[:, N - 2:N], func=AF.Copy),
                                     reads=[pbk], writes=[ccB2[(wp, i, ch)]])
                            for hf in range(2):
                                pt, pbk, _ = pss[hf]
                                w0, w1, w2, bcol, ch = W[hf]
                                Cb, CbB = cbuf[st][hf], cbB[st][hf]
                                S.op("dve", lambda e, Cb=Cb, pt=pt, w1=w1, N=N: e.scalar_tensor_tensor(out=Cb[:, 1:N], in0=pt[:, 0:N - 1], scalar=w1, in1=Cb[:, 1:N], op0=ALU.mult, op1=ALU.add),
                                     reads=[pbk, CbB, colsB], writes=[CbB])
                            for hf in range(2):
                                pt, pbk, _ = pss[hf]
                                w0, w1, w2, bcol, ch = W[hf]
                                Cb, CbB = cbuf[st][hf], cbB[st][hf]
                                S.op("dve", lambda e, Cb=Cb, pt=pt, w0=w0, N=N: e.scalar_tensor_tensor(out=Cb[:, 2:N], in0=pt[:, 0:N - 2], scalar=w0, in1=Cb[:, 2:N], op0=ALU.mult, op1=ALU.add),
                                     reads=[pbk, CbB, colsB], writes=[CbB])
                            for hf in range(2):
                                pt, pbk, _ = pss[hf]
                                w0, w1, w2, bcol, ch = W[hf]
                                Cb, CbB = cbuf[st][hf], cbB[st][hf]
                                S.op("dve", lambda e, Cb=Cb, w1=w1, ch=ch, rp=rp: e.scalar_tensor_tensor(out=Cb[:, 0:1], in0=ccarry[:, rp, i, 1:2, ch], scalar=w1, in1=Cb[:, 0:1], op0=ALU.mult, op1=ALU.add),
                                     reads=[ccB2[(rp, i, ch)], CbB, colsB], writes=[CbB])
                            for hf in range(2):
                                pt, pbk, _ = pss[hf]
                                w0, w1, w2, bcol, ch = W[hf]
                                Cb, CbB = cbuf[st][hf], cbB[st][hf]
                                S.op("dve", lambda e, Cb=Cb, w0=w0, ch=ch, rp=rp: e.scalar_tensor_tensor(out=Cb[:, 0:2], in0=ccarry[:, rp, i, :, ch], scalar=w0, in1=Cb[:, 0:2], op0=ALU.mult, op1=ALU.add),
                                     reads=[ccB2[(rp, i, ch)], CbB, colsB], writes=[CbB])
                        else:
                            for hf in range(2):
                                pt, pbk, _ = pss[hf]
                                w0, w1, w2, bcol, ch = W[hf]
                                Cb, CbB = cbuf[st][hf], cbB[st][hf]
                                U3 = ugs[hf][:, 0:96].rearrange("p (s k) -> p s k", k=6)
                                UB = ugsB[hf]
                                S.op("act", lambda e, U3=U3, pt=pt: e.activation(out=U3[:, :, 2:6], in_=pt[:, 0:64].rearrange("p (t s) -> p s t", s=16), func=AF.Copy),
                                     reads=[pbk], writes=[UB])
                                S.op("dve", lambda e, U3=U3, ch=ch: e.tensor_copy(out=U3[:, :, 0:2], in_=cprev[:, ch, :].rearrange("p (s r) -> p s r", r=2)),
                                     reads=[cprevB[ch]], writes=[UB])
                                S.op("dve", lambda e, U3=U3, ch=ch: e.tensor_copy(out=cprev[:, ch, :].rearrange("p (s r) -> p s r", r=2), in_=U3[:, :, 4:6]),
                                     reads=[UB], writes=[cprevB[ch]])
                                t0, t1, t2 = U3[:, :, 0:4], U3[:, :, 1:5], U3[:, :, 2:6]
                                cv_ = Cb[:, 0:64].rearrange("p (t s) -> p s t", s=16)
                                S.op("act", lambda e, cv_=cv_, t0=t0, w0=w0, bcol=bcol: e.activation(out=cv_, in_=t0, func=AF.Identity, bias=bcol, scale=w0),
                                     reads=[UB, colsB], writes=[CbB])
                                S.op("dve", lambda e, cv_=cv_, t1=t1, w1=w1: e.scalar_tensor_tensor(out=cv_, in0=t1, scalar=w1, in1=cv_, op0=ALU.mult, op1=ALU.add),
                                     reads=[UB, CbB, colsB], writes=[CbB])
                                S.op("dve", lambda e, cv_=cv_, t2=t2, w2=w2: e.scalar_tensor_tensor(out=cv_, in0=t2, scalar=w2, in1=cv_, op0=ALU.mult, op1=ALU.add),
                                     reads=[UB, CbB, colsB], writes=[CbB])
                        if pend[0] is not None:
                            pend[0]()
                        Cg, Cv = cbuf[st][0], cbuf[st][1]

                        def fin(Cg=Cg, Cv=Cv, st=st, jj=jj, b=b, c0=c0, N=N):
                            S.op("act", lambda e: e.activation(out=Cg[:, 0:N], in_=Cg[:, 0:N], func=AF.Gelu_apprx_tanh), reads=[cbB[st][0]], writes=[cbB[st][0]])
                            S.op(PROD_ENG, lambda e: e.tensor_tensor(out=actT[:, jj, c0:c0 + N], in0=Cg[:, 0:N], in1=Cv[:, 0:N], op=ALU.mult),
                                 reads=[cbB[st][0], cbB[st][1]], writes=[actB[(jj, b)]])
                        pend[0] = fin
                if pend[0] is not None:
                    pend[0]()
                    pend[0] = None
                if pi + 1 < len(PARTS):
                    load_wdn(i, pi + 1)
                wd, wdB = wdn[pi % 2], wdnB[pi % 2]
                nj = j1 - j0
                for b, (c0, N, kind) in enumerate(blocks):
                    for m in range(NCH):
                        pt, pbk, _ = bank()
                        for jj in range(nj):
                            S.op("pe", lambda e, pt=pt, jj=jj, m=m, wd=wd, c0=c0, N=N: e.matmul(pt[:, 0:N], lhsT=wd[:, jj, m * 128:(m + 1) * 128], rhs=actT[:, jj, c0:c0 + N], start=(jj == 0), stop=(jj == nj - 1)),
                                 reads=[wdB, actB[(jj, b)]], writes=[pbk], signal=(jj == nj - 1))
                        S.op("dve", lambda e, pt=pt, m=m, c0=c0, N=N: e.tensor_tensor(out=hT[:, m, c0:c0 + N], in0=hT[:, m, c0:c0 + N], in1=pt[:, 0:N], op=ALU.add),
                             reads=[pbk, hB[(m, b)]], writes=[hB[(m, b)]])

        def conv_state_out(i, sbi):
            if sbi != 1:
                return
            pt, pbk, _ = bank()
            S.op("pe", lambda e, pt=pt: e.transpose(out=pt[0:88, 0:128], in_=ccarry[:, 1, i, :, :], identity=ident_f[:]),
                 reads=[ccB2[(1, i, ch)] for ch in range(44)] + [idfB], writes=[pbk])
            copy_op(evac_eng(), ost[0:88, 0:128], pt[0:88, 0:128], [pbk], [ostB])
            S.dma("sp", [(convp_o[i, r, :].rearrange("(j p) -> j p", p=128), ost[44 * r:44 * r + 44, 0:128]) for r in range(2)], reads=[ostB], sem="ost", final=True)
            for x in range(11):
                pt, pbk, _ = bank()
                S.op("pe", lambda e, pt=pt, x=x: e.transpose(out=pt[:, 0:128], in_=cprev[:, 4 * x:4 * x + 4, :], identity=ident_f[:]),
                     reads=[cprevB[ch] for ch in range(4 * x, 4 * x + 4)] + [idfB], writes=[pbk])
                copy_op(evac_eng(), ost[:, 0:128], pt[:, 0:128], [pbk], [ostB])
                S.dma("sp", [(convs_o[i, :, :, (4 * x + j4) * 128:(4 * x + j4 + 1) * 128].rearrange("s r p -> (s r) p"), ost[32 * j4:32 * j4 + 32, 0:128]) for j4 in range(4)],
                      reads=[ostB], sem="ost", final=True)

        def ple_block(i, sbi, blk, b):
            c0, N, kind = blk
            row0 = 0 if sbi == 0 else TA
            rms_block(blk, b, C_NP + i * 8, lambda c: xm[:, c, 0:N], xmB, k=0)
            if kind == "p":
                tiles = [(row0 + c0 + 128 * t, 128 * t, 128) for t in range(N // 128)]
            else:
                tiles = [(NPROMPT, 0, 64)]
            for ti, (r0, lc, rows) in enumerate(tiles):
                q = ti % 2
                S.dma("sp", [(pst[q][0:rows, :], pin[i, r0:r0 + rows, :])], writes=[pstB[q]], sem=f"pst{q}")
                pt, pbk, _ = bank()
                for kc in range(2):
                    S.op("pe", lambda e, pt=pt, kc=kc, q=q, rows=rows: e.transpose(out=pt[:, kc * 128:kc * 128 + rows], in_=pst[q][0:rows, kc * 128:(kc + 1) * 128], identity=ident_f[0:rows, 0:rows]),
                         reads=[pstB[q], idfB], writes=[pbk], signal=(kc == 1))
                copy_op(evac_eng(), pT[:, :, lc:lc + rows], pt[:, 0:256].rearrange("p (a b) -> p a b", a=2)[:, :, 0:rows], [pbk], [pTB])
            for m in range(NCH):
                q = m % 2
                pg, pgB, _ = bank()
                for k in range(NCH):
                    S.op("pe", lambda e, pg=pg, k=k, m=m: e.matmul(pg[:, 0:N], lhsT=wg[:, k, m * 128:(m + 1) * 128], rhs=xm[:, k, 0:N], start=(k == 0), stop=(k == 7)),
                         reads=[wgB, xmB], writes=[pgB], signal=(k == 7))
                pp, ppB, _ = bank()
                for k in range(2):
                    S.op("pe", lambda e, pp=pp, k=k, m=m: e.matmul(pp[:, 0:N], lhsT=wpr[:, k, m * 128:(m + 1) * 128], rhs=pT[:, k, 0:N], start=(k == 0), stop=(k == 1)),
                         reads=[wprB, pTB], writes=[ppB], signal=(k == 1))
                S.op("act", lambda e, pg=pg, q=q: e.activation(out=sg[q][:, 0:N], in_=pg[:, 0:N], func=AF.Sigmoid), reads=[pgB], writes=[sgB[q]])
                S.op("dve", lambda e, pp=pp, q=q: e.tensor_tensor(out=tmpb[q][:, 0:N], in0=sg[q][:, 0:N], in1=pp[:, 0:N], op=ALU.mult),
                     reads=[sgB[q], ppB], writes=[tmpB[q]])
                S.op("dve", lambda e, q=q, m=m: e.tensor_tensor(out=hT[:, m, c0:c0 + N], in0=hT[:, m, c0:c0 + N], in1=tmpb[q][:, 0:N], op=ALU.add),
                     reads=[tmpB[q], hB[(m, b)]], writes=[hB[(m, b)]])

        SBS = [
            [(0, 512, "p"), (512, 512, "p"), (1024, 256, "p")],
            [(0, 512, "p"), (512, 512, "p"), (1024, 64, "s")],
        ]
        import os
        STAGE = int(os.environ.get("KSTAGE", "9"))
        if STAGE >= 3:
            load_mixer(0)
        for sbi in range(2 if STAGE >= 5 else (1 if STAGE >= 2 else 0)):
            blocks = SBS[sbi]
            load_x(sbi, blocks)
            KLB = int(os.environ.get("KLB", "4"))
            KREP = int(os.environ.get("KREP", "0"))
            for i in (list(range(4 if STAGE >= 4 else (1 if STAGE >= 3 else 0))) if sbi == 0 else list(range(KLB)) + [1] * KREP):
                load_wup(i, 0)
                load_wup(i, 1)
                load_wdn(i, 0)
                if i == 2:
                    load_w(wk, wkd.rearrange("k p n -> p k n"), wkB, "wk")
                    load_w(wv, wvd.rearrange("k p n -> p k n"), wvB, "wv")
                    for b, blk in enumerate(blocks):
                        last_prompt = (b == 1)
                        if blk[2] == "s" and "skv" in SKIP:
                            continue
                        kv_block(sbi, blk, b, last_prompt)
                if sbi == 1:
                    load_cprev(i)
                for b, blk in enumerate(blocks):
                    if i < 2:
                        pool_block(i, sbi, blk, b)
                    else:
                        attn_block(i, sbi, blk, b)
                load_ple(i)
                ffn(i, sbi, blocks)
                conv_state_out(i, sbi)
                nxt = (sbi, i + 1) if i < 3 else ((sbi + 1, 0) if sbi == 0 else None)
                if STAGE == 3 or (STAGE == 4 and i == 3):
                    nxt = None
                if sbi == 0 and i == 3 and KLB == 0:
                    nxt = None
                if sbi == 1 and i == KLB - 1:
                    nxt = None
                if nxt is not None:
                    load_mixer(nxt[1])
                for b, blk in enumerate(blocks):
                    ple_block(i, sbi, blk, b)
            store_y(sbi, blocks)
            if sbi == 0:
                S.op("act", lambda e: e.activation(out=KT[:, :, 0:128], in_=KT[:, :, TA:TA + 128], func=AF.Copy), reads=[KTB[2]], writes=[KTcB])
                S.op("act", lambda e: e.activation(out=Vt[:, 0, :], in_=Vt[:, 10, :], func=AF.Copy), reads=[VtB[2]], writes=[VtcB])
            elif KLB >= 2:
                for i in range(2):
                    for hb in range(2):
                        pt, pbk, _ = bank()
                        for cq in range(4):
                            c = hb * 4 + cq
                            S.op("pe", lambda e, pt=pt, c=c, cq=cq, i=i: e.transpose(out=pt[0:15, cq * 128:(cq + 1) * 128], in_=pcarry[:, i, c, :], identity=ident_f[:]),
                                 reads=[pcB[(i, c)], idfB], writes=[pbk], signal=(cq == 3))
                        copy_op(evac_eng(), ost[0:15, hb * 512:(hb + 1) * 512], pt[0:15, :], [pbk], [ostB])
                    S.dma("sp", [(poolp_o[i], ost[0:15, :])], reads=[ostB], sem="ost", final=True)
        S.finish()
        S.run()
    return nc


_PROG = {}


def _host_inputs(inp):
    f32 = np.float32
    x_prompt, x_sample = inp["x_prompt"], inp["x_sample"]
    p_prompt, p_sample = inp["p_prompt"], inp["p_sample"]
    perm = np.zeros(1024, np.int64)
    for cp in range(8):
        for hf in range(2):
            head = 4 * (2 * (cp // 4) + hf) + cp % 4
            perm[cp * 128 + hf * 64: cp * 128 + hf * 64 + 64] = head * 64 + np.arange(64)
    wq = np.ascontiguousarray(inp["w_q"][:, :, perm]).reshape(2, 8, 128, 1024)
    wo = np.ascontiguousarray(inp["w_o"][:, perm, :]).reshape(2, 8, 128, 1024)
    wup = np.ascontiguousarray(inp["w_up"].reshape(4, 8, 128, 2, NJ, 128).transpose(0, 4, 2, 3, 1, 5)).reshape(4, NJ, 128, 2048)
    wdn = np.ascontiguousarray(inp["w_down"]).reshape(4, NJ, 128, 1024)
    wg = np.ascontiguousarray(inp["w_ple_gate"]).reshape(4, 8, 128, 1024)
    wpr = np.ascontiguousarray(inp["w_ple_proj"]).reshape(4, 2, 128, 1024)
    wpl = np.ascontiguousarray(inp["w_pool"]).reshape(2, 4, 2, 128, 256)
    wk = np.ascontiguousarray(inp["w_k"]).reshape(8, 128, 256)
    wv = np.ascontiguousarray(inp["w_v"]).reshape(8, 128, 256)

    cols = np.zeros((128, NCOLS), f32)

    def colmajor(v):
        return np.ascontiguousarray(v.reshape(-1, 128).T)
    for i in range(4):
        cols[:, C_NM + 8 * i:C_NM + 8 * i + 8] = colmajor(inp["norm_mix"][i])
        cols[:, C_NF + 8 * i:C_NF + 8 * i + 8] = colmajor(inp["norm_ffn"][i])
        cols[:, C_NP + 8 * i:C_NP + 8 * i + 8] = colmajor(inp["norm_ple"][i])
        for tap in range(3):
            cols[:, C_CW + (i * 3 + tap) * 44:C_CW + (i * 3 + tap + 1) * 44] = colmajor(inp["conv_w"][i, tap])
        cols[:, C_CB + i * 44:C_CB + (i + 1) * 44] = colmajor(inp["conv_b"][i])
    cols[:, C_KVN:C_KVN + 8] = colmajor(inp["kv_norm"])
    for i in range(2):
        cols[:, C_PSC + 8 * i:C_PSC + 8 * i + 8] = colmajor(inp["pool_scale"][i])
        cols[:, C_QN + i] = np.tile(inp["q_norm"][i], 2)
    cols[:, C_KN] = np.tile(inp["k_norm"], 2)
    cols[:, C_EPS] = EPS

    oh = np.zeros((33, 384), f32)
    ii = np.arange(384)
    dist = ii - 127
    valid = (dist >= 0) & (dist < 128)
    bk = t5_bucket_np(np.clip(dist, 0, 127))
    oh[bk[valid], ii[valid]] = 1.0
    oh[32, ~valid] = NEG

    shared = dict(cols=cols, relb=np.ascontiguousarray(inp["rel_bias"], f32), oh=oh, sinks=np.ascontiguousarray(inp["sinks"], f32),
                  wup=wup, wdn=wdn, wq=wq, wo=wo, wg=wg, wpr=wpr, wpl=wpl, wk=wk, wv=wv)
    maps = []
    for core in range(NCORES):
        bi, ch = core // 4, core % 4
        s = ch * CHUNK
        xin = np.zeros((NROWS, D), f32)
        pin = np.zeros((4, NROWS, 256), f32)
        lo = s - HALO
        if lo >= 0:
            xin[0:NPROMPT] = x_prompt[bi, lo:s + CHUNK]
            pin[:, 0:NPROMPT] = p_prompt[:, bi, lo:s + CHUNK]
        else:
            xin[HALO:NPROMPT] = x_prompt[bi, 0:CHUNK]
            pin[:, HALO:NPROMPT] = p_prompt[:, bi, 0:CHUNK]
        sq = slice(16 * core, 16 * core + 16)
        xin[NPROMPT:] = x_sample[sq].transpose(1, 0, 2).reshape(64, D)
        pin[:, NPROMPT:] = p_sample[:, sq].transpose(0, 2, 1, 3).reshape(4, 64, 256)
        ccv = np.zeros((128, NCC), f32)
        first = (ch == 0)
        ccv[:, CC_M] = 0.0 if first else 1.0
        for g, w in enumerate(POOL_W):
            pos = np.arange(15)
            cnt = np.minimum(pos + 1, w) if first else np.full(15, w)
            ccv[:, CC_INV + 15 * g:CC_INV + 15 * g + 15] = (1.0 / cnt.astype(np.float64)).astype(f32)[None, :]
        ccv[:, CC_FM:CC_FM + 128] = NEG if first else 0.0
        m = dict(shared)
        m.update(xin=xin, pin=pin, cc=ccv,
                 spool=np.ascontiguousarray(inp["state_pool"][:, sq]),
                 sconv=np.ascontiguousarray(inp["state_conv"][:, sq]),
                 ck=np.ascontiguousarray(inp["cache_k"][sq]).reshape(16, 128, 256),
                 cv=np.ascontiguousarray(inp["cache_v"][sq]).reshape(16, 128, 256))
        maps.append(m)
    return maps


def kernel(**inputs):
    inp = {k: np.asarray(v) for k, v in inputs.items()}
    if "nc" not in _PROG:
        _PROG["nc"] = build_program()
    nc = _PROG["nc"]
    maps = _host_inputs(inp)
    res = run_bass_kernel_spmd(nc, maps, core_ids=list(range(NCORES)))
    R = res.results
    f32 = np.float32
    y_prompt = np.zeros((2, 8192, D), f32)
    y_sample = np.zeros((128, 4, D), f32)
    pool_p = np.zeros((2, 2, 15, D), f32)
    pool_s = np.zeros((2, 128, 15, D), f32)
    conv_p = np.zeros((4, 2, 2, 2 * FF), f32)
    conv_s = np.zeros((4, 128, 2, 2 * FF), f32)
    k_p = np.zeros((2, 128, 4, 64), f32)
    v_p = np.zeros((2, 128, 4, 64), f32)
    k_s = np.zeros((128, 128, 4, 64), f32)
    v_s = np.zeros((128, 128, 4, 64), f32)
    for core in range(NCORES):
        bi, ch = core // 4, core % 4
        r = R[core]
        y = np.asarray(r["y"])
        y_prompt[bi, ch * CHUNK:(ch + 1) * CHUNK] = y[HALO:NPROMPT]
        sq = slice(16 * core, 16 * core + 16)
        y_sample[sq] = y[NPROMPT:].reshape(4, 16, D).transpose(1, 0, 2)
        pool_s[:, sq] = np.asarray(r["pools"])
        conv_s[:, sq] = np.asarray(r["convs"])
        k_s[sq] = np.asarray(r["cks"]).reshape(16, 128, 4, 64)
        v_s[sq] = np.asarray(r["cvs"]).reshape(16, 128, 4, 64)
        if ch == 3:
            pool_p[:, bi] = np.asarray(r["poolp"])
            conv_p[:, bi] = np.asarray(r["convp"])
            k_p[bi] = np.asarray(r["ckp"]).reshape(128, 4, 64)
            v_p[bi] = np.asarray(r["cvp"]).reshape(128, 4, 64)
    return (y_prompt, y_sample, pool_p, pool_s, conv_p, conv_s, k_p, k_s, v_p, v_s)
```

```python
import math
import types
import numpy as np
import concourse.bass as bass
import concourse.mybir as mybir
from concourse.bass_utils import run_bass_kernel_spmd
from contextlib import ExitStack

F32 = mybir.dt.float32
BF16 = mybir.dt.bfloat16
AF = mybir.ActivationFunctionType
ALU = mybir.AluOpType
AX = mybir.AxisListType

NCORES = 8
D = 1024
NCH = 8
FF = 2816
NJ = 22
NEG = -30000.0
EPS = 1e-6
HALO = 256
CHUNK = 2048
NPROMPT = HALO + CHUNK
NSAMP = 64
NROWS = NPROMPT + NSAMP
TA = 1280
POOL_W = (2, 4, 8, 16)

C_NM, C_NF, C_NP, C_KVN, C_PSC, C_CW, C_CB, C_QN, C_KN, C_EPS = 0, 32, 64, 96, 104, 120, 648, 824, 826, 827
NCOLS = 828
CC_M, CC_INV, CC_FM, NCC = 0, 1, 61, 189

PARTS = [(0, 4), (4, 8), (8, 12), (12, 16), (16, 19), (19, 22)]


def _freeze(fn):
    if fn.__closure__ is None:
        return fn
    cells = []
    for c in fn.__closure__:
        try:
            cells.append(types.CellType(c.cell_contents))
        except ValueError:
            cells.append(c)
    return types.FunctionType(fn.__code__, fn.__globals__, fn.__name__, fn.__defaults__, tuple(cells))


class Buf:
    __slots__ = ("name", "w", "r", "alias", "excl")

    def __init__(self, name, excl=False):
        self.name = name
        self.w = None
        self.r = {}
        self.alias = []
        self.excl = excl


class Sched:
    ENG = ("pe", "act", "dve", "pool", "sp")
    SEM_LIMIT = 3500

    def __init__(self, nc, es):
        self.nc = nc
        self.es = es
        self.rec = {e: [] for e in self.ENG}
        self.sem = {e: es.enter_context(nc.semaphore("s_" + e)) for e in ("pe", "act", "dve", "pool")}
        self.cnt = {e: 0 for e in self.sem}
        self.epoch = {e: 0 for e in self.sem}
        self.key = {e: e + "#0" for e in self.sem}
        self.pending = {e: False for e in self.sem}
        self.waited = {e: {} for e in self.ENG}
        self.dsem = {}
        self.final = {}

    def _wait(self, eng, ev):
        if ev is None:
            return
        sem, val, key = ev
        if key == self.key.get(eng) and val > self.cnt[eng]:
            return
        w = self.waited[eng]
        if w.get(key, 0) >= val:
            return
        w[key] = val
        self.rec[eng].append(lambda e, sem=sem, val=val: e.wait_ge(sem, val))

    def _deps(self, eng, reads, writes):
        for b in reads:
            self._wait(eng, b.w)
            if b.excl:
                mine = self.key.get(eng, eng).split("#")[0]
                for k, ev in list(b.r.items()):
                    if k.split("#")[0] != mine:
                        self._wait(eng, ev)
        for b in writes:
            self._wait(eng, b.w)
            for ev in list(b.r.values()):
                self._wait(eng, ev)
            for a in b.alias:
                self._wait(eng, a.w)
                for ev in list(a.r.values()):
                    self._wait(eng, ev)

    def _mark(self, ev, reads, writes):
        for b in reads:
            old = b.r.get(ev[2])
            if old is None or old[1] < ev[1]:
                b.r[ev[2]] = ev
        for b in writes:
            b.w = ev
            b.r = {}
            for a in b.alias:
                a.w = None
                a.r = {}

    def op(self, eng, fn, reads=(), writes=(), signal=True):
        fn = _freeze(fn)
        if self.cnt[eng] >= self.SEM_LIMIT and not self.pending[eng]:
            self.epoch[eng] += 1
            self.sem[eng] = self.es.enter_context(self.nc.semaphore(f"s_{eng}_{self.epoch[eng]}"))
            self.cnt[eng] = 0
            self.key[eng] = f"{eng}#{self.epoch[eng]}"
        self._deps(eng, reads, writes)
        s = self.sem[eng]
        if signal:
            self.cnt[eng] += 1
            ev = (s, self.cnt[eng], self.key[eng])
            self.rec[eng].append(lambda e, fn=fn, s=s: fn(e).then_inc(s, 1))
            self.pending[eng] = False
        else:
            ev = (s, self.cnt[eng] + 1, self.key[eng])
            self.rec[eng].append(lambda e, fn=fn: fn(e))
            self.pending[eng] = True
        self._mark(ev, reads, writes)
        return ev

    def dma(self, q, pairs, reads=(), writes=(), sem=None, final=False, **kw):
        self._deps(q, reads, writes)
        if sem not in self.dsem:
            self.dsem[sem] = [self.es.enter_context(self.nc.semaphore("d_" + sem)), 0]
        ent = self.dsem[sem]
        for (o, i) in pairs:
            ent[1] += 16
            s = ent[0]
            self.rec[q].append(lambda e, o=o, i=i, s=s, kw=kw: e.dma_start(out=o, in_=i, **kw).then_inc(s, 16))
        ev = (ent[0], ent[1], "dma_" + sem)
        self._mark(ev, reads, writes)
        if final:
            self.final[sem] = ev
        return ev

    def finish(self):
        for name, ev in self.final.items():
            self._wait("sp", ev)
        for e in ("pe", "act", "dve"):
            if self.cnt[e]:
                self._wait("sp", (self.sem[e], self.cnt[e], self.key[e]))

    def run(self):
        nc = self.nc
        engs = {"pe": "tensor", "act": "scalar", "dve": "vector", "pool": "gpsimd", "sp": "sync"}
        with nc.Block() as block:
            for k, attr in engs.items():
                lst = self.rec[k]
                if not lst:
                    continue

                def body(e, lst=lst):
                    for f in lst:
                        f(e)
                getattr(block, attr)(body)


class Arena:
    def __init__(self, nc, es, name, nbytes):
        self.name = name
        self.nbytes = nbytes
        self.t = es.enter_context(nc.sbuf_tensor(name, [128, nbytes // 4], F32))
        self.regs = []

    def ap(self, off, shape, dtype):
        isz = 4 if dtype == F32 else 2
        n = 1
        for s in shape:
            n *= s
        nb = n * isz
        assert off % 4 == 0 and nb % 4 == 0 and off + nb <= self.nbytes, (self.name, off, nb)
        a = self.t[:, off // 4:(off + nb) // 4]
        if dtype != F32:
            a = a.bitcast(dtype)
        if len(shape) == 2:
            a = a.rearrange("p (a b) -> p a b", a=shape[0])
        elif len(shape) == 3:
            a = a.rearrange("p (a b c) -> p a b c", a=shape[0], b=shape[1])
        return a

    def bufs(self, name, off, nbytes, n=1):
        assert off + nbytes <= self.nbytes, (self.name, name, off, nbytes)
        new = [Buf(f"{name}{k}") for k in range(n)]
        for (lo, hi, bs) in self.regs:
            if lo < off + nbytes and off < hi:
                for b in new:
                    for o in bs:
                        b.alias.append(o)
                        o.alias.append(b)
        self.regs.append((off, off + nbytes, new))
        return new


def t5_bucket_np(d):
    n = np.maximum(d, 0)
    nf = np.maximum(n, 1).astype(np.float32)
    large = 16 + (np.log(nf / np.float32(16)) / np.float32(math.log(128 / 16)) * np.float32(16)).astype(np.int32)
    large = np.minimum(large, 31)
    return np.where(n < 16, n, large)


def build_program():
    nc = bass.Bass("TRN2", target_bir_lowering=False)

    def din(name, shape):
        return nc.dram_tensor(name, shape, F32, kind="ExternalInput").ap()

    def dout(name, shape):
        return nc.dram_tensor(name, shape, F32, kind="ExternalOutput").ap()

    xin = din("xin", [NROWS, D])
    pin = din("pin", [4, NROWS, 256])
    spool = din("spool", [2, 16, 15, D])
    sconv = din("sconv", [4, 16, 2, 2 * FF])
    ckin = din("ck", [16, 128, 256])
    cvin = din("cv", [16, 128, 256])
    colsd = din("cols", [128, NCOLS])
    ccd = din("cc", [128, NCC])
    relbd = din("relb", [32, 16])
    ohd = din("oh", [33, 384])
    sinksd = din("sinks", [2, 16])
    wupd = din("wup", [4, NJ, 128, 2048])
    wdnd = din("wdn", [4, NJ, 128, 1024])
    wqd = din("wq", [2, 8, 128, 1024])
    wod = din("wo", [2, 8, 128, 1024])
    wgd = din("wg", [4, 8, 128, 1024])
    wprd = din("wpr", [4, 2, 128, 1024])
    wpld = din("wpl", [2, 4, 2, 128, 256])
    wkd = din("wk", [8, 128, 256])
    wvd = din("wv", [8, 128, 256])

    yout = dout("y", [NROWS, D])
    poolp_o = dout("poolp", [2, 15, D])
    pools_o = dout("pools", [2, 16, 15, D])
    convp_o = dout("convp", [4, 2, 2 * FF])
    convs_o = dout("convs", [4, 16, 2, 2 * FF])
    ckp_o = dout("ckp", [128, 256])
    cvp_o = dout("cvp", [128, 256])
    cks_o = dout("cks", [16, 128, 256])
    cvs_o = dout("cvs", [16, 128, 256])
    gdt = nc.dram_tensor("gd", [16, 384], F32, kind="Internal")
    gd = gdt.ap()

    with ExitStack() as es:
        S = Sched(nc, es)

        def sb(name, shape, dt):
            return es.enter_context(nc.sbuf_tensor(name, shape, dt))

        hT = sb("hT", [128, NCH, TA], F32)
        KT = sb("KT", [128, 2, 128 + TA], BF16)
        Vt = sb("Vt", [128, 11, 256], BF16)
        Tb = sb("Tb", [128, 16, 256], F32)
        Ts = sb("Ts", [128, 4, 256], F32)
        skb = sb("skb", [128, 32], F32)
        sks = sb("sks", [128, 2, 4], F32)
        cols = sb("colst", [128, NCOLS], F32)
        cc = sb("cct", [128, NCC], F32)
        ident_f = sb("ident_f", [128, 128], F32)
        ident_b = sb("ident_b", [128, 128], BF16)
        Jm = sb("Jm", [128, 128], F32)
        ones_b = sb("ones_b", [128, 128], BF16)
        blk_b = sb("blk_b", [128, 128], BF16)
        pcarry = sb("pcarry", [128, 2, NCH, 15], F32)
        ccarry = sb("ccarry", [128, 2, 4, 2, 44], F32)
        wup = [sb(f"wup{k}", [128, 2, 8, 128], BF16) for k in range(3)]
        wdn = [sb(f"wdn{k}", [128, 4, 1024], BF16) for k in range(2)]
        xm = sb("xm", [128, NCH, 512], BF16)
        pT = sb("pT", [128, 2, 512], BF16)
        pst = [sb(f"pst{k}", [128, 256], F32) for k in range(2)]
        sqb = [sb(f"sqb{k}", [128, 512], BF16) for k in range(2)]
        cprev = sb("cprev", [128, 44, 32], F32)
        KTs = sb("KTs", [128, 2, 64], BF16)
        Vsm = sb("Vsm", [64, 256], BF16)
        Ks = [sb(f"Ks{k}", [128, 2, 256], BF16) for k in range(2)]
        Vs = [sb(f"Vs{k}", [128, 256], BF16) for k in range(2)]
        Vp = [sb(f"Vp{k}", [128, 256], BF16) for k in range(2)]
        small = sb("small", [128, 64], F32)
        relb_aug = sb("relb_aug", [33, 16], F32)
        xns = sb("xns", [128, 64], F32)
        xnsB = Buf("xns")
        ugs = [sb(f"ugs{hf}", [128, 96], F32) for hf in range(2)]
        ugsB = [Buf(f"ugs{hf}") for hf in range(2)]
        qs2 = sb("qs2", [128, 2, 16, 16], BF16)
        qs2B = Buf("qs2")

        hB = {(c, b): Buf(f"h{c}_{b}") for c in range(NCH) for b in range(3)}
        KTB = [Buf(f"KT{b}") for b in range(3)]
        KTcB = Buf("KTc")
        VtB = [Buf(f"Vt{b}") for b in range(3)]
        VtcB = Buf("Vtc")
        TbB, TsB, skbB, sksB, colsB, ccB = Buf("Tb"), Buf("Ts"), Buf("skb"), Buf("sks"), Buf("cols"), Buf("cc")
        idfB, idbB, JB, onesB, blkB = Buf("idf"), Buf("idb"), Buf("J"), Buf("ones"), Buf("blk")
        pcB = {(i, c): Buf(f"pc{i}_{c}") for i in range(2) for c in range(NCH)}
        ccB2 = {(p_, i, ch): Buf(f"ccar{p_}_{i}_{ch}") for p_ in range(2) for i in range(4) for ch in range(44)}
        wupB = [Buf(f"wup{k}") for k in range(3)]
        wdnB = [Buf(f"wdn{k}") for k in range(2)]
        xmB = Buf("xm")
        pTB = Buf("pT")
        pstB = [Buf(f"pst{k}") for k in range(2)]
        sqB = [Buf(f"sq{k}") for k in range(2)]
        cprevB = [Buf(f"cprev{ch}") for ch in range(44)]
        KTsB, VsmB = Buf("KTs"), Buf("Vsm")
        KsB = [Buf(f"Ks{k}") for k in range(2)]
        VsB = [Buf(f"Vs{k}") for k in range(2)]
        VpB = [Buf(f"Vp{k}") for k in range(2)]
        smallB = [Buf(f"small{k}") for k in range(8)]
        relbB = Buf("relb")
        gdB = Buf("gd")

        AB = Arena(nc, es, "arenaB", 32768)
        AC = Arena(nc, es, "arenaC", 20480)
        AF_ = Arena(nc, es, "arenaF", 18432)

        xnf = AB.ap(0, (NCH, TA), BF16)
        xnfB = AB.bufs("xnf", 0, NCH * TA * 2, 3)
        actT = AB.ap(20480, (4, TA), BF16)
        _actl = AB.bufs("act", 20480, 4 * TA * 2, 12)
        actB = {(jj, b): _actl[jj * 3 + b] for jj in range(4) for b in range(3)}
        wq = AB.ap(0, (8, 1024), BF16)
        wqB = AB.bufs("wq", 0, 16384)[0]
        wo = AB.ap(16384, (8, 1024), BF16)
        woB = AB.bufs("wo", 16384, 16384)[0]
        wpl = AB.ap(0, (4, 2, 256), BF16)
        wplB = AB.bufs("wpl", 0, 4096)[0]
        Hk = AB.ap(0, (16, 256), F32)
        HkB = AB.bufs("Hk", 0, 16384)[0]

        qT = AC.ap(0, (NCH, 512), BF16)
        qTB = AC.bufs("qT", 0, 8192)[0]
        OT = AC.ap(8192, (NCH, 512), BF16)
        OTB = AC.bufs("OT", 8192, 8192)[0]
        ckst = [AC.ap(16384 + 1024 * k, (256,), F32) for k in range(2)]
        ckstB = [AC.bufs(f"ckst{k}", 16384 + 1024 * k, 1024)[0] for k in range(2)]
        cvst = [AC.ap(18432 + 1024 * k, (256,), F32) for k in range(2)]
        cvstB = [AC.bufs(f"cvst{k}", 18432 + 1024 * k, 1024)[0] for k in range(2)]
        wg = AC.ap(0, (8, 1024), BF16)
        wgB = AC.bufs("wg", 0, 16384)[0]
        wpr = AC.ap(16384, (2, 1024), BF16)
        wprB = AC.bufs("wpr", 16384, 4096)[0]
        wk = AC.ap(0, (8, 256), BF16)
        wkB = AC.bufs("wk", 0, 4096)[0]
        wv = AC.ap(4096, (8, 256), BF16)
        wvB = AC.bufs("wv", 4096, 4096)[0]
        spst = [AC.ap(4096 * k, (1024,), F32) for k in range(2)]
        spstB = [AC.bufs(f"spst{k}", 4096 * k, 4096)[0] for k in range(2)]

        cbuf = [[AF_.ap((s * 2 + hf) * 2056, (514,), F32) for hf in range(2)] for s in range(3)]
        cbB = [[AF_.bufs(f"cb{s}{hf}", (s * 2 + hf) * 2056, 2056)[0] for hf in range(2)] for s in range(3)]
        xe = [AF_.ap(2108 * k, (527,), F32) for k in range(2)]
        xeB = [AF_.bufs(f"xe{k}", 2108 * k, 2108)[0] for k in range(2)]
        tab = [AF_.ap(2108 * (2 + k), (527,), F32) for k in range(2)]
        tabB = [AF_.bufs(f"tab{k}", 2108 * (2 + k), 2108)[0] for k in range(2)]
        nr = [AF_.ap(8432 + 2048 * k, (512,), F32) for k in range(2)]
        nrB = [AF_.bufs(f"nr{k}", 8432 + 2048 * k, 2048)[0] for k in range(2)]
        scb = [AF_.ap(1024 * k, (256,), F32) for k in range(8)]
        scB = [AF_.bufs(f"sc{k}", 1024 * k, 1024)[0] for k in range(8)]
        eb = [AF_.ap(8192 + 512 * k, (256,), BF16) for k in range(8)]
        ebB = [AF_.bufs(f"e{k}", 8192 + 512 * k, 512)[0] for k in range(8)]
        pTs = [AF_.ap(12528 + 512 * k, (256,), BF16) for k in range(4)]
        pTsB = [AF_.bufs(f"pTs{k}", 12528 + 512 * k, 512)[0] for k in range(4)]
        pb = [AF_.ap(14576 + 512 * k, (256,), BF16) for k in range(4)]
        pbB = [AF_.bufs(f"p{k}", 14576 + 512 * k, 512)[0] for k in range(4)]
        knf = AF_.ap(5120, (512,), F32)
        knfB = AF_.bufs("knf", 5120, 2048)[0]
        sg = [AF_.ap(2048 * k, (512,), F32) for k in range(2)]
        sgB = [AF_.bufs(f"sg{k}", 2048 * k, 2048)[0] for k in range(2)]
        tmpb = [AF_.ap(4096 + 2048 * k, (512,), F32) for k in range(2)]
        tmpB = [AF_.bufs(f"tmp{k}", 4096 + 2048 * k, 2048)[0] for k in range(2)]
        xst = [AF_.ap(4096 * k, (1024,), F32) for k in range(2)]
        xstB = [AF_.bufs(f"xst{k}", 4096 * k, 4096)[0] for k in range(2)]
        scst = AF_.ap(12528, (1408,), F32)
        scstB = AF_.bufs("scst", 12528, 5632)[0]
        ost = AF_.ap(12528, (1024,), F32)
        ostB = AF_.bufs("ost", 12528, 4096)[0]
        gsb = AF_.ap(0, (384,), F32)
        gsbB = AF_.bufs("gsb", 0, 1536)[0]
        oht = AF_.ap(2048, (384,), F32)
        ohB = AF_.bufs("oht", 2048, 1536)[0]

        pst_t = [es.enter_context(nc.psum_tensor(f"ps{k}", [128, 512], F32)) for k in range(8)]
        psB = [Buf(f"ps{k}", excl=True) for k in range(8)]
        free = list(range(8))

        def bank(hold=False):
            k = free.pop(0)
            if not hold:
                free.append(k)
            return pst_t[k], psB[k], k

        def release(k):
            free.append(k)

        flip = [0]

        def evac_eng():
            flip[0] ^= 1
            return "act" if flip[0] else "dve"

        def copy_op(eng, out, in_, reads, writes):
            if eng == "act":
                S.op("act", lambda e: e.activation(out=out, in_=in_, func=AF.Copy), reads=reads, writes=writes)
            else:
                S.op("dve", lambda e: e.tensor_copy(out=out, in_=in_), reads=reads, writes=writes)

        def col(k):
            return cols[:, k:k + 1]

        S.dma("sp", [(cols[:], colsd)], writes=[colsB], sem="cols")
        S.dma("sp", [(cc[:], ccd)], writes=[ccB], sem="cc")
        S.op("pool", lambda e: e.memset(ident_f[:], 0.0), writes=[idfB])
        S.op("pool", lambda e: e.affine_select(out=ident_f[:], in_=ident_f[:], compare_op=ALU.not_equal, fill=1.0,
                                                base=0, pattern=[[-1, 128]], channel_multiplier=1),
             reads=[idfB], writes=[idfB])
        S.op("pool", lambda e: e.memset(Jm[:], 0.0), writes=[JB])
        S.op("pool", lambda e: e.affine_select(out=Jm[:], in_=Jm[:], compare_op=ALU.not_equal, fill=1.0,
                                                base=-127, pattern=[[1, 128]], channel_multiplier=1),
             reads=[JB], writes=[JB])
        S.op("dve", lambda e: e.tensor_copy(out=ident_b[:], in_=ident_f[:]), reads=[idfB], writes=[idbB])
        S.op("dve", lambda e: e.memset(ones_b[:], 1.0), writes=[onesB])
        S.op("dve", lambda e: e.memset(blk_b[:], 0.0), writes=[blkB])
        S.op("dve", lambda e: e.memset(blk_b[0:64, 0:64], 1.0), writes=[blkB])
        S.op("dve", lambda e: e.memset(blk_b[64:128, 64:128], 1.0), writes=[blkB])
        S.op("dve", lambda e: e.memset(KT[:, :, 0:128], 0.0), writes=[KTcB])
        S.op("dve", lambda e: e.memset(Vt[:, 0, :], 0.0), writes=[VtcB])
        S.op("dve", lambda e: e.memset(pcarry[:], 0.0), writes=list(pcB.values()))
        S.op("dve", lambda e: e.memset(ccarry[:], 0.0), writes=list(ccB2.values()))
        for k in range(2):
            S.op("dve", lambda e, k=k: e.memset(Ks[k][:], 0.0), writes=[KsB[k]])
            S.op("dve", lambda e, k=k: e.memset(Vp[k][:], 0.0), writes=[VpB[k]])
        S.op("dve", lambda e: e.memset(Ts[:], 0.0), writes=[TsB])
        S.op("dve", lambda e: e.memset(sks[:], 0.0), writes=[sksB])
        S.op("dve", lambda e: e.memset(small[:], 0.0), writes=smallB)

        S.op("dve", lambda e: e.memset(relb_aug[:], 1.0), writes=[relbB])
        S.dma("sp", [(relb_aug[0:32, :], relbd)], writes=[relbB], sem="relb")
        S.dma("sp", [(oht[0:33, :], ohd)], writes=[ohB], sem="oh")
        pt, pbk, _ = bank()
        S.op("pe", lambda e: e.matmul(pt[0:16, 0:384], lhsT=relb_aug[0:33, 0:16], rhs=oht[0:33, 0:384], start=True, stop=True),
             reads=[relbB, ohB], writes=[pbk])
        S.op("act", lambda e: e.activation(out=gsb[0:16, :], in_=pt[0:16, 0:384], func=AF.Copy), reads=[pbk], writes=[gsbB])
        S.dma("sp", [(gd, gsb[0:16, :])], reads=[gsbB], writes=[gdB], sem="gd")
        S.dma("sp", [(Hk, bass.AP(gdt, 0, [[1, 128], [384, 16], [1, 256]]))], reads=[gdB], writes=[HkB], sem="hk")
        for h in range(16):
            pt, pbk, _ = bank()
            S.op("pe", lambda e, pt=pt, h=h: e.matmul(pt[:, 128:256], lhsT=Hk[:, h, 0:128], rhs=Jm[:], start=True, stop=True),
                 reads=[HkB, JB], writes=[pbk], signal=False)
            S.op("pe", lambda e, pt=pt, h=h: e.matmul(pt[:, 0:128], lhsT=Hk[:, h, 128:256], rhs=Jm[:], start=True, stop=True),
                 reads=[HkB, JB], writes=[pbk])
            copy_op(evac_eng(), Tb[:, h, :], pt[:, 0:256], [pbk], [TbB])
        for kv in range(4):
            for g in range(4):
                S.dma("sp", [(Ts[4 * g:4 * g + 4, kv, :], Tb[0:4, 4 * kv + g, :])], reads=[TbB], writes=[TsB], sem="ts")
        S.dma("sp", [(skb[:], sinksd.rearrange("a b -> (a b)").partition_broadcast(128))], writes=[skbB], sem="skb")
        for j in range(2):
            for kv in range(4):
                for g in range(4):
                    S.dma("sp", [(sks[4 * g:4 * g + 4, j, kv:kv + 1],
                                  sinksd[j, 4 * kv + g:4 * kv + g + 1].partition_broadcast(4))],
                          writes=[sksB], sem="sks")

        def load_w(dst, src, buf, sem):
            S.dma("pool", [(dst, src)], writes=[buf], sem=sem)

        def load_wup(i, j):
            k = j % 3
            load_w(wup[k][:].rearrange("p h k c -> p (h k c)"), wupd[i, j], wupB[k], f"wup{k}")

        def load_wdn(i, pi):
            j0, j1 = PARTS[pi]
            k = pi % 2
            load_w(wdn[k][:, 0:j1 - j0, :], wdnd[i, j0:j1].rearrange("j p n -> p j n"), wdnB[k], f"wdn{k}")

        def load_mixer(i):
            if i < 2:
                load_w(wpl, wpld[i].rearrange("g k p n -> p g k n"), wplB, "wpl")
            else:
                load_w(wq, wqd[i - 2].rearrange("k p n -> p k n"), wqB, "wq")
                load_w(wo, wod[i - 2].rearrange("k p n -> p k n"), woB, "wo")

        def load_ple(i):
            load_w(wg, wgd[i].rearrange("k p n -> p k n"), wgB, "wg")
            load_w(wpr, wprd[i].rearrange("k p n -> p k n"), wprB, "wpr")

        import os
        SKIP = os.environ.get("KSKIP", "").split(",")

        def hreads(b):
            return [hB[(c, b)] for c in range(NCH)]

        def rstd_of(c0, N, b, scale, lhs, lhsB, src_fn, src_reads, nk, k):
            pt, pbk, _ = bank()
            for c in range(nk):
                q = c % 2
                S.op("act", lambda e, c=c, q=q: e.activation(out=sqb[q][:, 0:N], in_=src_fn(c), func=AF.Square),
                     reads=src_reads(c), writes=[sqB[q]])
                S.op("pe", lambda e, c=c, q=q, pt=pt: e.matmul(pt[:, 0:N], lhsT=lhs, rhs=sqb[q][:, 0:N], start=(c == 0), stop=(c == nk - 1)),
                     reads=[sqB[q], lhsB], writes=[pbk], signal=True)
            S.op("act", lambda e, pt=pt: e.activation(out=nr[k][:, 0:N], in_=pt[:, 0:N], func=AF.Ln, bias=col(C_EPS), scale=scale),
                 reads=[pbk, colsB], writes=[nrB[k]])
            S.op("act", lambda e: e.activation(out=nr[k][:, 0:N], in_=nr[k][:, 0:N], func=AF.Exp, scale=-0.5), reads=[nrB[k]], writes=[nrB[k]])

        def rms_block(blk, b, gbase, out, outB, k=0):
            c0, N, kind = blk
            rstd_of(c0, N, b, 1.0 / D, ones_b[:], onesB, lambda c: hT[:, c, c0:c0 + N], lambda c: [hB[(c, b)]], NCH, k)
            for c in range(NCH):
                S.op("dve", lambda e, c=c: e.scalar_tensor_tensor(out=out(c), in0=hT[:, c, c0:c0 + N], scalar=col(gbase + c),
                                                                  in1=nr[k][:, 0:N], op0=ALU.mult, op1=ALU.mult),
                     reads=[hB[(c, b)], nrB[k], colsB], writes=[outB])

        def load_x(sbi, blocks):
            row0 = 0 if sbi == 0 else TA
            tiles = []
            for (c0, N, kind) in blocks:
                if kind == "p":
                    for t in range(N // 128):
                        tiles.append((row0 + c0 + 128 * t, c0 + 128 * t, 128))
                else:
                    tiles.append((NPROMPT, c0, 64))
            for ti, (r0, cc0, rows) in enumerate(tiles):
                q = ti % 2
                b = cc0 // 512
                S.dma("sp", [(xst[q][0:rows, :], xin[r0:r0 + rows, :])], writes=[xstB[q]], sem=f"xst{q}")
                for hb in range(2):
                    pt, pbk, _ = bank()
                    for cq in range(4):
                        c = hb * 4 + cq
                        S.op("pe", lambda e, pt=pt, c=c, cq=cq, q=q, rows=rows: e.transpose(
                            out=pt[:, cq * 128:cq * 128 + rows], in_=xst[q][0:rows, c * 128:(c + 1) * 128], identity=ident_f[0:rows, 0:rows]),
                            reads=[xstB[q], idfB], writes=[pbk], signal=(cq == 3))
                    src = pt[:, :].rearrange("p (a b) -> p a b", a=4)[:, :, 0:rows]
                    copy_op(evac_eng(), hT[:, hb * 4:hb * 4 + 4, cc0:cc0 + rows], src, [pbk], [hB[(c, b)] for c in range(hb * 4, hb * 4 + 4)])

        def store_y(sbi, blocks):
            row0 = 0 if sbi == 0 else TA
            tiles = []
            for (c0, N, kind) in blocks:
                if kind == "p":
                    for t in range(N // 128):
                        tiles.append((row0 + c0 + 128 * t, c0 + 128 * t, 128))
                else:
                    tiles.append((NPROMPT, c0, 64))
            for ti, (r0, cc0, rows) in enumerate(tiles):
                q = ti % 2
                b = cc0 // 512
                for hb in range(2):
                    pt, pbk, _ = bank()
                    for cq in range(4):
                        c = hb * 4 + cq
                        S.op("pe", lambda e, pt=pt, c=c, cq=cq, rows=rows, cc0=cc0: e.transpose(
                            out=pt[0:rows, cq * 128:(cq + 1) * 128], in_=hT[:, c, cc0:cc0 + rows], identity=ident_f[:]),
                            reads=[hB[(c, b)], idfB], writes=[pbk], signal=(cq == 3))
                    copy_op(evac_eng(), xst[q][0:rows, hb * 512:(hb + 1) * 512], pt[0:rows, :], [pbk], [xstB[q]])
                S.dma("sp", [(yout[r0:r0 + rows, :], xst[q][0:rows, :])], reads=[xstB[q]], sem=f"xst{q}", final=True)

        def pool_block(i, sbi, blk, b):
            c0, N, kind = blk
            rstd_of(c0, N, b, 1.0 / D, ones_b[:], onesB, lambda c: hT[:, c, c0:c0 + N], lambda c: [hB[(c, b)]], NCH, 0)
            first = (sbi == 0 and b == 0)
            if kind == "s":
                for hf in range(2):
                    S.dma("sp", [(spst[hf][0:120, :], spool[i, 8 * hf:8 * hf + 8].rearrange("s r d -> (s r) d"))],
                          writes=[spstB[hf]], sem=f"spst{hf}")
                for hf in range(2):
                    S.dma("sp", [(pools_o[i, 8 * hf + s_, 0:11, :], spst[hf][15 * s_ + 4:15 * s_ + 15, :]) for s_ in range(8)],
                          reads=[spstB[hf]], sem=f"spo{hf}", final=True)
                hold = [bank(hold=True), bank(hold=True)]
            for c in range(NCH):
                g = c // 2
                w = POOL_W[g]
                q = c % 2
                X = xe[q]
                if kind == "p":
                    L = 15 + N
                    S.op("act", lambda e, X=X, c=c: e.activation(out=X[:, 0:15], in_=pcarry[:, i, c, :], func=AF.Copy),
                         reads=[pcB[(i, c)]], writes=[xeB[q]])
                    S.op("dve", lambda e, X=X, c=c: e.scalar_tensor_tensor(out=X[:, 15:15 + N], in0=hT[:, c, c0:c0 + N], scalar=col(C_NM + i * 8 + c),
                                                                           in1=nr[0][:, 0:N], op0=ALU.mult, op1=ALU.mult),
                         reads=[hB[(c, b)], nrB[0], colsB], writes=[xeB[q]])
                    if first:
                        S.op("dve", lambda e, X=X: e.tensor_scalar(out=X[:, 15 + 241:15 + 256], in0=X[:, 15 + 241:15 + 256],
                                                                   scalar1=cc[:, CC_M:CC_M + 1], scalar2=None, op0=ALU.mult),
                             reads=[xeB[q], ccB], writes=[xeB[q]])
                    S.op("act", lambda e, X=X, c=c: e.activation(out=pcarry[:, i, c, :], in_=X[:, N:N + 15], func=AF.Copy),
                         reads=[xeB[q]], writes=[pcB[(i, c)]])
                    xnew = X[:, 15:15 + N]
                    dout_ap = xm[:, c, 0:N]
                else:
                    L = 304
                    X3 = X[:, 0:304].rearrange("p (s k) -> p s k", k=19)
                    pt, pbk, _ = bank()
                    for hf in range(2):
                        S.op("pe", lambda e, pt=pt, hf=hf, c=c: e.transpose(out=pt[:, hf * 120:hf * 120 + 120], in_=spst[hf][0:120, c * 128:(c + 1) * 128],
                                                                            identity=ident_f[0:120, 0:120]),
                             reads=[spstB[hf], idfB], writes=[pbk], signal=(hf == 1))
                    copy_op("act", X3[:, :, 0:15], pt[:, 0:240].rearrange("p (s k) -> p s k", k=15), [pbk], [xeB[q]])
                    hv = hT[:, c, c0:c0 + 64].rearrange("p (t s) -> p s t", s=16)
                    nv = nr[0][:, 0:64].rearrange("p (t s) -> p s t", s=16)
                    S.op("dve", lambda e, c=c: e.scalar_tensor_tensor(out=xns[:, :], in0=hT[:, c, c0:c0 + 64], scalar=col(C_NM + i * 8 + c),
                                                                      in1=nr[0][:, 0:64], op0=ALU.mult, op1=ALU.mult),
                         reads=[hB[(c, b)], nrB[0], colsB], writes=[xnsB])
                    S.op("act", lambda e, X3=X3: e.activation(out=X3[:, :, 15:19], in_=xns[:, :].rearrange("p (t s) -> p s t", s=16), func=AF.Copy),
                         reads=[xnsB], writes=[xeB[q]])
                    hp, hpb, _ = hold[c // 4]
                    S.op("pe", lambda e, hp=hp, c=c: e.transpose(out=hp[0:64, (c % 4) * 128:(c % 4 + 1) * 128], in_=xns[:, :], identity=ident_f[:]),
                         reads=[xnsB, idfB], writes=[hpb])
                    xnew = X3[:, :, 15:19]
                    dout_ap = xm[:, c, 0:64].rearrange("p (t s) -> p s t", s=16)
                a, aB = X, xeB[q]
                lo = 0
                for si, s in enumerate((1, 2, 4, 8)[:g + 1]):
                    lo2 = lo + s
                    tb_, tbB_ = tab[si % 2], tabB[si % 2]
                    S.op("dve", lambda e, a=a, tb_=tb_, lo2=lo2, s=s, L=L: e.tensor_tensor(out=tb_[:, lo2:L], in0=a[:, lo2:L], in1=a[:, lo2 - s:L - s], op=ALU.add),
                         reads=[aB], writes=[tbB_])
                    a, aB, lo = tb_, tbB_, lo2
                if kind == "p":
                    asum = a[:, 15:15 + N]
                else:
                    asum = a[:, 0:304].rearrange("p (s k) -> p s k", k=19)[:, :, 15:19]
                S.op("dve", lambda e, asum=asum, xnew=xnew, dout_ap=dout_ap, w=w: e.scalar_tensor_tensor(out=dout_ap, in0=asum, scalar=1.0 / w, in1=xnew,
                                                                                                      op0=ALU.mult, op1=ALU.subtract),
                     reads=[aB, xeB[q]], writes=[xmB])
                if first:
                    t15 = tab[(g + 1) % 2][:, 0:15]
                    S.op("dve", lambda e, a=a, t15=t15, g=g: e.tensor_tensor(out=t15, in0=a[:, 15 + 256:15 + 271], in1=cc[:, CC_INV + 15 * g:CC_INV + 15 * g + 15], op=ALU.mult),
                         reads=[aB, ccB], writes=[tabB[(g + 1) % 2]])
                    S.op("dve", lambda e, X=X, t15=t15, c=c: e.tensor_tensor(out=xm[:, c, 256:271], in0=t15, in1=X[:, 15 + 256:15 + 271], op=ALU.subtract),
                         reads=[tabB[(g + 1) % 2], xeB[q]], writes=[xmB])
            if kind == "s":
                for hb in range(2):
                    hp, hpb, hk = hold[hb]
                    copy_op(evac_eng(), ost[0:64, hb * 512:(hb + 1) * 512], hp[0:64, :], [hpb], [ostB])
                    release(hk)
                S.dma("sp", [(pools_o[i, :, 11 + t, :], ost[16 * t:16 * t + 16, :]) for t in range(4)], reads=[ostB], sem="ost", final=True)
            M = N
            for g in range(4):
                for mo in range(2):
                    pt, pbk, _ = bank()
                    for kc in range(2):
                        S.op("pe", lambda e, pt=pt, g=g, mo=mo, kc=kc: e.matmul(pt[:, 0:M], lhsT=wpl[:, g, kc, mo * 128:(mo + 1) * 128], rhs=xm[:, 2 * g + kc, 0:M],
                                                                               start=(kc == 0), stop=(kc == 1)),
                             reads=[wplB, xmB], writes=[pbk], signal=(kc == 1))
                    m = 2 * g + mo
                    S.op("dve", lambda e, pt=pt, m=m: e.scalar_tensor_tensor(out=hT[:, m, c0:c0 + M], in0=pt[:, 0:M], scalar=col(C_PSC + i * 8 + m),
                                                                            in1=hT[:, m, c0:c0 + M], op0=ALU.mult, op1=ALU.add),
                         reads=[pbk, hB[(m, b)], colsB], writes=[hB[(m, b)]])

        def qk_norm(pt, pbk, N, gcol, out_bf, out_bf_B, out_f32=None, out_f32_B=None, k=1):
            rstd_of(0, N, 0, 1.0 / 64, blk_b[:], blkB, lambda c: pt[:, 0:N], lambda c: [pbk], 1, k)
            S.op("dve", lambda e: e.scalar_tensor_tensor(out=out_bf, in0=pt[:, 0:N], scalar=col(gcol), in1=nr[k][:, 0:N], op0=ALU.mult, op1=ALU.mult),
                 reads=[pbk, nrB[k], colsB], writes=[out_bf_B])
            if out_f32 is not None:
                S.op("dve", lambda e: e.scalar_tensor_tensor(out=out_f32, in0=pt[:, 0:N], scalar=col(gcol), in1=nr[k][:, 0:N], op0=ALU.mult, op1=ALU.mult),
                     reads=[pbk, nrB[k], colsB], writes=[out_f32_B])

        def kv_block(sbi, blk, b, last_prompt):
            c0, N, kind = blk
            rms_block(blk, b, C_KVN, lambda c: xm[:, c, 0:N], xmB, k=0)
            want_cache = (sbi == 1) and (kind == "s" or last_prompt) and ("kvout" not in SKIP)
            for kp in range(2):
                if kind == "s" and "skv_k" in SKIP:
                    continue
                pt, pbk, _ = bank()
                for k in range(NCH):
                    S.op("pe", lambda e, pt=pt, k=k, kp=kp: e.matmul(pt[:, 0:N], lhsT=wk[:, k, kp * 128:(kp + 1) * 128], rhs=xm[:, k, 0:N], start=(k == 0), stop=(k == 7)),
                         reads=[wkB, xmB], writes=[pbk], signal=(k == 7))
                if kind == "p":
                    qk_norm(pt, pbk, N, C_KN, KT[:, kp, 128 + c0:128 + c0 + N], KTB[b],
                            knf[:, 0:N] if want_cache else None, knfB)
                else:
                    qk_norm(pt, pbk, N, C_KN, KTs[:, kp, 0:64], KTsB, knf[:, 0:64], knfB)
                if want_cache:
                    rows = 128 if kind == "p" else 64
                    p2, p2b, _ = bank()
                    src = knf[:, N - 128:N] if kind == "p" else knf[:, 0:64]
                    S.op("pe", lambda e, p2=p2, src=src, rows=rows: e.transpose(out=p2[0:rows, 0:128], in_=src, identity=ident_f[:]),
                         reads=[knfB, idfB], writes=[p2b])
                    copy_op(evac_eng(), ost[0:rows, kp * 128:(kp + 1) * 128], p2[0:rows, 0:128], [p2b], [ostB])
            if want_cache:
                if kind == "p":
                    S.dma("sp", [(ckp_o, ost[:, 0:256])], reads=[ostB], sem="ost", final=True)
                else:
                    S.dma("sp", [(cks_o[:, 124 + t, :], ost[16 * t:16 * t + 16, 0:256]) for t in range(4)], reads=[ostB], sem="ost", final=True)
            if kind == "p":
                for t in range(N // 128):
                    pt, pbk, _ = bank()
                    for k in range(NCH):
                        S.op("pe", lambda e, pt=pt, k=k, t=t: e.matmul(pt[:, 0:256], lhsT=xm[:, k, 128 * t:128 * t + 128], rhs=wv[:, k, :], start=(k == 0), stop=(k == 7)),
                             reads=[wvB, xmB], writes=[pbk], signal=(k == 7))
                    gt = c0 // 128 + t
                    copy_op(evac_eng(), Vt[:, 1 + gt, :], pt[:, 0:256], [pbk], [VtB[b]])
                    if want_cache and t == N // 128 - 1:
                        copy_op("act", ost[:, 256:512], pt[:, 0:256], [pbk], [ostB])
                        S.dma("sp", [(cvp_o, ost[:, 256:512])], reads=[ostB], sem="ost", final=True)
            elif "skv_v" not in SKIP:
                pt, pbk, _ = bank()
                for k in range(NCH):
                    S.op("pe", lambda e, pt=pt, k=k: e.matmul(pt[0:64, 0:256], lhsT=xm[:, k, 0:64], rhs=wv[:, k, :], start=(k == 0), stop=(k == 7)),
                         reads=[wvB, xmB], writes=[pbk], signal=(k == 7))
                copy_op("dve", Vsm[:, :], pt[0:64, 0:256], [pbk], [VsmB])
                if "skv_o" not in SKIP:
                    copy_op("act", ost[0:64, 256:512], pt[0:64, 0:256], [pbk], [ostB])
                    S.dma("sp", [(cvs_o[:, 124 + t, :], ost[16 * t:16 * t + 16, 256:512]) for t in range(4)], reads=[ostB], sem="ost", final=True)

        sm_ctr = [0]

        def attn_stage_a(items):
            gsl = (sm_ctr[0] // 4) % 2
            for idx, it in enumerate(items):
                it["k"] = sm_ctr[0] % 8
                it["k4"] = sm_ctr[0] % 4
                sm_ctr[0] += 1
                it["sm"] = small[0:it["P"], 32 * gsl + 8 * idx:32 * gsl + 8 * idx + 8]
                it["smB"] = smallB[4 * gsl + idx]
                it["gsm"] = small[0:it["P"], 32 * gsl:32 * gsl + 32].rearrange("p (i c) -> p i c", c=8)
            for it in items:
                k, P = it["k"], it["P"]
                S.op("dve", lambda e, it=it, k=k, P=P: e.scalar_tensor_tensor(out=scb[k][0:P, :], in0=it["pss"], scalar=0.125, in1=it["bias"], op0=ALU.mult, op1=ALU.add),
                     reads=[it["pssB"], it["biasB"]], writes=[scB[k]])
                if it["mask"]:
                    S.op("dve", lambda e, k=k: e.tensor_tensor(out=scb[k][:, 0:128], in0=scb[k][:, 0:128], in1=cc[:, CC_FM:CC_FM + 128], op=ALU.add),
                         reads=[scB[k], ccB], writes=[scB[k]])
            for it in items:
                k, P, sm, smB = it["k"], it["P"], it["sm"], it["smB"]
                S.op("dve", lambda e, k=k, P=P, sm=sm: e.reduce_max(out=sm[:, 0:1], in_=scb[k][0:P, :], axis=AX.X), reads=[scB[k]], writes=[smB])
            for it in items:
                k, P, sm, smB = it["k"], it["P"], it["sm"], it["smB"]
                S.op("dve", lambda e, it=it, sm=sm: e.tensor_scalar(out=sm[:, 1:2], in0=sm[:, 0:1], scalar1=it["sink"], scalar2=-1.0, op0=ALU.max, op1=ALU.mult),
                     reads=[smB, it["sinkB"]], writes=[smB])
            for it in items:
                k, P, sm, smB = it["k"], it["P"], it["sm"], it["smB"]
                S.op("act", lambda e, k=k, P=P, sm=sm: e.activation(out=eb[k][0:P, :], in_=scb[k][0:P, :], func=AF.Exp, bias=sm[:, 1:2], accum_out=sm[:, 2:3]),
                     reads=[scB[k], smB], writes=[ebB[k], smB])
                S.op("act", lambda e, it=it, sm=sm: e.activation(out=sm[:, 3:4], in_=it["sink"], func=AF.Exp, bias=sm[:, 1:2]),
                     reads=[smB, it["sinkB"]], writes=[smB])

        def attn_stage_b(items):
            ptt, pttB, _ = bank()
            ptb = ptt[:, :].bitcast(BF16)
            g0 = items[0]
            gsm, gB = g0["gsm"], [it["smB"] for it in items]
            S.op("dve", lambda e, gsm=gsm: e.tensor_tensor(out=gsm[:, :, 4], in0=gsm[:, :, 2], in1=gsm[:, :, 3], op=ALU.add), reads=gB, writes=gB)
            S.op("dve", lambda e, gsm=gsm: e.reciprocal(out=gsm[:, :, 5], in_=gsm[:, :, 4]), reads=gB, writes=gB)
            for it in items:
                k, k4, P, sm, smB = it["k"], it["k4"], it["P"], it["sm"], it["smB"]
                S.op("dve", lambda e, k=k, k4=k4, P=P, sm=sm: e.tensor_scalar(out=pb[k4][0:P, :], in0=eb[k][0:P, :], scalar1=sm[:, 5:6], scalar2=None, op0=ALU.mult),
                     reads=[ebB[k], smB], writes=[pbB[k4]])
            for idx, it in enumerate(items):
                k4, P = it["k4"], it["P"]
                base = idx * 2 * P
                for hh in range(2):
                    S.op("pe", lambda e, k4=k4, P=P, hh=hh, base=base: e.transpose(out=ptb[:, base + hh * P:base + (hh + 1) * P], in_=pb[k4][0:P, hh * 128:(hh + 1) * 128], identity=ident_b[0:P, 0:P]),
                         reads=[pbB[k4], idbB], writes=[pttB], signal=(hh == 1))
            for idx, it in enumerate(items):
                k4, P = it["k4"], it["P"]
                base = idx * 2 * P
                copy_op("act", pTs[k4][:, 0:2 * P], ptb[:, base:base + 2 * P], [pttB], [pTsB[k4]])
            for it in items:
                k4, P = it["k4"], it["P"]
                for hh in range(2):
                    vap, vB = it["v"][hh]
                    S.op("pe", lambda e, it=it, k4=k4, P=P, hh=hh, vap=vap: e.matmul(it["po"], lhsT=vap, rhs=pTs[k4][:, hh * P:(hh + 1) * P], start=(hh == 0), stop=(hh == 1)),
                         reads=[pTsB[k4], vB], writes=[it["poB"]], signal=(hh == 1))

        pendg = [None]

        def attn_block(i, sbi, blk, b):
            j = i - 2
            c0, N, kind = blk
            rms_block(blk, b, C_NM + i * 8, lambda c: xm[:, c, 0:N], xmB, k=0)
            for cp in range(NCH):
                pt, pbk, _ = bank()
                for k in range(NCH):
                    S.op("pe", lambda e, pt=pt, k=k, cp=cp: e.matmul(pt[:, 0:N], lhsT=wq[:, k, cp * 128:(cp + 1) * 128], rhs=xm[:, k, 0:N], start=(k == 0), stop=(k == 7)),
                         reads=[wqB, xmB], writes=[pbk], signal=(k == 7))
                qk_norm(pt, pbk, N, C_QN + j, qT[:, cp, 0:N], qTB, k=1)
            if kind == "p":
                groups = [(t, cpp) for t in range(N // 128) for cpp in range(0, NCH, 2)]

                def emit_scores(t, cpp):
                    tcol = c0 + 128 * t
                    gt = tcol // 128
                    kb_prev = (KTcB, VtcB) if gt == 0 else (KTB[(tcol - 128) // 512], VtB[(tcol - 128) // 512])
                    items = []
                    for cp in (cpp, cpp + 1):
                        pss, pssB, _ = bank()
                        kp = cp // 4
                        for hf in range(2):
                            kv = 2 * kp + hf
                            head = 4 * kv + cp % 4
                            r0, r1 = 64 * hf, 64 * hf + 64
                            S.op("pe", lambda e, pss=pss, hf=hf, r0=r0, r1=r1, cp=cp, t=t, kp=kp, tcol=tcol: e.matmul(
                                pss[:, 256 * hf:256 * hf + 256], lhsT=qT[r0:r1, cp, 128 * t:128 * t + 128], rhs=KT[r0:r1, kp, tcol:tcol + 256], start=True, stop=True),
                                reads=[qTB, KTB[b], kb_prev[0]], writes=[pssB])
                            items.append(dict(pss=pss[:, 256 * hf:256 * hf + 256], pssB=pssB, bias=Tb[:, head, :], biasB=TbB,
                                              sink=skb[:, 16 * j + head:16 * j + head + 1], sinkB=skbB, mask=(sbi == 0 and gt == 2),
                                              v=[(Vt[:, gt, kv * 64:(kv + 1) * 64], kb_prev[1]), (Vt[:, gt + 1, kv * 64:(kv + 1) * 64], VtB[b])],
                                              P=128, cp=cp, rows=(r0, r1)))
                    return items

                def fin_p(items, t):
                    outs = {}
                    for it in items:
                        if it["cp"] not in outs:
                            outs[it["cp"]] = bank()
                        psO, psOB, _ = outs[it["cp"]]
                        it["po"] = psO[it["rows"][0]:it["rows"][1], 0:128]
                        it["poB"] = psOB
                    attn_stage_b(items)

                    def cp_out(outs=outs, t=t):
                        for cp, (psO, psOB, _) in outs.items():
                            copy_op("act", OT[:, cp, 128 * t:128 * t + 128], psO[:, 0:128], [psOB], [OTB])
                    return cp_out

                sc_items = [None] * len(groups)
                sc_items[0] = emit_scores(*groups[0])
                prev = None
                late = None
                for gi, (t, cpp) in enumerate(groups):
                    if gi + 1 < len(groups):
                        sc_items[gi + 1] = emit_scores(*groups[gi + 1])
                    if late is not None:
                        late()
                        late = None
                    attn_stage_a(sc_items[gi])
                    if prev is not None:
                        late = fin_p(*prev)
                    prev = (sc_items[gi], t)
                if late is not None:
                    late()
                if prev is not None:
                    fin_p(*prev)()
            elif "sattn" in SKIP:
                S.op("dve", lambda e: e.memset(OT[:, :, 0:64], 0.0), writes=[OTB])
            else:
                for cp in range(NCH):
                    S.op("dve", lambda e, cp=cp: e.tensor_copy(out=qs2[:, cp // 4, :, 4 * (cp % 4):4 * (cp % 4) + 4],
                                                             in_=qT[:, cp, 0:64].rearrange("p (t s) -> p s t", s=16)),
                         reads=[qTB], writes=[qs2B])
                for s in range(16):
                    q = s % 2
                    S.dma("sp", [(ckst[q][:, :], ckin[s])], writes=[ckstB[q]], sem=f"ckst{q}")
                    S.dma("sp", [(cvst[q][:, :], cvin[s])], writes=[cvstB[q]], sem=f"cvst{q}")
                    if j == 0:
                        S.dma("sp", [(cks_o[s, 0:124, :], ckst[q][4:128, :]), (cvs_o[s, 0:124, :], cvst[q][4:128, :])],
                              reads=[ckstB[q], cvstB[q]], sem=f"cko{q}", final=True)
                    pt, pbk, _ = bank()
                    for kp in range(2):
                        S.op("pe", lambda e, pt=pt, kp=kp, q=q: e.transpose(out=pt[:, kp * 128:(kp + 1) * 128], in_=ckst[q][:, kp * 128:(kp + 1) * 128], identity=ident_f[:]),
                             reads=[ckstB[q], idfB], writes=[pbk], signal=(kp == 1))
                    copy_op("act", Ks[q][:, :, 0:128], pt[:, 0:256].rearrange("p (a b) -> p a b", a=2), [pbk], [KsB[q]])
                    S.op("dve", lambda e, q=q, s=s: e.tensor_copy(out=Ks[q][:, :, 128:132], in_=KTs[:, :, s:64:16]), reads=[KTsB], writes=[KsB[q]])
                    copy_op("dve", Vs[q][:, :], cvst[q][:, :], [cvstB[q]], [VsB[q]])
                    S.dma("sp", [(Vp[q][t:t + 1, :], Vsm[16 * t + s:16 * t + s + 1, :]) for t in range(4)], reads=[VsmB], writes=[VpB[q]], sem=f"vp{q}")
                    items = []
                    for kp in range(2):
                        pss, pssB, _ = bank()
                        for hf in range(2):
                            kv = 2 * kp + hf
                            r0, r1 = 64 * hf, 64 * hf + 64
                            S.op("pe", lambda e, pss=pss, hf=hf, r0=r0, r1=r1, kp=kp, q=q, s=s: e.matmul(
                                pss[0:16, 256 * hf:256 * hf + 256], lhsT=qs2[r0:r1, kp, s, :], rhs=Ks[q][r0:r1, kp, :], start=True, stop=True),
                                reads=[qs2B, KsB[q]], writes=[pssB])
                            items.append(dict(pss=pss[0:16, 256 * hf:256 * hf + 256], pssB=pssB, bias=Ts[0:16, kv, :], biasB=TsB,
                                              sink=sks[0:16, j, kv:kv + 1], sinkB=sksB, mask=False,
                                              v=[(Vs[q][:, kv * 64:(kv + 1) * 64], VsB[q]), (Vp[q][:, kv * 64:(kv + 1) * 64], VpB[q])],
                                              P=16, cp=kp, rows=(r0, r1)))
                    attn_stage_a(items)
                    if pendg[0] is not None:
                        pendg[0]()

                    def fin_s(items=items, s=s):
                        outs = {}
                        for it in items:
                            if it["cp"] not in outs:
                                outs[it["cp"]] = bank()
                            psO, psOB, _ = outs[it["cp"]]
                            it["po"] = psO[it["rows"][0]:it["rows"][1], 0:16]
                            it["poB"] = psOB
                        attn_stage_b(items)
                        for kp, (psO, psOB, _) in outs.items():
                            for hf in range(2):
                                r0, r1 = 64 * hf, 64 * hf + 64
                                src = psO[r0:r1, 0:16].rearrange("p (g t) -> p g t", t=4)
                                copy_op(evac_eng(), OT[r0:r1, 4 * kp:4 * kp + 4, s:64:16], src, [psOB], [OTB])
                    pendg[0] = fin_s
                if pendg[0] is not None:
                    pendg[0]()
                    pendg[0] = None
            for m in range(NCH):
                pt, pbk, _ = bank()
                for cp in range(NCH):
                    S.op("pe", lambda e, pt=pt, cp=cp, m=m: e.matmul(pt[:, 0:N], lhsT=wo[:, cp, m * 128:(m + 1) * 128], rhs=OT[:, cp, 0:N], start=(cp == 0), stop=(cp == 7)),
                         reads=[woB, OTB], writes=[pbk], signal=(cp == 7))
                S.op("dve", lambda e, pt=pt, m=m: e.tensor_tensor(out=hT[:, m, c0:c0 + N], in0=hT[:, m, c0:c0 + N], in1=pt[:, 0:N], op=ALU.add),
                     reads=[pbk, hB[(m, b)]], writes=[hB[(m, b)]])

        def load_cprev(i):
            for pc in range(4):
                S.dma("sp", [(scst[0:32, :], sconv[i, :, :, pc * 1408:(pc + 1) * 1408].rearrange("s r n -> (s r) n"))], writes=[scstB], sem="scst")
                pt, pbk, _ = bank()
                for x in range(11):
                    S.op("pe", lambda e, pt=pt, x=x: e.transpose(out=pt[:, x * 32:(x + 1) * 32], in_=scst[0:32, x * 128:(x + 1) * 128], identity=ident_f[0:32, 0:32]),
                         reads=[scstB, idfB], writes=[pbk], signal=(x == 10))
                copy_op(evac_eng(), cprev[:, pc * 11:(pc + 1) * 11, :], pt[:, 0:352].rearrange("p (a b) -> p a b", a=11), [pbk],
                        [cprevB[ch] for ch in range(pc * 11, (pc + 1) * 11)])

        ws_ctr = [0]
        PROD_ENG = os.environ.get("KPROD", "pool")

        def ffn(i, sbi, blocks):
            for b, blk in enumerate(blocks):
                c0, N, kind = blk
                rms_block(blk, b, C_NF + i * 8, lambda c, c0=c0, N=N: xnf[:, c, c0:c0 + N], xnfB[b], k=b % 2)
            pend = [None]
            for pi, (j0, j1) in enumerate(PARTS):
                for j in range(j0, j1):
                    if j + 2 < NJ:
                        load_wup(i, j + 2)
                    wu, wuB = wup[j % 3], wupB[j % 3]
                    jj = j - j0
                    for b, (c0, N, kind) in enumerate(blocks):
                        st = ws_ctr[0] % 3
                        ws_ctr[0] += 1
                        pss = [bank(), bank()]
                        for hf in range(2):
                            pt, pbk, _ = pss[hf]
                            for k in range(NCH):
                                S.op("pe", lambda e, pt=pt, hf=hf, k=k, wu=wu, c0=c0, N=N: e.matmul(pt[:, 0:N], lhsT=wu[:, hf, k, :], rhs=xnf[:, k, c0:c0 + N], start=(k == 0), stop=(k == 7)),
                                     reads=[wuB, xnfB[b]], writes=[pbk], signal=(k == 7))
                        W = []
                        for hf in range(2):
                            ch = hf * NJ + j
                            W.append((col(C_CW + (i * 3 + 0) * 44 + ch), col(C_CW + (i * 3 + 1) * 44 + ch), col(C_CW + (i * 3 + 2) * 44 + ch), col(C_CB + i * 44 + ch), ch))
                        if kind == "p":
                            gb = sbi * 3 + b
                            rp, wp = gb % 2, (gb + 1) % 2
                            if sbi == 0 and b == 0:
                                for hf in range(2):
                                    pt, pbk, _ = pss[hf]
                                    S.op("dve", lambda e, pt=pt: e.tensor_scalar(out=pt[:, 254:256], in0=pt[:, 254:256], scalar1=cc[:, CC_M:CC_M + 1], scalar2=None, op0=ALU.mult),
                                         reads=[pbk, ccB], writes=[pbk])
                            for hf in range(2):
                                pt, pbk, _ = pss[hf]
                                w0, w1, w2, bcol, ch = W[hf]
                                Cb, CbB = cbuf[st][hf], cbB[st][hf]
                                S.op("act", lambda e, Cb=Cb, pt=pt, w2=w2, bcol=bcol, N=N: e.activation(out=Cb[:, 0:N], in_=pt[:, 0:N], func=AF.Identity, bias=bcol, scale=w2),
                                     reads=[pbk, colsB], writes=[CbB])
                                S.op("act", lambda e, pt=pt, ch=ch, N=N, wp=wp: e.activation(out=ccarry[:, wp, i, :, ch], in_=pt[:, N - 2:N], func=AF.Copy),
                                     reads=[pbk], writes=[ccB2[(wp, i, ch)]])
                            for hf in range(2):
                                pt, pbk, _ = pss[hf]
                                w0, w1, w2, bcol, ch = W[hf]
                                Cb, CbB = cbuf[st][hf], cbB[st][hf]
                                S.op("dve", lambda e, Cb=Cb, pt=pt, w1=w1, N=N: e.scalar_tensor_tensor(out=Cb[:, 1:N], in0=pt[:, 0:N - 1], scalar=w1, in1=Cb[:, 1:N], op0=ALU.mult, op1=ALU.add),
                                     reads=[pbk, CbB, colsB], writes=[CbB])
                            for hf in range(2):
                                pt, pbk, _ = pss[hf]
                                w0, w1, w2, bcol, ch = W[hf]
                                Cb, CbB = cbuf[st][hf], cbB[st][hf]
                                S.op("dve", lambda e, Cb=Cb, pt=pt, w0=w0, N=N: e.scalar_tensor_tensor(out=Cb[:, 2:N], in0=pt[:, 0:N - 2], scalar=w0, in1=Cb[:, 2:N], op0=ALU.mult, op1=ALU.add),
                                     reads=[pbk, CbB, colsB], writes=[CbB])
                            for hf in range(2):
                                pt, pbk, _ = pss[hf]
                                w0, w1, w2, bcol, ch = W[hf]
                                Cb, CbB = cbuf[st][hf], cbB[st][hf]
                                S.op("dve", lambda e, Cb=Cb, w1=w1, ch=ch, rp=rp: e.scalar_tensor_tensor(out=Cb[:, 0:1], in0=ccarry[:, rp, i, 1:2, ch], scalar=w1, in1=Cb[:, 0:1], op0=ALU.mult, op1=ALU.add),
                                     reads=[ccB2[(rp, i, ch)], CbB, colsB], writes=[CbB])
                            for hf in range(2):
                                pt, pbk, _ = pss[hf]
                                w0, w1, w2, bcol, ch = W[hf]
                                Cb, CbB = cbuf[st][hf], cbB[st][hf]
                                S.op("dve", lambda e, Cb=Cb, w0=w0, ch=ch, rp=rp: e.scalar_tensor_tensor(out=Cb[:, 0:2], in0=ccarry[:, rp, i, :, ch], scalar=w0, in1=Cb[:, 0:2], op0=ALU.mult, op1=ALU.add),
                                     reads=[ccB2[(rp, i, ch)], CbB, colsB], writes=[CbB])
                        else:
                            for hf in range(2):
                                pt, pbk, _ = pss[hf]
                                w0, w1, w2, bcol, ch = W[hf]
                                Cb, CbB = cbuf[st][hf], cbB[st][hf]
                                U3 = ugs[hf][:, 0:96].rearrange("p (s k) -> p s k", k=6)
                                UB = ugsB[hf]
                                S.op("act", lambda e, U3=U3, pt=pt: e.activation(out=U3[:, :, 2:6], in_=pt[:, 0:64].rearrange("p (t s) -> p s t", s=16), func=AF.Copy),
                                     reads=[pbk], writes=[UB])
                                S.op("dve", lambda e, U3=U3, ch=ch: e.tensor_copy(out=U3[:, :, 0:2], in_=cprev[:, ch, :].rearrange("p (s r) -> p s r", r=2)),
                                     reads=[cprevB[ch]], writes=[UB])
                                S.op("dve", lambda e, U3=U3, ch=ch: e.tensor_copy(out=cprev[:, ch, :].rearrange("p (s r) -> p s r", r=2), in_=U3[:, :, 4:6]),
                                     reads=[UB], writes=[cprevB[ch]])
                                t0, t1, t2 = U3[:, :, 0:4], U3[:, :, 1:5], U3[:, :, 2:6]
                                cv_ = Cb[:, 0:64].rearrange("p (t s) -> p s t", s=16)
                                S.op("act", lambda e, cv_=cv_, t0=t0, w0=w0, bcol=bcol: e.activation(out=cv_, in_=t0, func=AF.Identity, bias=bcol, scale=w0),
                                     reads=[UB, colsB], writes=[CbB])
                                S.op("dve", lambda e, cv_=cv_, t1=t1, w1=w1: e.scalar_tensor_tensor(out=cv_, in0=t1, scalar=w1, in1=cv_, op0=ALU.mult, op1=ALU.add),
                                     reads=[UB, CbB, colsB], writes=[CbB])
                                S.op("dve", lambda e, cv_=cv_, t2=t2, w2=w2: e.scalar_tensor_tensor(out=cv_, in0=t2, scalar=w2, in1=cv_, op0=ALU.mult, op1=ALU.add),
                                     reads=[UB, CbB, colsB], writes=[CbB])
                        if pend[0] is not None:
                            pend[0]()
                        Cg, Cv = cbuf[st][0], cbuf[st][1]

                        def fin(Cg=Cg, Cv=Cv, st=st, jj=jj, b=b, c0=c0, N=N):
                            S.op("act", lambda e: e.activation(out=Cg[:, 0:N], in_=Cg[:, 0:N], func=AF.Gelu_apprx_tanh), reads=[cbB[st][0]], writes=[cbB[st][0]])
                            S.op(PROD_ENG, lambda e: e.tensor_tensor(out=actT[:, jj, c0:c0 + N], in0=Cg[:, 0:N], in1=Cv[:, 0:N], op=ALU.mult),
                                 reads=[cbB[st][0], cbB[st][1]], writes=[actB[(jj, b)]])
                        pend[0] = fin
                if pend[0] is not None:
                    pend[0]()
                    pend[0] = None
                if pi + 1 < len(PARTS):
                    load_wdn(i, pi + 1)
                wd, wdB = wdn[pi % 2], wdnB[pi % 2]
                nj = j1 - j0
                for b, (c0, N, kind) in enumerate(blocks):
                    for m in range(NCH):
                        pt, pbk, _ = bank()
                        for jj in range(nj):
                            S.op("pe", lambda e, pt=pt, jj=jj, m=m, wd=wd, c0=c0, N=N: e.matmul(pt[:, 0:N], lhsT=wd[:, jj, m * 128:(m + 1) * 128], rhs=actT[:, jj, c0:c0 + N], start=(jj == 0), stop=(jj == nj - 1)),
                                 reads=[wdB, actB[(jj, b)]], writes=[pbk], signal=(jj == nj - 1))
                        S.op("dve", lambda e, pt=pt, m=m, c0=c0, N=N: e.tensor_tensor(out=hT[:, m, c0:c0 + N], in0=hT[:, m, c0:c0 + N], in1=pt[:, 0:N], op=ALU.add),
                             reads=[pbk, hB[(m, b)]], writes=[hB[(m, b)]])

        def conv_state_out(i, sbi):
            if sbi != 1:
                return
            pt, pbk, _ = bank()
            S.op("pe", lambda e, pt=pt: e.transpose(out=pt[0:88, 0:128], in_=ccarry[:, 1, i, :, :], identity=ident_f[:]),
                 reads=[ccB2[(1, i, ch)] for ch in range(44)] + [idfB], writes=[pbk])
            copy_op(evac_eng(), ost[0:88, 0:128], pt[0:88, 0:128], [pbk], [ostB])
            S.dma("sp", [(convp_o[i, r, :].rearrange("(j p) -> j p", p=128), ost[44 * r:44 * r + 44, 0:128]) for r in range(2)], reads=[ostB], sem="ost", final=True)
            for x in range(11):
                pt, pbk, _ = bank()
                S.op("pe", lambda e, pt=pt, x=x: e.transpose(out=pt[:, 0:128], in_=cprev[:, 4 * x:4 * x + 4, :], identity=ident_f[:]),
                     reads=[cprevB[ch] for ch in range(4 * x, 4 * x + 4)] + [idfB], writes=[pbk])
                copy_op(evac_eng(), ost[:, 0:128], pt[:, 0:128], [pbk], [ostB])
                S.dma("sp", [(convs_o[i, :, :, (4 * x + j4) * 128:(4 * x + j4 + 1) * 128].rearrange("s r p -> (s r) p"), ost[32 * j4:32 * j4 + 32, 0:128]) for j4 in range(4)],
                      reads=[ostB], sem="ost", final=True)

        def ple_block(i, sbi, blk, b):
            c0, N, kind = blk
            row0 = 0 if sbi == 0 else TA
            rms_block(blk, b, C_NP + i * 8, lambda c: xm[:, c, 0:N], xmB, k=0)
            if kind == "p":
                tiles = [(row0 + c0 + 128 * t, 128 * t, 128) for t in range(N // 128)]
            else:
                tiles = [(NPROMPT, 0, 64)]
            for ti, (r0, lc, rows) in enumerate(tiles):
                q = ti % 2
                S.dma("sp", [(pst[q][0:rows, :], pin[i, r0:r0 + rows, :])], writes=[pstB[q]], sem=f"pst{q}")
                pt, pbk, _ = bank()
                for kc in range(2):
                    S.op("pe", lambda e, pt=pt, kc=kc, q=q, rows=rows: e.transpose(out=pt[:, kc * 128:kc * 128 + rows], in_=pst[q][0:rows, kc * 128:(kc + 1) * 128], identity=ident_f[0:rows, 0:rows]),
                         reads=[pstB[q], idfB], writes=[pbk], signal=(kc == 1))
                copy_op(evac_eng(), pT[:, :, lc:lc + rows], pt[:, 0:256].rearrange("p (a b) -> p a b", a=2)[:, :, 0:rows], [pbk], [pTB])
            for m in range(NCH):
                q = m % 2
                pg, pgB, _ = bank()
                for k in range(NCH):
                    S.op("pe", lambda e, pg=pg, k=k, m=m: e.matmul(pg[:, 0:N], lhsT=wg[:, k, m * 128:(m + 1) * 128], rhs=xm[:, k, 0:N], start=(k == 0), stop=(k == 7)),
                         reads=[wgB, xmB], writes=[pgB], signal=(k == 7))
                pp, ppB, _ = bank()
                for k in range(2):
                    S.op("pe", lambda e, pp=pp, k=k, m=m: e.matmul(pp[:, 0:N], lhsT=wpr[:, k, m * 128:(m + 1) * 128], rhs=pT[:, k, 0:N], start=(k == 0), stop=(k == 1)),
                         reads=[wprB, pTB], writes=[ppB], signal=(k == 1))
                S.op("act", lambda e, pg=pg, q=q: e.activation(out=sg[q][:, 0:N], in_=pg[:, 0:N], func=AF.Sigmoid), reads=[pgB], writes=[sgB[q]])
                S.op("dve", lambda e, pp=pp, q=q: e.tensor_tensor(out=tmpb[q][:, 0:N], in0=sg[q][:, 0:N], in1=pp[:, 0:N], op=ALU.mult),
                     reads=[sgB[q], ppB], writes=[tmpB[q]])
                S.op("dve", lambda e, q=q, m=m: e.tensor_tensor(out=hT[:, m, c0:c0 + N], in0=hT[:, m, c0:c0 + N], in1=tmpb[q][:, 0:N], op=ALU.add),
                     reads=[tmpB[q], hB[(m, b)]], writes=[hB[(m, b)]])

        SBS = [
            [(0, 512, "p"), (512, 512, "p"), (1024, 256, "p")],
            [(0, 512, "p"), (512, 512, "p"), (1024, 64, "s")],
        ]
        import os
        STAGE = int(os.environ.get("KSTAGE", "9"))
        if STAGE >= 3:
            load_mixer(0)
        for sbi in range(2 if STAGE >= 5 else (1 if STAGE >= 2 else 0)):
            blocks = SBS[sbi]
            load_x(sbi, blocks)
            KLB = int(os.environ.get("KLB", "4"))
            KREP = int(os.environ.get("KREP", "0"))
            for i in (list(range(4 if STAGE >= 4 else (1 if STAGE >= 3 else 0))) if sbi == 0 else list(range(KLB)) + [1] * KREP):
                load_wup(i, 0)
                load_wup(i, 1)
                load_wdn(i, 0)
                if i == 2:
                    load_w(wk, wkd.rearrange("k p n -> p k n"), wkB, "wk")
                    load_w(wv, wvd.rearrange("k p n -> p k n"), wvB, "wv")
                    for b, blk in enumerate(blocks):
                        last_prompt = (b == 1)
                        if blk[2] == "s" and "skv" in SKIP:
                            continue
                        kv_block(sbi, blk, b, last_prompt)
                if sbi == 1:
                    load_cprev(i)
                for b, blk in enumerate(blocks):
                    if i < 2:
                        pool_block(i, sbi, blk, b)
                    else:
                        attn_block(i, sbi, blk, b)
                load_ple(i)
                ffn(i, sbi, blocks)
                conv_state_out(i, sbi)
                nxt = (sbi, i + 1) if i < 3 else ((sbi + 1, 0) if sbi == 0 else None)
                if STAGE == 3 or (STAGE == 4 and i == 3):
                    nxt = None
                if sbi == 0 and i == 3 and KLB == 0:
                    nxt = None
                if sbi == 1 and i == KLB - 1:
                    nxt = None
                if nxt is not None:
                    load_mixer(nxt[1])
                for b, blk in enumerate(blocks):
                    ple_block(i, sbi, blk, b)
            store_y(sbi, blocks)
            if sbi == 0:
                S.op("act", lambda e: e.activation(out=KT[:, :, 0:128], in_=KT[:, :, TA:TA + 128], func=AF.Copy), reads=[KTB[2]], writes=[KTcB])
                S.op("act", lambda e: e.activation(out=Vt[:, 0, :], in_=Vt[:, 10, :], func=AF.Copy), reads=[VtB[2]], writes=[VtcB])
            elif KLB >= 2:
                for i in range(2):
                    for hb in range(2):
                        pt, pbk, _ = bank()
                        for cq in range(4):
                            c = hb * 4 + cq
                            S.op("pe", lambda e, pt=pt, c=c, cq=cq, i=i: e.transpose(out=pt[0:15, cq * 128:(cq + 1) * 128], in_=pcarry[:, i, c, :], identity=ident_f[:]),
                                 reads=[pcB[(i, c)], idfB], writes=[pbk], signal=(cq == 3))
                        copy_op(evac_eng(), ost[0:15, hb * 512:(hb + 1) * 512], pt[0:15, :], [pbk], [ostB])
                    S.dma("sp", [(poolp_o[i], ost[0:15, :])], reads=[ostB], sem="ost", final=True)
        S.finish()
        S.run()
    return nc


_PROG = {}


def _host_inputs(inp):
    f32 = np.float32
    x_prompt, x_sample = inp["x_prompt"], inp["x_sample"]
    p_prompt, p_sample = inp["p_prompt"], inp["p_sample"]
    perm = np.zeros(1024, np.int64)
    for cp in range(8):
        for hf in range(2):
            head = 4 * (2 * (cp // 4) + hf) + cp % 4
            perm[cp * 128 + hf * 64: cp * 128 + hf * 64 + 64] = head * 64 + np.arange(64)
    wq = np.ascontiguousarray(inp["w_q"][:, :, perm]).reshape(2, 8, 128, 1024)
    wo = np.ascontiguousarray(inp["w_o"][:, perm, :]).reshape(2, 8, 128, 1024)
    wup = np.ascontiguousarray(inp["w_up"].reshape(4, 8, 128, 2, NJ, 128).transpose(0, 4, 2, 3, 1, 5)).reshape(4, NJ, 128, 2048)
    wdn = np.ascontiguousarray(inp["w_down"]).reshape(4, NJ, 128, 1024)
    wg = np.ascontiguousarray(inp["w_ple_gate"]).reshape(4, 8, 128, 1024)
    wpr = np.ascontiguousarray(inp["w_ple_proj"]).reshape(4, 2, 128, 1024)
    wpl = np.ascontiguousarray(inp["w_pool"]).reshape(2, 4, 2, 128, 256)
    wk = np.ascontiguousarray(inp["w_k"]).reshape(8, 128, 256)
    wv = np.ascontiguousarray(inp["w_v"]).reshape(8, 128, 256)

    cols = np.zeros((128, NCOLS), f32)

    def colmajor(v):
        return np.ascontiguousarray(v.reshape(-1, 128).T)
    for i in range(4):
        cols[:, C_NM + 8 * i:C_NM + 8 * i + 8] = colmajor(inp["norm_mix"][i])
        cols[:, C_NF + 8 * i:C_NF + 8 * i + 8] = colmajor(inp["norm_ffn"][i])
        cols[:, C_NP + 8 * i:C_NP + 8 * i + 8] = colmajor(inp["norm_ple"][i])
        for tap in range(3):
            cols[:, C_CW + (i * 3 + tap) * 44:C_CW + (i * 3 + tap + 1) * 44] = colmajor(inp["conv_w"][i, tap])
        cols[:, C_CB + i * 44:C_CB + (i + 1) * 44] = colmajor(inp["conv_b"][i])
    cols[:, C_KVN:C_KVN + 8] = colmajor(inp["kv_norm"])
    for i in range(2):
        cols[:, C_PSC + 8 * i:C_PSC + 8 * i + 8] = colmajor(inp["pool_scale"][i])
        cols[:, C_QN + i] = np.tile(inp["q_norm"][i], 2)
    cols[:, C_KN] = np.tile(inp["k_norm"], 2)
    cols[:, C_EPS] = EPS

    oh = np.zeros((33, 384), f32)
    ii = np.arange(384)
    dist = ii - 127
    valid = (dist >= 0) & (dist < 128)
    bk = t5_bucket_np(np.clip(dist, 0, 127))
    oh[bk[valid], ii[valid]] = 1.0
    oh[32, ~valid] = NEG

    shared = dict(cols=cols, relb=np.ascontiguousarray(inp["rel_bias"], f32), oh=oh, sinks=np.ascontiguousarray(inp["sinks"], f32),
                  wup=wup, wdn=wdn, wq=wq, wo=wo, wg=wg, wpr=wpr, wpl=wpl, wk=wk, wv=wv)
    maps = []
    for core in range(NCORES):
        bi, ch = core // 4, core % 4
        s = ch * CHUNK
        xin = np.zeros((NROWS, D), f32)
        pin = np.zeros((4, NROWS, 256), f32)
        lo = s - HALO
        if lo >= 0:
            xin[0:NPROMPT] = x_prompt[bi, lo:s + CHUNK]
            pin[:, 0:NPROMPT] = p_prompt[:, bi, lo:s + CHUNK]
        else:
            xin[HALO:NPROMPT] = x_prompt[bi, 0:CHUNK]
            pin[:, HALO:NPROMPT] = p_prompt[:, bi, 0:CHUNK]
        sq = slice(16 * core, 16 * core + 16)
        xin[NPROMPT:] = x_sample[sq].transpose(1, 0, 2).reshape(64, D)
        pin[:, NPROMPT:] = p_sample[:, sq].transpose(0, 2, 1, 3).reshape(4, 64, 256)
        ccv = np.zeros((128, NCC), f32)
        first = (ch == 0)
        ccv[:, CC_M] = 0.0 if first else 1.0
        for g, w in enumerate(POOL_W):
            pos = np.arange(15)
            cnt = np.minimum(pos + 1, w) if first else np.full(15, w)
            ccv[:, CC_INV + 15 * g:CC_INV + 15 * g + 15] = (1.0 / cnt.astype(np.float64)).astype(f32)[None, :]
        ccv[:, CC_FM:CC_FM + 128] = NEG if first else 0.0
        m = dict(shared)
        m.update(xin=xin, pin=pin, cc=ccv,
                 spool=np.ascontiguousarray(inp["state_pool"][:, sq]),
                 sconv=np.ascontiguousarray(inp["state_conv"][:, sq]),
                 ck=np.ascontiguousarray(inp["cache_k"][sq]).reshape(16, 128, 256),
                 cv=np.ascontiguousarray(inp["cache_v"][sq]).reshape(16, 128, 256))
        maps.append(m)
    return maps


def kernel(**inputs):
    inp = {k: np.asarray(v) for k, v in inputs.items()}
    if "nc" not in _PROG:
        _PROG["nc"] = build_program()
    nc = _PROG["nc"]
    maps = _host_inputs(inp)
    res = run_bass_kernel_spmd(nc, maps, core_ids=list(range(NCORES)))
    R = res.results
    f32 = np.float32
    y_prompt = np.zeros((2, 8192, D), f32)
    y_sample = np.zeros((128, 4, D), f32)
    pool_p = np.zeros((2, 2, 15, D), f32)
    pool_s = np.zeros((2, 128, 15, D), f32)
    conv_p = np.zeros((4, 2, 2, 2 * FF), f32)
    conv_s = np.zeros((4, 128, 2, 2 * FF), f32)
    k_p = np.zeros((2, 128, 4, 64), f32)
    v_p = np.zeros((2, 128, 4, 64), f32)
    k_s = np.zeros((128, 128, 4, 64), f32)
    v_s = np.zeros((128, 128, 4, 64), f32)
    for core in range(NCORES):
        bi, ch = core // 4, core % 4
        r = R[core]
        y = np.asarray(r["y"])
        y_prompt[bi, ch * CHUNK:(ch + 1) * CHUNK] = y[HALO:NPROMPT]
        sq = slice(16 * core, 16 * core + 16)
        y_sample[sq] = y[NPROMPT:].reshape(4, 16, D).transpose(1, 0, 2)
        pool_s[:, sq] = np.asarray(r["pools"])
        conv_s[:, sq] = np.asarray(r["convs"])
        k_s[sq] = np.asarray(r["cks"]).reshape(16, 128, 4, 64)
        v_s[sq] = np.asarray(r["cvs"]).reshape(16, 128, 4, 64)
        if ch == 3:
            pool_p[:, bi] = np.asarray(r["poolp"])
            conv_p[:, bi] = np.asarray(r["convp"])
            k_p[bi] = np.asarray(r["ckp"]).reshape(128, 4, 64)
            v_p[bi] = np.asarray(r["cvp"]).reshape(128, 4, 64)
    return (y_prompt, y_sample, pool_p, pool_s, conv_p, conv_s, k_p, k_s, v_p, v_s)
```
